# Optimizing a Trainium2 kernel written in Bass

```python
import math
import jax, jax.numpy as jnp
from jax import lax
import numpy as np

D_MODEL = 1024
BATCH = 16
SEQ = 2048
DEPTH = 1
DEC_BATCH = 128
DEC_SEQ = 8
PAST_LEN = 16384
PAGE_SIZE = 128

ATTN_HEADS = 8
ATTN_KV_HEADS = 2
GROUP = ATTN_HEADS // ATTN_KV_HEADS
HEAD_DIM = 64
ATTN_WIDTH = ATTN_HEADS * HEAD_DIM
KV_WIDTH = ATTN_KV_HEADS * HEAD_DIM
WINDOW = 128
ATTN_SCALE = HEAD_DIM ** -0.5
HGRN_HEADS = 4
HGRN_DK = 128
HGRN_DV = (D_MODEL - ATTN_WIDTH) // HGRN_HEADS
HGRN_KW = HGRN_HEADS * HGRN_DK
HGRN_VW = HGRN_HEADS * HGRN_DV
CHUNK = 64
D_MIX = ATTN_WIDTH + HGRN_VW
D_IN = ATTN_WIDTH + 2 * KV_WIDTH + 2 * HGRN_KW + 2 * HGRN_VW
D_FF = ((8 * D_MODEL // 3 + 127) // 128) * 128
ALPHA = (2.0 * DEPTH) ** 0.25
BETA = (8.0 * DEPTH) ** -0.25
LN_EPS = 1e-5
RMS_EPS = 1e-6

kernel_name = "hymba_hgrn2_swa_sink_macaron_deepnorm_step"


def _layer_norm(x, g, b):
    xf = x.astype(jnp.float32)
    mu = jnp.mean(xf, axis=-1, keepdims=True)
    var = jnp.mean(jnp.square(xf - mu), axis=-1, keepdims=True)
    y = (xf - mu) * lax.rsqrt(var + LN_EPS) * g.astype(jnp.float32) + b.astype(jnp.float32)
    return y.astype(x.dtype)


def _rms_norm(x, g):
    xf = x.astype(jnp.float32)
    return xf * lax.rsqrt(jnp.mean(jnp.square(xf), axis=-1, keepdims=True) + RMS_EPS) * g.astype(jnp.float32)


def _swiglu(x, w13, w2):
    gate, up = jnp.split(x @ w13, 2, axis=-1)
    return (jax.nn.silu(gate) * up) @ w2


def _alibi_slopes():
    h = jnp.arange(1, ATTN_HEADS + 1, dtype=jnp.float32)
    return jnp.exp2(-8.0 * h / ATTN_HEADS).reshape(ATTN_KV_HEADS, GROUP)


def _sink_attend(s, sinks, v, eq):
    sk = sinks.astype(jnp.float32)[:, :, None]
    m = jnp.maximum(jnp.max(s, axis=-1), sk)
    p = jnp.exp(s - m[..., None])
    denom = jnp.sum(p, axis=-1) + jnp.exp(sk - m)
    p = p / denom[..., None]
    return jnp.einsum(eq, p, v.astype(jnp.float32))


def _swa_prompt(q, k, v, sinks):
    B, L = q.shape[:2]
    nb = L // WINDOW
    qb = q.reshape(B, nb, WINDOW, ATTN_KV_HEADS, GROUP, HEAD_DIM)

    def band(a):
        ap = jnp.pad(a, ((0, 0), (WINDOW, 0), (0, 0), (0, 0)))
        ap = ap.reshape(B, nb + 1, WINDOW, ATTN_KV_HEADS, HEAD_DIM)
        return jnp.concatenate([ap[:, :-1], ap[:, 1:]], axis=2)

    kb, vb = band(k), band(v)
    s = jnp.einsum('bnqhgd,bnkhd->bnhgqk', qb, kb, preferred_element_type=jnp.float32) * ATTN_SCALE
    qi = jnp.arange(WINDOW)
    kc = jnp.arange(2 * WINDOW)
    dist = qi[:, None] + WINDOW - kc[None, :]
    key_pos = (jnp.arange(nb)[:, None] - 1) * WINDOW + kc[None, :]
    mask = ((dist >= 0) & (dist < WINDOW))[None] & (key_pos >= 0)[:, None, :]
    bias = -_alibi_slopes()[:, :, None, None] * dist.astype(jnp.float32)
    s = jnp.where(mask[None, :, None, None], s + bias, -jnp.inf)
    o = _sink_attend(s, sinks, vb, 'bnhgqk,bnkhd->bnqhgd')
    return o.reshape(B, L, ATTN_WIDTH), k[:, -WINDOW:], v[:, -WINDOW:]


def _swa_sample(q, k, v, buf_k, buf_v, sinks):
    B, T = q.shape[:2]
    wb = buf_k.shape[1]
    qg = q.reshape(B, T, ATTN_KV_HEADS, GROUP, HEAD_DIM)
    kk = jnp.concatenate([buf_k.astype(k.dtype), k], axis=1)
    vv = jnp.concatenate([buf_v.astype(v.dtype), v], axis=1)
    s = jnp.einsum('bqhgd,bkhd->bhgqk', qg, kk, preferred_element_type=jnp.float32) * ATTN_SCALE
    dist = wb + jnp.arange(T)[:, None] - jnp.arange(wb + T)[None, :]
    mask = (dist >= 0) & (dist < WINDOW)
    bias = -_alibi_slopes()[:, :, None, None] * dist.astype(jnp.float32)
    s = jnp.where(mask, s + bias, -jnp.inf)
    o = _sink_attend(s, sinks, vv, 'bhgqk,bkhd->bqhgd')
    return o.reshape(B, T, ATTN_WIDTH), kk[:, -wb:], vv[:, -wb:]


def _hgrn2(q, k, v, logf, S0):
    B, L = q.shape[:2]
    C = CHUNK if L % CHUNK == 0 else L
    nc = L // C

    def chunks(a):
        return a.astype(jnp.float32).reshape(B, nc, C, *a.shape[2:]).swapaxes(0, 1)

    causal = jnp.tril(jnp.ones((C, C), dtype=bool))[None, :, :, None, None]

    def step(S, xs):
        qc, kc, vc, gc = xs
        b = jnp.cumsum(gc, axis=1)
        o = jnp.einsum('bthk,bhkv->bthv', qc * jnp.exp(b), S)
        dec = jnp.exp(jnp.where(causal, b[:, :, None] - b[:, None, :], -jnp.inf))
        a = jnp.einsum('bthk,bshk,btshk->bhts', qc, kc, dec)
        o = o + jnp.einsum('bhts,bshv->bthv', a, vc)
        b_last = b[:, -1]
        S = jnp.exp(b_last)[..., None] * S + jnp.einsum(
            'bshk,bshv->bhkv', kc * jnp.exp(b_last[:, None] - b), vc)
        return S, o

    S, o = lax.scan(step, S0.astype(jnp.float32), (chunks(q), chunks(k), chunks(v), chunks(logf)))
    return o.swapaxes(0, 1).reshape(B, L, HGRN_HEADS, HGRN_DV), S


def _mixer(h, S0, buf_k, buf_v, lb, w_in, attn_sink, attn_norm_g, hgrn_norm_g, w_out):
    B, L, _ = h.shape
    offs = [ATTN_WIDTH, ATTN_WIDTH + KV_WIDTH, ATTN_WIDTH + 2 * KV_WIDTH,
            ATTN_WIDTH + 2 * KV_WIDTH + HGRN_KW, ATTN_WIDTH + 2 * KV_WIDTH + 2 * HGRN_KW,
            ATTN_WIDTH + 2 * KV_WIDTH + 2 * HGRN_KW + HGRN_VW]
    qa, ka, va, hq, hf, hi, hg = jnp.split(h @ w_in, offs, axis=-1)
    qa = qa.reshape(B, L, ATTN_HEADS, HEAD_DIM)
    ka = ka.reshape(B, L, ATTN_KV_HEADS, HEAD_DIM)
    va = va.reshape(B, L, ATTN_KV_HEADS, HEAD_DIM)
    sinks = attn_sink.reshape(ATTN_KV_HEADS, GROUP)
    if buf_k is None:
        oa, nk, nv = _swa_prompt(qa, ka, va, sinks)
    else:
        oa, nk, nv = _swa_sample(qa, ka, va, buf_k, buf_v, sinks)
    oa = _rms_norm(oa, attn_norm_g)
    fq = jax.nn.silu(hq.astype(jnp.float32)).reshape(B, L, HGRN_HEADS, HGRN_DK)
    f = lb + (1.0 - lb) * jax.nn.sigmoid(hf.astype(jnp.float32))
    logf = jnp.log(f).reshape(B, L, HGRN_HEADS, HGRN_DK)
    kin = (1.0 - f).reshape(B, L, HGRN_HEADS, HGRN_DK)
    vin = hi.reshape(B, L, HGRN_HEADS, HGRN_DV)
    oh, S = _hgrn2(fq, kin, vin, logf, S0)
    oh = _rms_norm(oh, hgrn_norm_g.reshape(HGRN_HEADS, HGRN_DV)).reshape(B, L, HGRN_VW)
    oh = oh * jax.nn.silu(hg.astype(jnp.float32))
    y = jnp.concatenate([oa, oh], axis=-1).astype(h.dtype) @ w_out
    return y, S, nk, nv


def _decoder_layer(x, S0, buf_k, buf_v, lb, ln1_g, ln1_b, ffn1_w13, ffn1_w2, w_in, attn_sink,
                   attn_norm_g, hgrn_norm_g, w_out, ln2_g, ln2_b, ffn2_w13, ffn2_w2, ln3_g, ln3_b):
    x = _layer_norm(ALPHA * x + 0.5 * _swiglu(x, ffn1_w13, ffn1_w2), ln1_g, ln1_b)
    y, S, nk, nv = _mixer(x, S0, buf_k, buf_v, lb, w_in, attn_sink, attn_norm_g, hgrn_norm_g, w_out)
    x = _layer_norm(ALPHA * x + y, ln2_g, ln2_b)
    x = _layer_norm(ALPHA * x + 0.5 * _swiglu(x, ffn2_w13, ffn2_w2), ln3_g, ln3_b)
    return x, S, nk, nv


def setup_inputs(seed: int = 0) -> dict:
    key = jax.random.key(seed)
    ks = jax.random.split(key, 24)
    D = D_MODEL
    wb = min(WINDOW, PAST_LEN)

    def nrm(k, shape, scale):
        return jax.random.normal(k, shape, jnp.float32) * scale

    col_scale = jnp.concatenate([
        jnp.ones((ATTN_WIDTH + KV_WIDTH,), jnp.float32),
        jnp.full((KV_WIDTH,), BETA, jnp.float32),
        jnp.ones((2 * HGRN_KW,), jnp.float32),
        jnp.full((HGRN_VW,), BETA, jnp.float32),
        jnp.ones((HGRN_VW,), jnp.float32)])
    return {
        "x_prompt": nrm(ks[0], (BATCH, SEQ, D), 1.0),
        "x_sample": nrm(ks[1], (DEC_BATCH, DEC_SEQ, D), 1.0),
        "state_hgrn": nrm(ks[2], (DEPTH, DEC_BATCH, HGRN_HEADS, HGRN_DK, HGRN_DV), 0.3),
        "cache_win_k": nrm(ks[3], (DEPTH, DEC_BATCH, wb, ATTN_KV_HEADS, HEAD_DIM), 1.0),
        "cache_win_v": nrm(ks[4], (DEPTH, DEC_BATCH, wb, ATTN_KV_HEADS, HEAD_DIM), BETA),
        "ln1_g": 1.0 + nrm(ks[5], (DEPTH, D), 0.02),
        "ln1_b": nrm(ks[6], (DEPTH, D), 0.02),
        "ffn1_w13": nrm(ks[7], (DEPTH, D, 2 * D_FF), D ** -0.5),
        "ffn1_w2": nrm(ks[8], (DEPTH, D_FF, D), BETA * D_FF ** -0.5),
        "w_in": nrm(ks[9], (DEPTH, D, D_IN), D ** -0.5) * col_scale,
        "lb_param": nrm(ks[10], (DEPTH + 1, HGRN_KW), 0.1),
        "attn_sink": nrm(ks[11], (DEPTH, ATTN_HEADS), 0.5),
        "attn_norm_g": 1.0 + nrm(ks[12], (DEPTH, ATTN_WIDTH), 0.02),
        "hgrn_norm_g": 1.0 + nrm(ks[13], (DEPTH, HGRN_VW), 0.02),
        "w_out": nrm(ks[14], (DEPTH, D_MIX, D), BETA * D_MIX ** -0.5),
        "ln2_g": 1.0 + nrm(ks[15], (DEPTH, D), 0.02),
        "ln2_b": nrm(ks[16], (DEPTH, D), 0.02),
        "ffn2_w13": nrm(ks[17], (DEPTH, D, 2 * D_FF), D ** -0.5),
        "ffn2_w2": nrm(ks[18], (DEPTH, D_FF, D), BETA * D_FF ** -0.5),
        "ln3_g": 1.0 + nrm(ks[19], (DEPTH, D), 0.02),
        "ln3_b": nrm(ks[20], (DEPTH, D), 0.02),
    }


def reference(x_prompt, x_sample, state_hgrn, cache_win_k, cache_win_v, ln1_g, ln1_b, ffn1_w13,
              ffn1_w2, w_in, lb_param, attn_sink, attn_norm_g, hgrn_norm_g, w_out, ln2_g, ln2_b,
              ffn2_w13, ffn2_w2, ln3_g, ln3_b):
    lb_all = jnp.cumsum(jax.nn.softmax(lb_param.astype(jnp.float32), axis=0), axis=0)
    xp, xs = x_prompt, x_sample
    sp_list, kp_list, vp_list, ss_list, ks_list, vs_list = [], [], [], [], [], []
    for l in range(DEPTH):
        prm = (ln1_g[l], ln1_b[l], ffn1_w13[l], ffn1_w2[l], w_in[l], attn_sink[l], attn_norm_g[l],
               hgrn_norm_g[l], w_out[l], ln2_g[l], ln2_b[l], ffn2_w13[l], ffn2_w2[l], ln3_g[l], ln3_b[l])
        S0p = jnp.zeros((xp.shape[0], HGRN_HEADS, HGRN_DK, HGRN_DV), jnp.float32)
        xp, Sp, kp, vp = _decoder_layer(xp, S0p, None, None, lb_all[l], *prm)
        xs, Ss, kq, vq = _decoder_layer(xs, state_hgrn[l], cache_win_k[l], cache_win_v[l], lb_all[l], *prm)
        sp_list.append(Sp); kp_list.append(kp); vp_list.append(vp)
        ss_list.append(Ss); ks_list.append(kq); vs_list.append(vq)
    return (xp, xs, jnp.stack(sp_list), jnp.stack(kp_list), jnp.stack(vp_list),
            jnp.stack(ss_list), jnp.stack(ks_list), jnp.stack(vs_list))
```

```python
import numpy as np
import concourse.bass as bass
import concourse.mybir as mybir
from concourse.bass_utils import run_bass_kernel_spmd

F32 = mybir.dt.float32
BF16 = mybir.dt.bfloat16
AF = mybir.ActivationFunctionType
ALU = mybir.AluOpType
AX = mybir.AxisListType

ENGS = ("pe", "act", "dve", "pool", "sp")
EPOCH = 30000


class Sched:
    def __init__(self, nc):
        self.nc = nc
        self.prog = {e: [] for e in ENGS}
        self.known = {e: {} for e in ENGS}
        self.rw = {}
        self.rr = {}
        self.dcount = {}
        self.needed = {e: set() for e in ENGS if e != "sp"}

    def _need(self, eng, tok, waits):
        kind, k, v = tok
        if kind == "E" and k == eng and eng == "pe":
            return
        if kind == "D":
            v = max(v, self.dcount.get(k, 0))
            tok = (kind, k, v)
        kk = (kind, k)
        if self.known[eng].get(kk, 0) >= v:
            return
        self.known[eng][kk] = v
        waits.append(tok)
        if kind == "E":
            self.needed[k].add(v)

    def _deps(self, eng, reads, writes):
        waits = []
        for r in reads:
            for tok in self.rw.get(r, ()):
                self._need(eng, tok, waits)
            if isinstance(r, tuple) and r[0] == "ps":
                for tok in self.rr.get(r, ()):
                    if not (tok[0] == "E" and tok[1] == eng):
                        self._need(eng, tok, waits)
        for w in writes:
            for tok in self.rw.get(w, ()):
                self._need(eng, tok, waits)
            for tok in self.rr.get(w, ()):
                self._need(eng, tok, waits)
        return waits

    def _commit(self, tok, reads, writes):
        for r in reads:
            if r in writes:
                continue
            self.rr.setdefault(r, []).append(tok)
            if len(self.rr[r]) > 24:
                self.rr[r] = self._compact(self.rr[r])
        for w in writes:
            self.rw[w] = [tok]
            self.rr[w] = []

    @staticmethod
    def _compact(toks):
        best = {}
        for t in toks:
            kk = (t[0], t[1])
            if kk not in best or best[kk][2] < t[2]:
                best[kk] = t
        return list(best.values())

    def op(self, eng, fn, reads=(), writes=()):
        reads = tuple(reads)
        writes = tuple(writes)
        waits = self._deps(eng, reads, writes)
        idx = len(self.prog[eng]) + 1
        self.prog[eng].append(dict(waits=waits, fn=fn, dma=None))
        tok = ("E", eng, idx)
        self._commit(tok, reads, writes)
        return tok

    def dma(self, queue, fn, key, reads=(), writes=()):
        reads = tuple(reads)
        writes = tuple(writes)
        waits = self._deps(queue, reads, writes)
        n = self.dcount.get(key, 0) + 1
        self.dcount[key] = n
        tok = ("D", key, n)
        self.prog[queue].append(dict(waits=waits, fn=fn, dma=tok))
        self._commit(tok, reads, writes)
        return tok

    def emit(self, stack):
        nc = self.nc
        esem = {}
        nidx = {}
        for e in self.needed:
            need = sorted(self.needed[e])
            nidx[e] = {v: i + 1 for i, v in enumerate(need)}
            nep = (len(need) + EPOCH - 1) // EPOCH + 1
            esem[e] = [stack.enter_context(nc.semaphore(f"s_{e}_{k}")) for k in range(nep)]
        dsem = {k: stack.enter_context(nc.semaphore(f"d_{i}")) for i, k in enumerate(self.dcount)}

        def semval(tok):
            kind, k, v = tok
            if kind == "D":
                return dsem[k], 16 * v
            i = nidx[k][v]
            return esem[k][(i - 1) // EPOCH], (i - 1) % EPOCH + 1

        block = stack.enter_context(nc.Block())
        final = [("D", k, n) for k, n in self.dcount.items()]

        def run(eng_name):
            def body(eng):
                for i, rec in enumerate(self.prog[eng_name]):
                    for tok in rec["waits"]:
                        s, v = semval(tok)
                        eng.wait_ge(s, v)
                    ins = rec["fn"](eng)
                    if rec["dma"] is not None:
                        s, _ = semval(rec["dma"])
                        ins.then_inc(s, 16)
                    elif eng_name != "sp" and (i + 1) in nidx[eng_name]:
                        s, _ = semval(("E", eng_name, i + 1))
                        ins.then_inc(s, 1)
                if eng_name == "sp":
                    for tok in final:
                        s, v = semval(tok)
                        eng.wait_ge(s, v)
            return body

        block.tensor(run("pe"))
        block.scalar(run("act"))
        block.vector(run("dve"))
        block.gpsimd(run("pool"))
        block.sync(run("sp"))


D = 1024
DFF = 2816
NJ = 22
ALPHA = 2.0 ** 0.25
LN_EPS = 1e-5
RMS_EPS = 1e-6
SCALE = 0.125
NPIECE_FF = 11
NPIECE_IN = 6

C_ID = 0
C_DMP = 128
C_SCAN = C_DMP + 2048
C_AM = C_SCAN + 256
C_DSC = C_AM + 256
C_DSN = C_DSC + 64
C_ONES = C_DSN + 64
C_SEQM = C_ONES + 128
C_EPS = C_SEQM + 16
NCST = C_EPS + 2


def make_consts():
    c = np.zeros((128, NCST), np.float32)
    c[:, C_ID:C_ID + 128] = np.eye(128, dtype=np.float32)
    slopes = (2.0 ** -(np.arange(1, 9, dtype=np.float64))).reshape(2, 4)
    k = np.arange(128)[:, None]
    q = np.arange(128)[None, :]
    for kh in range(2):
        for g in range(2):
            dist = (q - k) + (128 if kh == 0 else 0)
            valid = (dist >= 0) & (dist < 128)
            blk = np.zeros((128, 4, 128), np.float64)
            for j in range(4):
                blk[:, j, :] = np.where(valid, np.exp(-slopes[g, j] * dist), 0.0)
            o = C_DMP + (kh * 2 + g) * 512
            c[:, o:o + 512] = blk.reshape(128, 512)
    m64 = np.ones(128, np.float32); m64[::64] = 0
    m8 = np.ones(128, np.float32); m8[::8] = 0
    c[:, C_SCAN:C_SCAN + 128] = m64[None, :]
    c[:, C_SCAN + 128:C_SCAN + 256] = m8[None, :]
    s = np.arange(128)[:, None]
    t = np.arange(128)[None, :]
    c[:, C_AM:C_AM + 128] = ((s // 64 == t // 64) & (s <= t)).astype(np.float32)
    c[:, C_AM + 128:C_AM + 256] = ((s // 8 == t // 8) & (s <= t)).astype(np.float32)
    for g in range(2):
        blk = np.zeros((128, 4, 8), np.float64)
        blkn = np.zeros((128, 4, 8), np.float64)
        for j in range(4):
            for tt in range(8):
                dist = 128 + tt - np.arange(128)
                blk[:, j, tt] = np.where((dist >= 0) & (dist < 128), np.exp(-slopes[g, j] * dist), 0.0)
                distn = tt - (np.arange(128) % 8)
                blkn[:, j, tt] = np.where(distn >= 0, np.exp(-slopes[g, j] * np.maximum(distn, 0)), 0.0)
        c[:, C_DSC + g * 32:C_DSC + (g + 1) * 32] = blk.reshape(128, 32)
        c[:, C_DSN + g * 32:C_DSN + (g + 1) * 32] = blkn.reshape(128, 32)
    c[:, C_ONES:C_ONES + 128] = 1.0
    c[:, C_SEQM:C_SEQM + 16] = (np.arange(128)[:, None] // 8 == np.arange(16)[None, :]).astype(np.float32)
    c[:, C_EPS] = LN_EPS / (ALPHA * ALPHA)
    c[:, C_EPS + 1] = RMS_EPS
    return c


def build(n_ptiles=8, sample=True, stages="ABC"):
    from contextlib import ExitStack
    nc = bass.Bass("TRN2", target_bir_lowering=False)
    NT = 512 * n_ptiles + (128 if sample else 0)
    NSEQ = max(1, (n_ptiles + 3) // 4)

    def din(name, shape):
        return nc.dram_tensor(name, shape, F32, kind="ExternalInput").ap()

    def dout(name, shape):
        return nc.dram_tensor(name, shape, F32, kind="ExternalOutput").ap()

    xin = din("xin", [NT, D])
    w13 = {"a": din("w13a", [NPIECE_FF, 128, 4096]), "c": din("w13b", [NPIECE_FF, 128, 4096])}
    w2 = {"a": din("w2a", [128, NJ * 1024]), "c": din("w2b", [128, NJ * 1024])}
    win = din("win", [NPIECE_IN, 128, 4096])
    wout = din("wout", [128, 8 * 1024])
    lnp = din("lnp", [6, D])
    ang = din("ang", [1, 512])
    sink = din("sink", [1, 8])
    hng = din("hng", [128, 4])
    lbp = din("lbp", [128, 8])
    cst_d = din("cst", [128, NCST])
    st_in = din("st_in", [16, 4, 128, 128])
    ck_in = din("ck_in", [16, 128, 128])
    cv_in = din("cv_in", [16, 128, 128])
    y = dout("y", [NT, D])
    st_p = dout("st_p", [2, 4, 128, 128])
    ck_p = dout("ck_p", [2, 128, 128])
    cv_p = dout("cv_p", [2, 128, 128])
    st_s = dout("st_s", [16, 4, 128, 128])
    ck_s = dout("ck_s", [16, 128, 128])
    cv_s = dout("cv_s", [16, 128, 128])

    scr = {
        ("A",): nc.dram_tensor("scr_w13a", [NPIECE_FF, 128, 4096], BF16).ap(),
        ("C",): nc.dram_tensor("scr_w13b", [NPIECE_FF, 128, 4096], BF16).ap(),
        ("B",): nc.dram_tensor("scr_win", [NPIECE_IN, 128, 4096], BF16).ap(),
        ("w2", "a"): nc.dram_tensor("scr_w2a", [128, NJ * 1024], BF16).ap(),
        ("w2", "c"): nc.dram_tensor("scr_w2b", [128, NJ * 1024], BF16).ap(),
        ("wout",): nc.dram_tensor("scr_wout", [128, 8 * 1024], BF16).ap(),
    }
    st = ExitStack()
    S = Sched(nc)
    SKIP = set()

    def sb(name, shape, dt):
        return st.enter_context(nc.sbuf_tensor(name, shape, dt))

    cst = sb("cst_sb", [128, NCST], F32)
    identb = sb("identb", [128, 128], BF16)
    msk512 = sb("msk512", [128, 512], F32)
    lnbuf = sb("lnbuf", [128, 2048], F32)
    angb = sb("angb", [128, 512], F32)
    esink = sb("esink", [128, 8], F32)
    hngt = sb("hngt", [128, 4], F32)
    lbt = sb("lbt", [128, 8], F32)
    lbv = sb("lbv", [128, 16], F32)
    xres = sb("xres", [128, 4, D], F32)
    rbuf = sb("rbuf", [128, D], F32)
    xb = sb("xb", [128, D], BF16)
    xT = sb("xT", [128, 8, 512], BF16)
    hT = sb("hT", [128, NJ, 512], BF16)
    Wbig = sb("Wbig", [128, NJ * 1024], BF16)
    wst = [sb(f"wst{i}", [128, 4096], BF16) for i in range(3)]
    Wout = sb("Wout", [128, 8 * 1024], BF16)
    sil = [sb(f"sil{i}", [128, 512], F32) for i in range(2)]
    hq_s = sb("hq_s", [128, 4, 512], BF16)
    G = sb("G", [128, 4, 512], BF16)
    KT = sb("KT", [128, 5, 128], BF16)
    Vaug = sb("Vaug", [128, 5, 2, 65], BF16)
    Ef = [sb(f"Ef{i}", [128, 512], F32) for i in range(2)]
    Pb = sb("Pb", [128, 4, 512], BF16)
    oa = sb("oa", [128, 512], F32)
    oan = sb("oan", [128, 512], BF16)
    mixT = sb("mixT", [128, 8, 512], BF16)
    Ktok4 = [sb(f"Ktok4_{i}", [128, 4, 128], BF16) for i in range(2)]
    Vtok4 = [sb(f"Vtok4_{i}", [128, 4, 128], BF16) for i in range(2)]
    Am4 = sb("Am4", [128, 4, 128], BF16)
    Ktok = [Ktok4[0][:, i, :] for i in range(2)]
    Vtok = [Vtok4[0][:, i, :] for i in range(2)]
    Am = [Am4[:, i, :] for i in range(2)]
    Sf = sb("Sf", [128, 4, 128], F32)
    Sb = sb("Sb", [128, 4, 128], BF16)
    EBL = sb("EBL", [128, 4, 16], F32)
    ohf = [sb(f"ohf{i}", [128, 128], F32) for i in range(2)]
    osq = [sb(f"osq{i}", [128, 128], F32) for i in range(2)]
    ors = [sb(f"ors{i}", [128, 128], F32) for i in range(2)]
    kvf = sb("kvf", [128, 2, 128], F32)
    kvt = sb("kvt", [128, 2, 128], F32)
    st6 = sb("st6", [128, 12], F32)
    mv = sb("mv", [128, 2], F32)
    sm = sb("sm", [128, 8], F32)
    den = sb("den", [128, 16], F32)
    xres_b = xres.bitcast(BF16)
    S0f = [xres[:, 1, i * 512:(i + 1) * 512].rearrange("p (h v) -> p h v", h=4) for i in range(2)]
    Snew = [xres[:, 2, i * 512:(i + 1) * 512].rearrange("p (h v) -> p h v", h=4) for i in range(2)]
    S0b = [xres_b[:, 3, i * 512:(i + 1) * 512].rearrange("p (h v) -> p h v", h=4) for i in range(2)]
    Vm = [xres_b[:, 3, 1024 + i * 128:1024 + (i + 1) * 128] for i in range(2)]
    KTs = [xres_b[:, 3, 1280 + i * 128:1280 + (i + 1) * 128] for i in range(2)]
    Pc = [xres_b[:, 3, 1536 + i * 64:1536 + (i + 1) * 64] for i in range(2)]
    Pn = [xres_b[:, 3, 1664 + i * 64:1664 + (i + 1) * 64] for i in range(2)]
    SAMPLE_KEYS = ([(n, i) for n in ("Vm", "KTs", "Pc", "Pn") for i in range(2)]
                   + [(n, r) for n in ("S0fb", "S0bb", "Snewb") for r in range(8)])

    ps = [st.enter_context(nc.psum_tensor(f"ps{i}", [128, 512], F32)) for i in range(8)]
    psTb = ps[6].bitcast(BF16)

    wstf = [w.bitcast(F32) for w in wst]

    tiles = [("p", t, 512 * t) for t in range(n_ptiles)] + ([("s", 0, 512 * n_ptiles)] if sample else [])
    tinfo = [dict(kind=k_, t=t_, row0=r_, NB=(4 if k_ == "p" else 1), loaded=set(), feat=set()) for (k_, t_, r_) in tiles]
    plan = []
    piece_pos = []
    for ti, tl in enumerate(tiles):
        for stg in "ABC":
            if stg not in stages:
                continue
            npieces = NPIECE_IN if stg == "B" else NPIECE_FF
            for p in range(npieces):
                piece_pos.append(len(plan))
                plan.append(("piece", stg, p, len(piece_pos) - 1, ti))
                if stg == "A" and p in (3, 5, 7, 9):
                    plan.append(("w2q", "a", (p - 3) // 2, ti))
                if stg == "B" and p == 1:
                    plan.append(("wout", ti))
                if stg == "B" and p == NPIECE_IN - 1 and "C" in stages:
                    for q_ in range(4):
                        plan.append(("w2q", "c", q_, ti))
                if stg == "C" and "B" not in stages and p in (3, 5, 7, 9):
                    plan.append(("w2q", "c", (p - 3) // 2, ti))
                if stg == stages[-1] and p == npieces - 1 and ti + 1 < len(tiles):
                    plan.append(("xcast", ti + 1))
    state = dict(cursor=0, npiece=0, limit=len(plan))
    xcast_done = set()

    def ensure_xcast(tl):
        if id(tl) not in xcast_done:
            xcast_done.add(id(tl))
            emit_xcast(tl)

    pending_wb = []

    def flush_wb(keep=0):
        while len(pending_wb) > keep:
            fn, rk, wk = pending_wb.pop(0)
            S.dma("pool", fn, "scrwb", reads=rk, writes=wk)

    NCONV = max(1, min(8, n_ptiles))

    def emit_plan_entry(ent):
        kind = ent[0]
        if kind == "piece":
            _, stg, p, n, ti_ = ent
            slot = n % 3
            ct = p % NCONV
            if ti_ <= ct:
                src = win[p] if stg == "B" else w13["a" if stg == "A" else "c"][p]
                S.dma("pool", lambda e, slot=slot, src=src: e.dma_start(out=wst[slot][:], in_=src),
                      ("wst", slot), writes=[("wst", slot)])
            else:
                src = scr[(stg,)][p]
                S.dma("pool", lambda e, slot=slot, src=src: e.dma_start(out=wst[slot][:], in_=src),
                      ("wst", slot), reads=[("scr", stg, p)], writes=[("wst", slot)])
            flush_wb(0)
            if ti_ == ct:
                dst = scr[(stg,)][p]
                pending_wb.append((lambda e, slot=slot, dst=dst: e.dma_start(out=dst, in_=wst[slot][:]),
                                   [("wst", slot)], [("scr", stg, p)]))
        elif kind == "w2q":
            _, x_, q, ti_ = ent
            a, b_ = q * 5632, (q + 1) * 5632
            ct = q % NCONV
            if ti_ <= ct:
                src = w2[x_]
                S.dma("pool", lambda e, a=a, b_=b_, src=src: e.dma_start(out=Wbig[:, a:b_], in_=src[:, a:b_]),
                      ("Wbig", q), writes=[("Wbig", q)])
                if ti_ == ct:
                    dst = scr[("w2", x_)]
                    pending_wb.append((lambda e, a=a, b_=b_, dst=dst: e.dma_start(out=dst[:, a:b_], in_=Wbig[:, a:b_]),
                                       [("Wbig", q)], [("scr", "w2", x_, q)]))
            else:
                src = scr[("w2", x_)]
                S.dma("pool", lambda e, a=a, b_=b_, src=src: e.dma_start(out=Wbig[:, a:b_], in_=src[:, a:b_]),
                      ("Wbig", q), reads=[("scr", "w2", x_, q)], writes=[("Wbig", q)])
        elif kind == "xcast":
            ensure_xcast(tinfo[ent[1]])
        elif kind == "wout":
            ti_ = ent[1]
            for q in range(2):
                a, b_ = q * 4096, (q + 1) * 4096
                ct = (2 + q) % NCONV
                if ti_ <= ct:
                    S.dma("pool", lambda e, a=a, b_=b_: e.dma_start(out=Wout[:, a:b_], in_=wout[:, a:b_]),
                          ("Wout", q), writes=[("Wout", q)])
                    if ti_ == ct:
                        dst = scr[("wout",)]
                        pending_wb.append((lambda e, a=a, b_=b_, dst=dst: e.dma_start(out=dst[:, a:b_], in_=Wout[:, a:b_]),
                                           [("Wout", q)], [("scr", "wout", q)]))
                else:
                    src = scr[("wout",)]
                    S.dma("pool", lambda e, a=a, b_=b_, src=src: e.dma_start(out=Wout[:, a:b_], in_=src[:, a:b_]),
                          ("Wout", q), reads=[("scr", "wout", q)], writes=[("Wout", q)])

    def ensure(k):
        k = min(k, state["limit"] - 1, len(plan) - 1)
        while state["cursor"] <= k:
            emit_plan_entry(plan[state["cursor"]])
            state["cursor"] += 1

    def next_piece():
        n = state["npiece"]
        state["npiece"] += 1
        ensure(piece_pos[min(n + 2, len(piece_pos) - 1)])
        assert state["cursor"] > piece_pos[n], "piece not emitted (limit too tight)"
        return n % 3

    def wbig_keys(j):
        return [("Wbig", (j * 1024) // 5632), ("Wbig", (j * 1024 + 1023) // 5632)]

    S.dma("sp", lambda e: e.dma_start(out=cst[:], in_=cst_d), "cst", writes=["cst"])
    S.dma("sp", lambda e: e.dma_start(out=angb[:], in_=ang[0:1, :].to_broadcast([128, 512])), "angb", writes=["angb"])
    S.dma("sp", lambda e: e.dma_start(out=esink[:], in_=sink[0:1, :].to_broadcast([128, 8])), "esink", writes=["esink"])
    S.dma("sp", lambda e: e.dma_start(out=hngt[:], in_=hng), "hngt", writes=["hngt"])
    S.dma("sp", lambda e: e.dma_start(out=lbt[:], in_=lbp), "lbt", writes=["lbt"])
    S.op("dve", lambda e: e.tensor_copy(out=identb[:], in_=cst[:, C_ID:C_ID + 128]), reads=["cst"], writes=["identb"])
    for r in range(4):
        S.op("dve", lambda e, r=r: e.tensor_copy(out=msk512[:, r * 128:(r + 1) * 128], in_=cst[:, C_SCAN:C_SCAN + 128]),
             reads=["cst"], writes=[("msk512", r)])
    S.op("act", lambda e: e.activation(out=esink[:], in_=esink[:], func=AF.Exp), reads=["esink"], writes=["esink"])
    S.op("dve", lambda e: e.tensor_tensor(out=lbv[:, 12:16], in0=lbt[:, 4:8], in1=lbt[:, 0:4], op=ALU.subtract),
         reads=["lbt"], writes=["lbv"])
    S.op("act", lambda e: e.activation(out=lbv[:, 12:16], in_=lbv[:, 12:16], func=AF.Exp), reads=["lbv"], writes=["lbv"])
    S.op("dve", lambda e: e.tensor_scalar(out=lbv[:, 12:16], in0=lbv[:, 12:16], scalar1=1.0, scalar2=None, op0=ALU.add),
         reads=["lbv"], writes=["lbv"])
    S.op("dve", lambda e: e.reciprocal(out=lbv[:, 0:4], in_=lbv[:, 12:16]), reads=["lbv"], writes=["lbv"])
    S.op("dve", lambda e: e.tensor_scalar(out=lbv[:, 4:8], in0=lbv[:, 0:4], scalar1=-1.0, scalar2=1.0, op0=ALU.mult, op1=ALU.add),
         reads=["lbv"], writes=["lbv"])
    S.op("dve", lambda e: e.tensor_scalar(out=lbv[:, 8:12], in0=lbv[:, 4:8], scalar1=-1.0, scalar2=None, op0=ALU.mult),
         reads=["lbv"], writes=["lbv"])
    S.op("dve", lambda e: e.memset(Vaug[:], 1.0), writes=["Vaug_all"])

    eps_ln = cst[:, C_EPS:C_EPS + 1]
    eps_rms = cst[:, C_EPS + 1:C_EPS + 2]
    idf = cst[:, C_ID:C_ID + 128]
    onesf = cst[:, C_ONES:C_ONES + 128]

    def to_featmajor(NB, m, src_key):
        S.op("dve", lambda e: e.tensor_copy(out=xb[:], in_=xres[:, m, :]), reads=[src_key], writes=["xb"])
        for kc in range(8):
            S.op("pe", lambda e, kc=kc: e.transpose(psTb[:, kc * 128:(kc + 1) * 128], xb[:, kc * 128:(kc + 1) * 128], identb[:]),
                 reads=["xb", "identb"], writes=[("ps", 6)])
        S.op("act", lambda e: e.copy(out=xT[:, :, m * 128:(m + 1) * 128],
                                     in_=psTb[:, :].rearrange("p (k t) -> p k t", k=8)),
             reads=[("ps", 6)], writes=[("xT", m)])

    def load_ln(idx):
        S.dma("sp", lambda e: e.dma_start(out=lnbuf[:, 0:1024], in_=lnp[2 * idx:2 * idx + 1, :].to_broadcast([128, 1024])),
              "lnbuf_g", writes=["lnbuf_g"])
        S.dma("sp", lambda e: e.dma_start(out=lnbuf[:, 1024:2048], in_=lnp[2 * idx + 1:2 * idx + 2, :].to_broadcast([128, 1024])),
              "lnbuf_b", writes=["lnbuf_b"])

    def ln_epilogue(m, psd, scale):
        xk = ("xres", m)
        for h in range(2):
            S.op("dve", lambda e, h=h: e.scalar_tensor_tensor(out=xres[:, m, h * 512:(h + 1) * 512], in0=ps[psd[h]][:], scalar=scale / ALPHA,
                                                          in1=xres[:, m, h * 512:(h + 1) * 512], op0=ALU.mult, op1=ALU.add),
                 reads=[("ps", psd[h]), xk], writes=[xk])
        S.op("dve", lambda e: e.bn_stats(out=st6[:, 0:6], in_=xres[:, m, 0:512]), reads=[xk], writes=["st6a"])
        S.op("dve", lambda e: e.bn_stats(out=st6[:, 6:12], in_=xres[:, m, 512:1024]), reads=[xk], writes=["st6b"])
        S.op("dve", lambda e: e.bn_aggr(out=mv[:], in_=st6[:]), reads=["st6a", "st6b"], writes=["mv"])
        S.op("act", lambda e: e.activation(out=sm[:, 0:1], in_=mv[:, 1:2], func=AF.Ln, bias=eps_ln), reads=["mv", "cst"], writes=["sm0"])
        S.op("act", lambda e: e.activation(out=sm[:, 1:2], in_=sm[:, 0:1], func=AF.Exp, scale=-0.5), reads=["sm0"], writes=["sm1"])
        S.op("dve", lambda e: e.tensor_scalar(out=xres[:, m, :], in0=xres[:, m, :], scalar1=mv[:, 0:1], scalar2=sm[:, 1:2],
                                              op0=ALU.subtract, op1=ALU.mult),
             reads=[xk, "mv", "sm1"], writes=[xk])
        S.op("dve", lambda e: e.tensor_tensor(out=xres[:, m, :], in0=xres[:, m, :], in1=lnbuf[:, 0:1024], op=ALU.mult),
             reads=[xk, "lnbuf_g"], writes=[xk])
        S.op("dve", lambda e: e.tensor_tensor(out=xres[:, m, :], in0=xres[:, m, :], in1=lnbuf[:, 1024:2048], op=ALU.add),
             reads=[xk, "lnbuf_b"], writes=[xk])

    xstage = mixT[:].rearrange("p (m a) t -> p m (a t)", m=4)
    ALLMIX = ([("mixT", "a", mm) for mm in range(4)] + [("mixT", "h", mm) for mm in range(4)]
              + [("mixT", "hh", hh_, mm) for hh_ in range(3) for mm in range(4)])

    def emit_xcast(tl):
        for m in range(tl["NB"]):
            r0 = tl["row0"] + m * 128
            S.dma("pool", lambda e, m=m, r0=r0: e.dma_start(out=xstage[:, m, :], in_=xin[r0:r0 + 128, :]), ("xstage", m),
                  writes=ALLMIX + [("xstage", m)])

    def emit_feat_stage(tl, m):
        if m in tl["feat"]:
            return
        tl["feat"].add(m)
        for kc in range(8):
            S.op("pe", lambda e, kc=kc: e.transpose(psTb[:, kc * 128:(kc + 1) * 128], xstage[:, m, kc * 128:(kc + 1) * 128], identb[:]),
                 reads=ALLMIX + [("xstage", m), "identb"], writes=[("ps", 6)])
        S.op("act", lambda e: e.copy(out=xT[:, :, m * 128:(m + 1) * 128], in_=psTb[:, :].rearrange("p (k t) -> p k t", k=8)),
             reads=[("ps", 6)], writes=[("xT", m)])

    def emit_xload(tl, m):
        if m in tl["loaded"]:
            return
        tl["loaded"].add(m)
        r0 = tl["row0"] + m * 128
        S.dma("sp", lambda e: e.dma_start(out=xres[:, m, :], in_=xin[r0:r0 + 128, :]), ("xload", m), writes=[("xres", m)])

    def emit_feat(tl, m):
        if m in tl["feat"]:
            return
        emit_xload(tl, m)
        tl["feat"].add(m)
        to_featmajor(tl["NB"], m, ("xres", m))

    def ffn_stage(which, NB, row0, final, nxt=None):
        N = NB * 128
        load_ln(0 if which == "a" else 2)
        xkeys = [("xT", m) for m in range(NB)]
        for p in range(NPIECE_FF):
            slot = next_piece()
            for c in range(2):
                j = 2 * p + c
                bg, bu = 2 * (j % 2), 2 * (j % 2) + 1
                for kc in range(8):
                    S.op("pe", lambda e, kc=kc, c=c, slot=slot, bg=bg: e.matmul(
                        ps[bg][:, 0:N], lhsT=wst[slot][:, kc * 512 + c * 128:kc * 512 + (c + 1) * 128],
                        rhs=xT[:, kc, 0:N], start=(kc == 0), stop=(kc == 7)),
                        reads=[("wst", slot)] + xkeys, writes=[("ps", bg)])
                for kc in range(8):
                    S.op("pe", lambda e, kc=kc, c=c, slot=slot, bu=bu: e.matmul(
                        ps[bu][:, 0:N], lhsT=wst[slot][:, kc * 512 + 256 + c * 128:kc * 512 + 256 + (c + 1) * 128],
                        rhs=xT[:, kc, 0:N], start=(kc == 0), stop=(kc == 7)),
                        reads=[("wst", slot)] + xkeys, writes=[("ps", bu)])
                si = j % 2
                S.op("act", lambda e, si=si, bg=bg: e.activation(out=sil[si][:, 0:N], in_=ps[bg][:, 0:N], func=AF.Silu),
                     reads=[("ps", bg)], writes=[("sil", si)])
                S.op("dve", lambda e, si=si, bu=bu, j=j: e.tensor_tensor(out=hT[:, j, 0:N], in0=sil[si][:, 0:N], in1=ps[bu][:, 0:N], op=ALU.mult),
                     reads=[("sil", si), ("ps", bu)], writes=[("hT", j)])
        pairs = ((4, 5), (0, 1), (2, 3))
        for m in range(NB):
            pr = pairs[m % 3]
            for h in range(2):
                for j in range(NJ):
                    S.op("pe", lambda e, j=j, h=h, m=m, pr=pr: e.matmul(
                        ps[pr[h]][:], lhsT=hT[:, j, m * 128:(m + 1) * 128],
                        rhs=Wbig[:, j * 1024 + h * 512:j * 1024 + (h + 1) * 512], start=(j == 0), stop=(j == NJ - 1)),
                        reads=[("hT", j)] + wbig_keys(j), writes=[("ps", pr[h])])
            if final:
                if nxt is not None and m < nxt["NB"]:
                    emit_feat_stage(nxt, m)
            elif m >= 2:
                to_featmajor(NB, m - 2, ("xres", m - 2))
            ln_epilogue(m, pr, 0.5)
            if final:
                S.dma("sp", lambda e, m=m: e.dma_start(out=y[row0 + m * 128:row0 + (m + 1) * 128, :], in_=xres[:, m, :]),
                      ("yout", m), reads=[("xres", m)])
                if nxt is not None and m < nxt["NB"]:
                    emit_xload(nxt, m)
        if final:
            if nxt is not None:
                for m in range(nxt["NB"]):
                    emit_feat_stage(nxt, m)
        else:
            for m in range(max(0, NB - 2), NB):
                to_featmajor(NB, m, ("xres", m))

    QT = lambda: hT[:, 0:4, :]

    def attn_finish(m, pso, split=False):
        for g in range(2):
            pv = ps[pso[g]][:, 0:260].rearrange("p (j d) -> p j d", j=4)
            S.op("dve", lambda e, g=g, pv=pv: e.tensor_tensor(out=den[:, 4 * g:4 * g + 4], in0=pv[:, :, 64], in1=esink[:, 4 * g:4 * g + 4], op=ALU.add),
                 reads=[("ps", pso[g]), "esink"], writes=[("den", g)])
        S.op("dve", lambda e: e.reciprocal(out=den[:, 8:16], in_=den[:, 0:8]), reads=[("den", 0), ("den", 1)], writes=["rden"])
        for g in range(2):
            for j in range(4):
                hh = 4 * g + j
                S.op("dve", lambda e, g=g, j=j, hh=hh: e.tensor_scalar(
                    out=oa[:, hh * 64:(hh + 1) * 64], in0=ps[pso[g]][:, j * 65:j * 65 + 64], scalar1=den[:, 8 + hh:9 + hh], scalar2=None, op0=ALU.mult),
                    reads=[("ps", pso[g]), "rden"], writes=[("oa", hh)])
        oak = [("oa", hh) for hh in range(8)]
        S.op("act", lambda e: e.activation(out=sil[0][:, 0:512], in_=oa[:], func=AF.Square, accum_out=sm[:, 2:3]),
             reads=oak, writes=[("sil", 0), "sm2"])
        S.op("act", lambda e: e.activation(out=sm[:, 3:4], in_=sm[:, 2:3], func=AF.Ln, bias=eps_rms, scale=1.0 / 512.0),
             reads=["sm2", "cst"], writes=["sm3"])
        S.op("act", lambda e: e.activation(out=sm[:, 4:5], in_=sm[:, 3:4], func=AF.Exp, scale=-0.5), reads=["sm3"], writes=["sm4"])
        S.op("dve", lambda e: e.scalar_tensor_tensor(out=oan[:], in0=oa[:], scalar=sm[:, 4:5], in1=angb[:], op0=ALU.mult, op1=ALU.mult),
             reads=oak + ["sm4", "angb"], writes=["oan"])
        if not split:
            attn_finish_c(m)

    def attn_finish_c(m):
        for c in range(4):
            S.op("pe", lambda e, c=c: e.transpose(psTb[:, c * 128:(c + 1) * 128], oan[:, c * 128:(c + 1) * 128], identb[:]),
                 reads=["oan", "identb"], writes=[("ps", 6)])
        S.op("act", lambda e: e.copy(out=mixT[:, 0:4, m * 128:(m + 1) * 128], in_=psTb[:, 0:512].rearrange("p (k t) -> p k t", k=4)),
             reads=[("ps", 6)], writes=[("mixT", "a", m)])

    def hgrn_out(h, m, hi_):
        i = hi_ % 2
        src = ps[7][:, 128 * (h % 4):128 * (h % 4) + 128] if False else None
        return i

    def mixer_stage(kind, t, NB, row0, final):
        N = NB * 128
        is_s = (kind == "s")
        pos = 0 if is_s else (t % 4)
        seq = 0 if is_s else (t // 4)
        load_ln(1)
        xkeys = [("xT", m) for m in range(NB)]
        n0 = state["npiece"]
        state["limit"] = piece_pos[n0 + NPIECE_IN - 1] + 1 if True else 0
        state["limit"] = (piece_pos[n0 + NPIECE_IN] if (is_s and n0 + NPIECE_IN < len(piece_pos)) else len(plan))
        T1 = wstf[(n0 + 3) % 3]
        T2 = wstf[(n0 + 4) % 3]
        T3 = wstf[(n0 + 5) % 3]
        k1, k2, k3 = ("wst", (n0 + 3) % 3), ("wst", (n0 + 4) % 3), ("wst", (n0 + 5) % 3)
        want_cache = (is_s or pos == 3) and "nokvf" not in SKIP
        hTf = hT.bitcast(F32)
        mixTf = mixT.bitcast(F32)
        T1h = [hTf[:, 17:19, :].rearrange("p a t -> p (a t)"), hTf[:, 19:21, :].rearrange("p a t -> p (a t)"),
               mixTf[:, 4:6, :].rearrange("p a t -> p (a t)"), mixTf[:, 6:8, :].rearrange("p a t -> p (a t)")]
        MIXH = [("mixT", "h", mm) for mm in range(4)] + [("mixT", "hh", hh_, mm) for hh_ in range(3) for mm in range(4)]
        T1K = [[("T1h", 0)], [("T1h", 1)], [("T1h", 2)] + MIXH, [("T1h", 3)] + MIXH]
        per = 8 if is_s else 64
        mask = cst[:, C_SCAN + 128:C_SCAN + 256] if is_s else msk512[:, 0:N]
        mkeys = ["cst"] if is_s else [("msk512", r) for r in range(4)]

        def b2_chain(h):
            t1 = T1h[h]
            if h % 2 == 0:
                t2, t3, k2_, k3_ = rbuf[:, 0:N], rbuf[:, 512:512 + N], ("rbuf", 0), ("rbuf", 1)
            else:
                t2, t3, k2_, k3_ = sil[1][:, 0:N], xb.bitcast(F32)[:, 0:N], ("sil", 1), "xb"
            S.op("act", lambda e: e.activation(out=t2, in_=t1[:, 0:N], func=AF.Ln, bias=lbv[:, h:h + 1], scale=lbv[:, 4 + h:5 + h]),
                 reads=T1K[h] + ["lbv"], writes=[k2_])
            S.op("dve", lambda e: e.tensor_scalar(out=t1[:, 0:N], in0=t1[:, 0:N], scalar1=lbv[:, 8 + h:9 + h], scalar2=lbv[:, 4 + h:5 + h],
                                                  op0=ALU.mult, op1=ALU.add),
                 reads=T1K[h] + ["lbv", k2_], writes=T1K[h])
            S.op("dve", lambda e: e.tensor_tensor_scan(out=t3, data0=mask, data1=t2, initial=0.0, op0=ALU.mult, op1=ALU.add),
                 reads=[k2_] + mkeys, writes=[k3_])
            S.op("act", lambda e: e.activation(out=t2, in_=t3, func=AF.Exp), reads=[k3_], writes=[k2_])
            S.op("act", lambda e: e.activation(out=t3, in_=t3, func=AF.Exp, scale=-1.0), reads=[k3_], writes=[k3_])
            S.op("dve", lambda e: e.tensor_tensor(out=hT[:, 8 + h, 0:N], in0=hq_s[:, h, 0:N], in1=t2, op=ALU.mult),
                 reads=[("hq_s", h), k2_], writes=[("hT", 8 + h)])
            S.op("dve", lambda e: e.tensor_tensor(out=hT[:, 12 + h, 0:N], in0=t1[:, 0:N], in1=t3, op=ALU.mult),
                 reads=T1K[h] + [k3_], writes=[("hT", 12 + h)])
            nl = N // per
            S.op("act", lambda e: e.copy(out=EBL[:, h, 0:nl], in_=t2.rearrange("p (c s) -> p c s", s=per)[:, :, per - 1]),
                 reads=[k2_], writes=[("EBL", h)])

        pbanks = (0, 1, 7)
        nchunk = 0
        pend_a2 = []
        for p in range(NPIECE_IN):
            slot = next_piece()
            nch = 2 if p == 1 else 4
            for c in range(nch):
                b = pbanks[nchunk % 3]
                nchunk += 1
                for kc in range(8):
                    S.op("pe", lambda e, kc=kc, c=c, slot=slot, b=b: e.matmul(
                        ps[b][:, 0:N], lhsT=wst[slot][:, kc * 512 + c * 128:kc * 512 + (c + 1) * 128],
                        rhs=xT[:, kc, 0:N], start=(kc == 0), stop=(kc == 7)),
                        reads=[("wst", slot)] + xkeys, writes=[("ps", b)])
                pk = [("ps", b)]
                if c == 1 and pend_a2:
                    prompt_attn_a(pend_a2.pop(), pos, seq, gsel=(1,))
                if p == 0:
                    S.op("act", lambda e, c=c, b=b: e.copy(out=hT[:, c, 0:N], in_=ps[b][:, 0:N]), reads=pk, writes=[("hT", c)])
                elif p == 1 and c == 0:
                    S.op("act", lambda e, b=b: e.copy(out=KT[:, 1:1 + NB, :], in_=ps[b][:, 0:N].rearrange("p (m t) -> p m t", m=NB)),
                         reads=pk, writes=[("KT", 1 + m) for m in range(NB)])
                    if want_cache:
                        S.op("act", lambda e, b=b: e.copy(out=kvf[:, 0, :], in_=ps[b][:, N - 128:N]), reads=pk, writes=[("kvf", 0)])
                elif p == 1:
                    S.op("act", lambda e, b=b: e.copy(out=hT[:, 16, 0:N], in_=ps[b][:, 0:N]), reads=pk, writes=[("hT", 16)])
                    if want_cache:
                        S.op("act", lambda e, b=b: e.copy(out=kvf[:, 1, :], in_=ps[b][:, N - 128:N]), reads=pk, writes=[("kvf", 1)])
                elif p == 2:
                    S.op("act", lambda e, c=c, b=b: e.activation(out=hq_s[:, c, 0:N], in_=ps[b][:, 0:N], func=AF.Silu), reads=pk, writes=[("hq_s", c)])
                elif p == 3:
                    S.op("act", lambda e, c=c, b=b: e.activation(out=T1h[c][:, 0:N], in_=ps[b][:, 0:N], func=AF.Sigmoid), reads=pk, writes=T1K[c])
                elif p == 4:
                    S.op("dve", lambda e, c=c, b=b: e.tensor_copy(out=hT[:, 4 + c, 0:N], in_=ps[b][:, 0:N]), reads=pk, writes=[("hT", 4 + c)])
                    b2_chain(c)
                else:
                    S.op("act", lambda e, c=c, b=b: e.activation(out=G[:, c, 0:N], in_=ps[b][:, 0:N], func=AF.Silu), reads=pk, writes=[("G", c)])
            if not is_s:
                if p >= 3:
                    prompt_attn_c(p - 3, pos, seq)
                if p >= 2:
                    prompt_attn_b(p - 2, pos, seq)
                if 1 <= p <= 4:
                    prompt_attn_a(p - 1, pos, seq, gsel=(0,))
                    pend_a2.append(p - 1)
        if not is_s:
            if pend_a2:
                prompt_attn_a(pend_a2.pop(), pos, seq, gsel=(1,))
            prompt_hgrn1(0, pos, seq)
            prompt_attn_c(3, pos, seq)
        if is_s:
            sample_core(k1, k2, k3, T1, T2, T3)
            S.op("dve", lambda e: e.memset(sm[:, 7:8], 0.0), writes=[("xres", 1), ("xres", 2), ("xres", 3)] + SAMPLE_KEYS
                 + [("Ef", 0), ("Ef", 1), ("Ktok", 0), ("Ktok", 1), ("Vtok", 0), ("Vtok", 1), ("Am", 0), ("Am", 1),
                    ("Ktok4", 0), ("Vtok4", 0), ("Am4", 0), ("Am4", 1)])
        state["limit"] = len(plan)
        WB = (0, 1, 7)

        def wout_mm(m):
            wb = (WB[(2 * m) % 3], WB[(2 * m + 1) % 3])
            for hf_ in range(2):
                for c in range(8):
                    S.op("pe", lambda e, c=c, hf_=hf_, m=m, wb=wb: e.matmul(
                        ps[wb[hf_]][:], lhsT=mixT[:, c, m * 128:(m + 1) * 128],
                        rhs=Wout[:, c * 1024 + hf_ * 512:c * 1024 + (hf_ + 1) * 512], start=(c == 0), stop=(c == 7)),
                        reads=[("mixT", "a", m), ("mixT", "h", m), ("Wout", c // 4)], writes=[("ps", wb[hf_])])

        def wout_ep(m):
            wb = (WB[(2 * m) % 3], WB[(2 * m + 1) % 3])
            if (not final) and m >= 2:
                to_featmajor(NB, m - 2, ("xres", m - 2))
            ln_epilogue(m, wb, 1.0)
            if final:
                S.dma("sp", lambda e, m=m: e.dma_start(out=y[row0 + m * 128:row0 + (m + 1) * 128, :], in_=xres[:, m, :]),
                      ("yout", m), reads=[("xres", m)])

        for m in range(NB):
            if m >= 1:
                wout_mm(m - 1)
            if not is_s:
                prompt_hgrn2(m, pos, seq)
                prompt_hgrn3(m, pos, seq)
                if m + 1 < NB:
                    prompt_hgrn1(m + 1, pos, seq)
            if m >= 1:
                wout_ep(m - 1)
        wout_mm(NB - 1)
        wout_ep(NB - 1)
        if not final:
            for m in range(max(0, NB - 2), NB):
                to_featmajor(NB, m, ("xres", m))
        for m in range(0):
            pass
        if (not is_s) and pos < 3:
            S.op("act", lambda e: e.copy(out=KT[:, 0, :], in_=KT[:, 4, :]), reads=[("KT", 4)], writes=[("KT", 0)])
            S.op("act", lambda e: e.copy(out=Vaug[:, 0, :, 0:64], in_=Vaug[:, 4, :, 0:64]), reads=[("Vaug", 4)], writes=[("Vaug", 0)])

    def sample_core(k1, k2, k3, T1, T2, T3):
        n0s = state["npiece"] - NPIECE_IN
        slot2 = (n0s + 4) % 3
        CVaug = wst[slot2][:, 0:2080].rearrange("p (b g d) -> p b g d", b=16, g=2)
        seqm = cst[:, C_SEQM:C_SEQM + 16]
        S.op("dve", lambda e: e.memset(sm[:, 7:8], 0.0), writes=[("xres", 1), ("xres", 2), ("xres", 3)] + SAMPLE_KEYS)
        if "cload" not in SKIP:
            S.dma("sp", lambda e: e.dma_start(out=T1[:, 0:2048].rearrange("p (b f) -> p b f", b=16), in_=ck_in.rearrange("b w f -> w b f")),
                  "ckload", writes=[k1])
            S.dma("sp", lambda e: e.dma_start(out=T3[:, 0:2048].rearrange("p (b f) -> p b f", b=16), in_=cv_in.rearrange("b w f -> w b f")),
                  "cvload", writes=[k3])
        if "dramcopy" not in SKIP:
            S.dma("sp", lambda e: e.dma_start(out=ck_s[:, 0:120, :], in_=ck_in[:, 8:128, :]), "ckcopy")
            S.dma("sp", lambda e: e.dma_start(out=cv_s[:, 0:120, :], in_=cv_in[:, 8:128, :]), "cvcopy")
        if "cvms" not in SKIP:
            S.op("dve", lambda e: e.memset(wst[slot2][:, 0:2080], 1.0), writes=[k2])
        for g in range(2):
            if "cvaug" in SKIP:
                continue
            S.op("dve", lambda e, g=g: e.tensor_copy(out=CVaug[:, :, g, 0:64],
                                                  in_=T3[:, 0:2048].rearrange("p (b f) -> p b f", b=16)[:, :, g * 64:(g + 1) * 64]),
                 reads=[k3], writes=[k2])
        if "vt" not in SKIP:
            v_to_tok(0, 1)
        for w_, dst in ((0, ck_s), (1, cv_s)):
            if "kvtr" in SKIP:
                continue
            S.op("pe", lambda e, w_=w_: e.transpose(ps[3][:, w_ * 128:(w_ + 1) * 128], kvf[:, w_, :], idf), reads=[("kvf", w_), "cst"], writes=[("ps", 3)])
            S.op("act", lambda e, w_=w_: e.copy(out=kvt[:, w_, :], in_=ps[3][:, w_ * 128:(w_ + 1) * 128]), reads=[("ps", 3)], writes=[("kvt", w_)])
            for b in range(16):
                if "kvrows" in SKIP:
                    continue
                S.dma("sp", lambda e, w_=w_, dst=dst, b=b: e.dma_start(out=dst[b, 120:128, :], in_=kvt[8 * b:8 * b + 8, w_, :]),
                      ("kvt", w_), reads=[("kvt", w_)])
        def attn_seq(b):
            i = b % 2
            if "s_tr" not in SKIP:
                S.op("pe", lambda e, b=b: e.transpose(ps[6][:, 0:128], T1[:, b * 128:(b + 1) * 128], idf), reads=[k1, "cst"], writes=[("ps", 6)])
                S.op("act", lambda e, i=i: e.copy(out=KTs[i], in_=ps[6][:, 0:128]), reads=[("ps", 6)], writes=[("KTs", i)])
            for g in range(2):
                bk = 2 if g == 0 else 0
                q_ = hT[64 * g:64 * g + 64, 0:4, b * 8:(b + 1) * 8]
                S.op("pe", lambda e, g=g, i=i, q_=q_, bk=bk: e.matmul(ps[bk][:, 0:32], lhsT=KTs[i][64 * g:64 * g + 64, :], rhs=q_,
                                                                  start=True, stop=True),
                     reads=[("KTs", i)] + [("hT", c) for c in range(4)], writes=[("ps", bk)])
                S.op("pe", lambda e, g=g, q_=q_, bk=bk: e.matmul(ps[bk][:, 32:64], lhsT=KT[64 * g:64 * g + 64, 1, :], rhs=q_,
                                                             start=True, stop=True),
                     reads=[("KT", 1)] + [("hT", c) for c in range(4)], writes=[("ps", bk)])
                S.op("act", lambda e, i=i, g=g, bk=bk: e.activation(out=Ef[i][:, g * 64:(g + 1) * 64], in_=ps[bk][:, 0:64], func=AF.Exp, scale=SCALE),
                     reads=[("ps", bk)], writes=[("Ef", i)])
            efk = [("Ef", i)]
            efv = Ef[i][:, 0:128].rearrange("p (g c) -> p g c", g=2)
            S.op("dve", lambda e, i=i, efv=efv: e.tensor_tensor(out=Pc[i].rearrange("p (g c) -> p g c", g=2), in0=efv[:, :, 0:32],
                                                           in1=cst[:, C_DSC:C_DSC + 64].rearrange("p (g c) -> p g c", g=2), op=ALU.mult),
                 reads=efk + ["cst"], writes=[("Pc", i)])
            S.op("dve", lambda e, i=i, b=b, efv=efv: e.scalar_tensor_tensor(out=Pn[i].rearrange("p (g c) -> p g c", g=2), in0=efv[:, :, 32:64],
                                                                       scalar=seqm[:, b:b + 1],
                                                                       in1=cst[:, C_DSN:C_DSN + 64].rearrange("p (g c) -> p g c", g=2),
                                                                       op0=ALU.mult, op1=ALU.mult),
                 reads=efk + ["cst"], writes=[("Pn", i)])
            for g in range(2):
                if "s_pv" in SKIP:
                    continue
                o_ = ps[4 + g][0:65, b * 32:(b + 1) * 32]
                S.op("pe", lambda e, g=g, i=i, b=b, o_=o_: e.matmul(o_, lhsT=CVaug[:, b, g, :], rhs=Pc[i][:, g * 32:(g + 1) * 32], start=True, stop=False),
                     reads=[k2, ("Pc", i)], writes=[("ps", 4 + g)])
                S.op("pe", lambda e, g=g, i=i, o_=o_: e.matmul(o_, lhsT=Vaug[:, 1, g, :], rhs=Pn[i][:, g * 32:(g + 1) * 32], start=False, stop=True),
                     reads=[("Vaug", 1), "Vaug_all", ("Pn", i)], writes=[("ps", 4 + g)])
        def s0_load(n):
            h_, b_ = divmod(n, 16)
            r = n % 8
            S.dma("sp", lambda e, h_=h_, b_=b_, r=r: e.dma_start(out=S0f[r // 4][:, r % 4, :], in_=st_in[b_, h_]), ("S0f", r), writes=[("S0fb", r)])
        for n in range(4):
            if "shgrn" not in SKIP:
                s0_load(n)
        for h in range(4):
            if "hg" in SKIP:
                continue
            i = hgrn_head_common(h, 0, C_AM + 128)
            for b in range(16):
                if b % 4 == 0:
                    attn_seq(4 * h + b // 4)
                n = h * 16 + b
                if n + 4 < 64:
                    s0_load(n + 4)
                r = n % 8
                s0f = S0f[r // 4][:, r % 4, :]
                s0b = S0b[r // 4][:, r % 4, :]
                snw = Snew[r // 4][:, r % 4, :]
                vm = Vm[n % 2]
                cs = slice(b * 8, (b + 1) * 8)
                ebl = EBL[:, h, b:b + 1]
                S.op("act", lambda e, s0f=s0f, s0b=s0b: e.copy(out=s0b, in_=s0f), reads=[("S0fb", r)], writes=[("S0bb", r)])
                S.op("pe", lambda e, i=i, cs=cs: e.matmul(ps[7][:, 128 + cs.start:128 + cs.stop], lhsT=Vtok[i], rhs=Am[i][:, cs], start=True, stop=False),
                     reads=[("Vtok", i), ("Am", i)], writes=[("ps", 7)])
                S.op("pe", lambda e, h=h, cs=cs, s0b=s0b: e.matmul(ps[7][:, 128 + cs.start:128 + cs.stop], lhsT=s0b, rhs=hT[:, 8 + h, cs], start=False, stop=True),
                     reads=[("S0bb", r), ("hT", 8 + h)], writes=[("ps", 7)])
                S.op("dve", lambda e, i=i, b=b, vm=vm: e.tensor_scalar(out=vm, in0=Vtok[i], scalar1=seqm[:, b:b + 1], scalar2=None, op0=ALU.mult),
                     reads=[("Vtok", i), "cst"], writes=[("Vm", n % 2)])
                bu = 3 if n % 2 == 0 else 1
                S.op("pe", lambda e, i=i, vm=vm, bu=bu: e.matmul(ps[bu][:, 0:128], lhsT=Ktok[i], rhs=vm, start=True, stop=True),
                     reads=[("Ktok", i), ("Vm", n % 2)], writes=[("ps", bu)])
                S.op("dve", lambda e, s0f=s0f, ebl=ebl: e.tensor_scalar(out=s0f, in0=s0f, scalar1=ebl, scalar2=None, op0=ALU.mult),
                     reads=[("S0fb", r), ("S0bb", r), ("EBL", h)], writes=[("S0fb", r)])
                S.op("dve", lambda e, s0f=s0f, snw=snw, ebl=ebl, bu=bu: e.scalar_tensor_tensor(out=snw, in0=ps[bu][:, 0:128], scalar=ebl, in1=s0f,
                                                                                           op0=ALU.mult, op1=ALU.add),
                     reads=[("ps", bu), ("S0fb", r), ("EBL", h)], writes=[("Snewb", r)])
                S.dma("sp", lambda e, snw=snw, b=b, h=h: e.dma_start(out=st_s[b, h], in_=snw), ("Snew", r), reads=[("Snewb", r)])
            hgrn_finish(h, 0, i)
        for hh in range(8):
            if "ostr" in SKIP:
                continue
            g, j = divmod(hh, 4)
            i = hh % 2
            S.op("act", lambda e, g=g, j=j, i=i: e.copy(out=Ef[i][0:65, 256:384].rearrange("p (b t) -> p b t", b=16), in_=ps[4 + g][0:65, :].rearrange("p (b j t) -> p b j t", b=16, j=4)[:, :, j, :]),
                 reads=[("ps", 4 + g)], writes=[("Ef", i)])
            S.op("pe", lambda e, g=g, j=j, i=i: e.transpose(ps[2 + g][:, j * 65:(j + 1) * 65], Ef[i][0:65, 256:384], cst[0:65, C_ID:C_ID + 65]),
                 reads=[("Ef", i), "cst"], writes=[("ps", 2 + g)])
        if "af" not in SKIP:
            attn_finish(0, (2, 3))

    def v_to_tok(m, slot):
        S.op("pe", lambda e: e.transpose(psTb[:, 512:640], hT[:, 16, m * 128:(m + 1) * 128], identb[:]),
             reads=[("hT", 16), "identb"], writes=[("ps", 6)])
        S.op("act", lambda e: e.copy(out=Vaug[:, slot, :, 0:64], in_=psTb[:, 512:640].rearrange("p (g d) -> p g d", g=2)),
             reads=[("ps", 6), "Vaug_all"], writes=[("Vaug", slot)])

    def hgrn_head_common(h, m, am_off):
        i = h % 2
        S.op("pe", lambda e: e.transpose(psTb[:, 640:768], hT[:, 12 + h, m * 128:(m + 1) * 128], identb[:]),
             reads=[("hT", 12 + h), "identb"], writes=[("ps", 6)])
        S.op("act", lambda e: e.copy(out=Ktok[i], in_=psTb[:, 640:768]), reads=[("ps", 6)], writes=[("Ktok", i)])
        S.op("pe", lambda e: e.transpose(psTb[:, 768:896], hT[:, 4 + h, m * 128:(m + 1) * 128], identb[:]),
             reads=[("hT", 4 + h), "identb"], writes=[("ps", 6)])
        S.op("act", lambda e: e.copy(out=Vtok[i], in_=psTb[:, 768:896]), reads=[("ps", 6)], writes=[("Vtok", i)])
        S.op("pe", lambda e: e.matmul(ps[7][:, 0:128], lhsT=hT[:, 12 + h, m * 128:(m + 1) * 128], rhs=hT[:, 8 + h, m * 128:(m + 1) * 128],
                                      start=True, stop=True),
             reads=[("hT", 12 + h), ("hT", 8 + h)], writes=[("ps", 7)])
        S.op("dve", lambda e: e.tensor_tensor(out=Am[i], in0=ps[7][:, 0:128], in1=cst[:, am_off:am_off + 128], op=ALU.mult),
             reads=[("ps", 7), "cst"], writes=[("Am", i)])
        return i

    def hgrn_finish(h, m, i):
        S.op("act", lambda e: e.copy(out=ohf[i][:], in_=ps[7][:, 128:256]), reads=[("ps", 7)], writes=[("ohf", i)])
        S.op("act", lambda e: e.activation(out=osq[i][:], in_=ps[7][:, 128:256], func=AF.Square), reads=[("ps", 7)], writes=[("osq", i)])
        S.op("pe", lambda e: e.matmul(ps[7][:, 384:512], lhsT=onesf, rhs=osq[i][:], start=True, stop=True),
             reads=["cst", ("osq", i)], writes=[("ps", 7)])
        S.op("act", lambda e: e.activation(out=ors[i][:], in_=ps[7][:, 384:512], func=AF.Ln, bias=eps_rms, scale=1.0 / 128.0),
             reads=[("ps", 7), "cst"], writes=[("ors", i)])
        S.op("act", lambda e: e.activation(out=ors[i][:], in_=ors[i][:], func=AF.Exp, scale=-0.5), reads=[("ors", i)], writes=[("ors", i)])
        S.op("dve", lambda e: e.scalar_tensor_tensor(out=ohf[i][:], in0=ohf[i][:], scalar=hngt[:, h:h + 1], in1=ors[i][:], op0=ALU.mult, op1=ALU.mult),
             reads=[("ohf", i), ("ors", i), "hngt"], writes=[("ohf", i)])
        S.op("dve", lambda e: e.tensor_tensor(out=mixT[:, 4 + h, m * 128:(m + 1) * 128], in0=ohf[i][:], in1=G[:, h, m * 128:(m + 1) * 128], op=ALU.mult),
             reads=[("ohf", i), ("G", h)], writes=[("mixT", "h", m)] if h == 3 else [("mixT", "hh", h, m)])

    def prompt_attn_a(m, pos, seq, gsel=(0, 1)):
        gb = pos * 4 + m
        if 0 in gsel:
            v_to_tok(m, 1 + m)
        for g in gsel:
            for kh in range(2):
                if kh == 0 and gb == 0:
                    continue
                bk = 2 + kh
                S.op("pe", lambda e, g=g, kh=kh, bk=bk: e.matmul(
                    ps[bk][:], lhsT=KT[64 * g:64 * g + 64, m + kh, :], rhs=hT[64 * g:64 * g + 64, 0:4, m * 128:(m + 1) * 128],
                    start=True, stop=True),
                    reads=[("KT", m + kh)] + [("hT", c) for c in range(4)], writes=[("ps", bk)])
                S.op("act", lambda e, kh=kh, bk=bk: e.activation(out=Ef[kh][:], in_=ps[bk][:], func=AF.Exp, scale=SCALE),
                     reads=[("ps", bk)], writes=[("Ef", kh)])
                o = C_DMP + (kh * 2 + g) * 512
                S.op("dve", lambda e, kh=kh, g=g, o=o: e.tensor_tensor(out=Pb[:, kh * 2 + g, :], in0=Ef[kh][:], in1=cst[:, o:o + 512], op=ALU.mult),
                     reads=[("Ef", kh), "cst"], writes=[("Pb", kh * 2 + g)])

    def prompt_attn_b(m, pos, seq):
        gb = pos * 4 + m
        for g in range(2):
            for j in range(4):
                khs = [1] if gb == 0 else [0, 1]
                for n_, kh in enumerate(khs):
                    S.op("pe", lambda e, g=g, j=j, kh=kh, n_=n_, khs=khs: e.matmul(
                        ps[4 + g][:, j * 65:(j + 1) * 65], lhsT=Pb[:, kh * 2 + g, j * 128:(j + 1) * 128], rhs=Vaug[:, m + kh, g, :],
                        start=(n_ == 0), stop=(n_ == len(khs) - 1)),
                        reads=[("Pb", kh * 2 + g), ("Vaug", m + kh), "Vaug_all"], writes=[("ps", 4 + g)])
        attn_finish(m, (4, 5), split=True)

    def prompt_attn_c(m, pos, seq):
        gb = pos * 4 + m
        attn_finish_c(m)
        if pos == 3 and m == 3:
            for w_, dst in ((0, ck_p), (1, cv_p)):
                S.op("pe", lambda e, w_=w_: e.transpose(ps[3][:, w_ * 128:(w_ + 1) * 128], kvf[:, w_, :], idf), reads=[("kvf", w_), "cst"], writes=[("ps", 3)])
                S.op("act", lambda e, w_=w_: e.copy(out=kvt[:, w_, :], in_=ps[3][:, w_ * 128:(w_ + 1) * 128]), reads=[("ps", 3)], writes=[("kvt", w_)])
                S.dma("sp", lambda e, w_=w_, dst=dst: e.dma_start(out=dst[seq], in_=kvt[:, w_, :]), ("kvt", w_), reads=[("kvt", w_)])
    def OB(h):
        return 2 + h // 2

    def oc(h):
        return (h % 2) * 128

    def ac(h):
        return 256 + (h % 2) * 128


    def prompt_hgrn1(m, pos, seq):
        gb = pos * 4 + m
        par = m % 2
        blk = slice(m * 128, (m + 1) * 128)
        for h in range(4):
            S.op("pe", lambda e, h=h: e.transpose(psTb[:, h * 128:(h + 1) * 128], hT[:, 12 + h, blk], identb[:]),
                 reads=[("hT", 12 + h), "identb"], writes=[("ps", 6)])
        for h in range(4):
            S.op("pe", lambda e, h=h: e.transpose(psTb[:, 512 + h * 128:512 + (h + 1) * 128], hT[:, 4 + h, blk], identb[:]),
                 reads=[("hT", 4 + h), "identb"], writes=[("ps", 6)])
        S.op("act", lambda e: e.copy(out=Ktok4[par][:].rearrange("p h t -> p (h t)"), in_=psTb[:, 0:512]), reads=[("ps", 6)], writes=[("Ktok4", par)])
        S.op("act", lambda e: e.copy(out=Vtok4[par][:].rearrange("p h t -> p (h t)"), in_=psTb[:, 512:1024]), reads=[("ps", 6)], writes=[("Vtok4", par)])
        for h in range(4):
            B = OB(h)
            S.op("pe", lambda e, h=h, B=B: e.matmul(ps[B][:, ac(h):ac(h) + 128], lhsT=hT[:, 12 + h, blk], rhs=hT[:, 8 + h, blk], start=True, stop=True),
                 reads=[("hT", 12 + h), ("hT", 8 + h)], writes=[("ps", B)])
            S.op("dve", lambda e, h=h, B=B: e.tensor_tensor(out=Am4[:, h, :], in0=ps[B][:, ac(h):ac(h) + 128], in1=cst[:, C_AM:C_AM + 128], op=ALU.mult),
                 reads=[("ps", B), "cst"], writes=[("Am4", h)])

    def prompt_hgrn2(m, pos, seq):
        gb = pos * 4 + m
        par = m % 2
        blk = slice(m * 128, (m + 1) * 128)
        for c in range(2):
            first = (gb == 0 and c == 0)
            cs = slice(m * 128 + c * 64, m * 128 + (c + 1) * 64)
            if not first:
                for h in range(4):
                    ebl0 = EBL[:, h, 2 * m + c:2 * m + c + 1]
                    S.op("dve", lambda e, h=h, ebl0=ebl0: e.tensor_scalar(out=Sf[:, h, :], in0=Sf[:, h, :], scalar1=ebl0, scalar2=None, op0=ALU.mult),
                         reads=[("Sf", h), ("EBL", h)], writes=[("Sf", h)])
            for h in range(4):
                B = OB(h)
                S.op("pe", lambda e, h=h, B=B, c=c, first=first: e.matmul(ps[B][:, oc(h) + c * 64:oc(h) + (c + 1) * 64], lhsT=Vtok4[par][:, h, :],
                                                    rhs=Am4[:, h, c * 64:(c + 1) * 64], start=True, stop=first),
                     reads=[("Vtok4", par), ("Am4", h)], writes=[("ps", B)])
                if not first:
                    S.op("pe", lambda e, h=h, B=B, c=c, cs=cs: e.matmul(ps[B][:, oc(h) + c * 64:oc(h) + (c + 1) * 64], lhsT=Sb[:, h, :], rhs=hT[:, 8 + h, cs],
                                                        start=False, stop=True),
                         reads=[("Sb", h), ("hT", 8 + h)], writes=[("ps", B)])
                S.op("pe", lambda e, h=h, c=c: e.matmul(ps[4][:, h * 128:(h + 1) * 128], lhsT=Ktok4[par][64 * c:64 * c + 64, h, :],
                                                    rhs=Vtok4[par][64 * c:64 * c + 64, h, :], start=True, stop=True),
                     reads=[("Ktok4", par), ("Vtok4", par)], writes=[("ps", 4)])
            for h in range(4):
                B = 4
                ebl = EBL[:, h, 2 * m + c:2 * m + c + 1]
                if first:
                    S.op("dve", lambda e, h=h, B=B, ebl=ebl: e.tensor_scalar(out=Sb[:, h, :], in0=ps[4][:, h * 128:(h + 1) * 128], scalar1=ebl, scalar2=None, op0=ALU.mult),
                         reads=[("ps", B), ("EBL", h)], writes=[("Sb", h)])
                    S.op("dve", lambda e, h=h, B=B, ebl=ebl: e.tensor_scalar(out=Sf[:, h, :], in0=ps[4][:, h * 128:(h + 1) * 128], scalar1=ebl, scalar2=None, op0=ALU.mult),
                         reads=[("ps", B), ("EBL", h)], writes=[("Sf", h)])
                else:
                    S.op("dve", lambda e, h=h, B=B, ebl=ebl: e.scalar_tensor_tensor(out=Sb[:, h, :], in0=ps[4][:, h * 128:(h + 1) * 128], scalar=ebl, in1=Sf[:, h, :],
                                                                               op0=ALU.mult, op1=ALU.add),
                         reads=[("ps", B), ("EBL", h), ("Sf", h)], writes=[("Sb", h)])
                    S.op("dve", lambda e, h=h, B=B, ebl=ebl: e.scalar_tensor_tensor(out=Sf[:, h, :], in0=ps[4][:, h * 128:(h + 1) * 128], scalar=ebl, in1=Sf[:, h, :],
                                                                               op0=ALU.mult, op1=ALU.add),
                         reads=[("ps", B), ("EBL", h), ("Sf", h)], writes=[("Sf", h)])

    def prompt_hgrn3(m, pos, seq):
        gb = pos * 4 + m
        par = m % 2
        blk = slice(m * 128, (m + 1) * 128)
        for h in range(4):
            B = OB(h)
            i = h % 2
            S.op("act", lambda e, B=B, i=i, h=h: e.copy(out=ohf[i][:], in_=ps[B][:, oc(h):oc(h) + 128]), reads=[("ps", B)], writes=[("ohf", i)])
            S.op("act", lambda e, B=B, i=i, h=h: e.activation(out=osq[i][:], in_=ps[B][:, oc(h):oc(h) + 128], func=AF.Square), reads=[("ps", B)], writes=[("osq", i)])
            S.op("pe", lambda e, i=i, h=h: e.matmul(ps[5][:, h * 128:(h + 1) * 128], lhsT=onesf, rhs=osq[i][:], start=True, stop=True),
                 reads=["cst", ("osq", i)], writes=[("ps", 5)])
            S.op("act", lambda e, i=i, h=h: e.activation(out=ors[i][:], in_=ps[5][:, h * 128:(h + 1) * 128], func=AF.Ln, bias=eps_rms, scale=1.0 / 128.0),
                 reads=[("ps", 5), "cst"], writes=[("ors", i)])
            S.op("act", lambda e, i=i: e.activation(out=ors[i][:], in_=ors[i][:], func=AF.Exp, scale=-0.5), reads=[("ors", i)], writes=[("ors", i)])
            S.op("dve", lambda e, h=h, i=i: e.scalar_tensor_tensor(out=ohf[i][:], in0=ohf[i][:], scalar=hngt[:, h:h + 1], in1=ors[i][:], op0=ALU.mult, op1=ALU.mult),
                 reads=[("ohf", i), ("ors", i), "hngt"], writes=[("ohf", i)])
            S.op("dve", lambda e, h=h, i=i: e.tensor_tensor(out=mixT[:, 4 + h, blk], in0=ohf[i][:], in1=G[:, h, blk], op=ALU.mult),
                 reads=[("ohf", i), ("G", h)], writes=[("mixT", "h", m)] if h == 3 else [("mixT", "hh", h, m)])
        if pos == 3 and m == 3:
            S.dma("sp", lambda e: e.dma_start(out=st_p[seq].rearrange("h k v -> k h v"), in_=Sf[:]), "stp",
                  reads=[("Sf", h) for h in range(4)])

    for ti, tl in enumerate(tinfo):
        kind, t, row0, NB = tl["kind"], tl["t"], tl["row0"], tl["NB"]
        N = NB * 128
        nxt = tinfo[ti + 1] if ti + 1 < len(tinfo) else None
        if ti == 0:
            ensure_xcast(tl)
        for m in range(NB):
            emit_xload(tl, m)
        for m in range(NB):
            emit_feat_stage(tl, m)
        if "A" in stages:
            ffn_stage("a", NB, row0, final=(stages == "A"), nxt=nxt)
        if "B" in stages:
            mixer_stage(kind, t, NB, row0, final=(stages[-1] == "B"))
        if "C" in stages:
            ffn_stage("c", NB, row0, final=True, nxt=nxt)

    flush_wb(0)
    S.emit(st)
    st.close()
    return nc


def _w13_layout(w13):
    g = w13[:, :DFF].reshape(8, 128, NPIECE_FF, 256)
    u = w13[:, DFF:].reshape(8, 128, NPIECE_FF, 256)
    gu = np.concatenate([g, u], axis=3)
    return np.ascontiguousarray(gu.transpose(2, 1, 0, 3)).reshape(NPIECE_FF, 128, 4096)


def _w2_layout(w2):
    return np.ascontiguousarray(w2.reshape(NJ, 128, D).transpose(1, 0, 2)).reshape(128, NJ * D)


def _win_layout(w_in):
    qa = w_in[:, 0:512].reshape(D, 8, 64)
    qperm = np.concatenate([np.concatenate([qa[:, j], qa[:, j + 4]], axis=1) for j in range(4)], axis=1)
    ka = w_in[:, 512:640]
    va = w_in[:, 640:768]
    hq = w_in[:, 768:1280]
    hf = w_in[:, 1280:1792]
    hi = w_in[:, 1792:2304]
    hg = w_in[:, 2304:2816]
    pad = np.zeros((D, 256), np.float32)
    pieces = [qperm, np.concatenate([ka, va, pad], axis=1), hq, hf, hi, hg]
    out = np.stack([p.reshape(8, 128, 512).transpose(1, 0, 2).reshape(128, 4096) for p in pieces])
    return np.ascontiguousarray(out)


def prep_shared(inp):
    sh = {}
    sh["w13a"] = _w13_layout(inp["ffn1_w13"][0])
    sh["w13b"] = _w13_layout(inp["ffn2_w13"][0])
    sh["w2a"] = _w2_layout(inp["ffn1_w2"][0])
    sh["w2b"] = _w2_layout(inp["ffn2_w2"][0])
    sh["win"] = _win_layout(inp["w_in"][0])
    sh["wout"] = np.ascontiguousarray(inp["w_out"][0].reshape(8, 128, D).transpose(1, 0, 2)).reshape(128, 8 * D)
    sh["lnp"] = np.ascontiguousarray(np.stack([inp["ln1_g"][0], inp["ln1_b"][0], inp["ln2_g"][0], inp["ln2_b"][0],
                                               inp["ln3_g"][0], inp["ln3_b"][0]]))
    sh["ang"] = np.ascontiguousarray(inp["attn_norm_g"][0].reshape(1, 512))
    sh["sink"] = np.ascontiguousarray(inp["attn_sink"][0].reshape(1, 8))
    sh["hng"] = np.ascontiguousarray(inp["hgrn_norm_g"][0].reshape(4, 128).T)
    sh["lbp"] = np.ascontiguousarray(inp["lb_param"].reshape(2, 4, 128).transpose(2, 0, 1).reshape(128, 8))
    sh["cst"] = make_consts()
    return sh


def kernel(**inputs):
    inp = {k: np.asarray(v) for k, v in inputs.items()}
    n = 8
    sh = prep_shared(inp)
    xp = inp["x_prompt"]
    xs = inp["x_sample"]
    in_maps = []
    for c in range(n):
        m = dict(sh)
        m["xin"] = np.ascontiguousarray(np.concatenate(
            [xp[2 * c:2 * c + 2].reshape(4096, D), xs[16 * c:16 * c + 16].reshape(128, D)], axis=0))
        m["st_in"] = np.ascontiguousarray(inp["state_hgrn"][0, 16 * c:16 * c + 16])
        m["ck_in"] = np.ascontiguousarray(inp["cache_win_k"][0, 16 * c:16 * c + 16].reshape(16, 128, 128))
        m["cv_in"] = np.ascontiguousarray(inp["cache_win_v"][0, 16 * c:16 * c + 16].reshape(16, 128, 128))
        in_maps.append(m)
    nc = build(n_ptiles=8, sample=True, stages="ABC")
    res = run_bass_kernel_spmd(nc, in_maps, core_ids=list(range(n)))
    R = res.results
    y_p = np.concatenate([r["y"][:4096].reshape(2, 2048, D) for r in R], axis=0)
    y_s = np.concatenate([r["y"][4096:].reshape(16, 8, D) for r in R], axis=0)
    st_p = np.concatenate([r["st_p"] for r in R], axis=0)[None]
    ck_p = np.concatenate([r["ck_p"].reshape(2, 128, 2, 64) for r in R], axis=0)[None]
    cv_p = np.concatenate([r["cv_p"].reshape(2, 128, 2, 64) for r in R], axis=0)[None]
    st_s = np.concatenate([r["st_s"] for r in R], axis=0)[None]
    ck_s = np.concatenate([r["ck_s"].reshape(16, 128, 2, 64) for r in R], axis=0)[None]
    cv_s = np.concatenate([r["cv_s"].reshape(16, 128, 2, 64) for r in R], axis=0)[None]
    f = lambda a: np.ascontiguousarray(a, dtype=np.float32)
    return (f(y_p), f(y_s), f(st_p), f(ck_p), f(cv_p), f(st_s), f(ck_s), f(cv_s))
```

```python
import numpy as np
import concourse.bass as bass
import concourse.mybir as mybir
from concourse.bass_utils import run_bass_kernel_spmd

F32 = mybir.dt.float32
BF16 = mybir.dt.bfloat16
AF = mybir.ActivationFunctionType
ALU = mybir.AluOpType
AX = mybir.AxisListType

ENGS = ("pe", "act", "dve", "pool", "sp")
EPOCH = 30000


class Sched:
    def __init__(self, nc):
        self.nc = nc
        self.prog = {e: [] for e in ENGS}
        self.known = {e: {} for e in ENGS}
        self.rw = {}
        self.rr = {}
        self.dcount = {}
        self.needed = {e: set() for e in ENGS if e != "sp"}

    def _need(self, eng, tok, waits):
        kind, k, v = tok
        if kind == "E" and k == eng and eng == "pe":
            return
        if kind == "D":
            v = max(v, self.dcount.get(k, 0))
            tok = (kind, k, v)
        kk = (kind, k)
        if self.known[eng].get(kk, 0) >= v:
            return
        self.known[eng][kk] = v
        waits.append(tok)
        if kind == "E":
            self.needed[k].add(v)

    def _deps(self, eng, reads, writes):
        waits = []
        for r in reads:
            for tok in self.rw.get(r, ()):
                self._need(eng, tok, waits)
            if isinstance(r, tuple) and r[0] == "ps":
                for tok in self.rr.get(r, ()):
                    if not (tok[0] == "E" and tok[1] == eng):
                        self._need(eng, tok, waits)
        for w in writes:
            for tok in self.rw.get(w, ()):
                self._need(eng, tok, waits)
            for tok in self.rr.get(w, ()):
                self._need(eng, tok, waits)
        return waits

    def _commit(self, tok, reads, writes):
        for r in reads:
            if r in writes:
                continue
            self.rr.setdefault(r, []).append(tok)
            if len(self.rr[r]) > 24:
                self.rr[r] = self._compact(self.rr[r])
        for w in writes:
            self.rw[w] = [tok]
            self.rr[w] = []

    @staticmethod
    def _compact(toks):
        best = {}
        for t in toks:
            kk = (t[0], t[1])
            if kk not in best or best[kk][2] < t[2]:
                best[kk] = t
        return list(best.values())

    def op(self, eng, fn, reads=(), writes=()):
        reads = tuple(reads)
        writes = tuple(writes)
        waits = self._deps(eng, reads, writes)
        idx = len(self.prog[eng]) + 1
        self.prog[eng].append(dict(waits=waits, fn=fn, dma=None))
        tok = ("E", eng, idx)
        self._commit(tok, reads, writes)
        return tok

    def dma(self, queue, fn, key, reads=(), writes=()):
        reads = tuple(reads)
        writes = tuple(writes)
        waits = self._deps(queue, reads, writes)
        n = self.dcount.get(key, 0) + 1
        self.dcount[key] = n
        tok = ("D", key, n)
        self.prog[queue].append(dict(waits=waits, fn=fn, dma=tok))
        self._commit(tok, reads, writes)
        return tok

    def emit(self, stack):
        nc = self.nc
        esem = {}
        nidx = {}
        for e in self.needed:
            need = sorted(self.needed[e])
            nidx[e] = {v: i + 1 for i, v in enumerate(need)}
            nep = (len(need) + EPOCH - 1) // EPOCH + 1
            esem[e] = [stack.enter_context(nc.semaphore(f"s_{e}_{k}")) for k in range(nep)]
        dsem = {k: stack.enter_context(nc.semaphore(f"d_{i}")) for i, k in enumerate(self.dcount)}

        def semval(tok):
            kind, k, v = tok
            if kind == "D":
                return dsem[k], 16 * v
            i = nidx[k][v]
            return esem[k][(i - 1) // EPOCH], (i - 1) % EPOCH + 1

        block = stack.enter_context(nc.Block())
        final = [("D", k, n) for k, n in self.dcount.items()]

        def run(eng_name):
            def body(eng):
                for i, rec in enumerate(self.prog[eng_name]):
                    for tok in rec["waits"]:
                        s, v = semval(tok)
                        eng.wait_ge(s, v)
                    ins = rec["fn"](eng)
                    if rec["dma"] is not None:
                        s, _ = semval(rec["dma"])
                        ins.then_inc(s, 16)
                    elif eng_name != "sp" and (i + 1) in nidx[eng_name]:
                        s, _ = semval(("E", eng_name, i + 1))
                        ins.then_inc(s, 1)
                if eng_name == "sp":
                    for tok in final:
                        s, v = semval(tok)
                        eng.wait_ge(s, v)
            return body

        block.tensor(run("pe"))
        block.scalar(run("act"))
        block.vector(run("dve"))
        block.gpsimd(run("pool"))
        block.sync(run("sp"))


D = 1024
DFF = 2816
NJ = 22
ALPHA = 2.0 ** 0.25
LN_EPS = 1e-5
RMS_EPS = 1e-6
SCALE = 0.125
NPIECE_FF = 11
NPIECE_IN = 6

C_ID = 0
C_DMP = 128
C_SCAN = C_DMP + 2048
C_AM = C_SCAN + 256
C_DSC = C_AM + 256
C_DSN = C_DSC + 64
C_ONES = C_DSN + 64
C_SEQM = C_ONES + 128
C_EPS = C_SEQM + 16
NCST = C_EPS + 2


def make_consts():
    c = np.zeros((128, NCST), np.float32)
    c[:, C_ID:C_ID + 128] = np.eye(128, dtype=np.float32)
    slopes = (2.0 ** -(np.arange(1, 9, dtype=np.float64))).reshape(2, 4)
    k = np.arange(128)[:, None]
    q = np.arange(128)[None, :]
    for kh in range(2):
        for g in range(2):
            dist = (q - k) + (128 if kh == 0 else 0)
            valid = (dist >= 0) & (dist < 128)
            blk = np.zeros((128, 4, 128), np.float64)
            for j in range(4):
                blk[:, j, :] = np.where(valid, np.exp(-slopes[g, j] * dist), 0.0)
            o = C_DMP + (kh * 2 + g) * 512
            c[:, o:o + 512] = blk.reshape(128, 512)
    m64 = np.ones(128, np.float32); m64[::64] = 0
    m8 = np.ones(128, np.float32); m8[::8] = 0
    c[:, C_SCAN:C_SCAN + 128] = m64[None, :]
    c[:, C_SCAN + 128:C_SCAN + 256] = m8[None, :]
    s = np.arange(128)[:, None]
    t = np.arange(128)[None, :]
    c[:, C_AM:C_AM + 128] = ((s // 64 == t // 64) & (s <= t)).astype(np.float32)
    c[:, C_AM + 128:C_AM + 256] = ((s // 8 == t // 8) & (s <= t)).astype(np.float32)
    for g in range(2):
        blk = np.zeros((128, 4, 8), np.float64)
        blkn = np.zeros((128, 4, 8), np.float64)
        for j in range(4):
            for tt in range(8):
                dist = 128 + tt - np.arange(128)
                blk[:, j, tt] = np.where((dist >= 0) & (dist < 128), np.exp(-slopes[g, j] * dist), 0.0)
                distn = tt - (np.arange(128) % 8)
                blkn[:, j, tt] = np.where(distn >= 0, np.exp(-slopes[g, j] * np.maximum(distn, 0)), 0.0)
        c[:, C_DSC + g * 32:C_DSC + (g + 1) * 32] = blk.reshape(128, 32)
        c[:, C_DSN + g * 32:C_DSN + (g + 1) * 32] = blkn.reshape(128, 32)
    c[:, C_ONES:C_ONES + 128] = 1.0
    c[:, C_SEQM:C_SEQM + 16] = (np.arange(128)[:, None] // 8 == np.arange(16)[None, :]).astype(np.float32)
    c[:, C_EPS] = LN_EPS / (ALPHA * ALPHA)
    c[:, C_EPS + 1] = RMS_EPS
    return c


def build(n_ptiles=8, sample=True, stages="ABC"):
    from contextlib import ExitStack
    nc = bass.Bass("TRN2", target_bir_lowering=False)
    NT = 512 * n_ptiles + (128 if sample else 0)
    NSEQ = max(1, (n_ptiles + 3) // 4)

    def din(name, shape):
        return nc.dram_tensor(name, shape, F32, kind="ExternalInput").ap()

    def dout(name, shape):
        return nc.dram_tensor(name, shape, F32, kind="ExternalOutput").ap()

    xin = din("xin", [NT, D])
    w13 = {"a": din("w13a", [NPIECE_FF, 128, 4096]), "c": din("w13b", [NPIECE_FF, 128, 4096])}
    w2 = {"a": din("w2a", [128, NJ * 1024]), "c": din("w2b", [128, NJ * 1024])}
    win = din("win", [NPIECE_IN, 128, 4096])
    wout = din("wout", [128, 8 * 1024])
    lnp = din("lnp", [6, D])
    ang = din("ang", [1, 512])
    sink = din("sink", [1, 8])
    hng = din("hng", [128, 4])
    lbp = din("lbp", [128, 8])
    cst_d = din("cst", [128, NCST])
    st_in = din("st_in", [16, 4, 128, 128])
    ck_in = din("ck_in", [16, 128, 128])
    cv_in = din("cv_in", [16, 128, 128])
    y = dout("y", [NT, D])
    st_p = dout("st_p", [2, 4, 128, 128])
    ck_p = dout("ck_p", [2, 128, 128])
    cv_p = dout("cv_p", [2, 128, 128])
    st_s = dout("st_s", [16, 4, 128, 128])
    ck_s = dout("ck_s", [16, 128, 128])
    cv_s = dout("cv_s", [16, 128, 128])

    scr = {
        ("A",): nc.dram_tensor("scr_w13a", [NPIECE_FF, 128, 4096], BF16).ap(),
        ("C",): nc.dram_tensor("scr_w13b", [NPIECE_FF, 128, 4096], BF16).ap(),
        ("B",): nc.dram_tensor("scr_win", [NPIECE_IN, 128, 4096], BF16).ap(),
        ("w2", "a"): nc.dram_tensor("scr_w2a", [128, NJ * 1024], BF16).ap(),
        ("w2", "c"): nc.dram_tensor("scr_w2b", [128, NJ * 1024], BF16).ap(),
        ("wout",): nc.dram_tensor("scr_wout", [128, 8 * 1024], BF16).ap(),
    }
    st = ExitStack()
    S = Sched(nc)
    SKIP = set()

    def sb(name, shape, dt):
        return st.enter_context(nc.sbuf_tensor(name, shape, dt))

    cst = sb("cst_sb", [128, NCST], F32)
    identb = sb("identb", [128, 128], BF16)
    msk512 = sb("msk512", [128, 512], F32)
    lnbuf = sb("lnbuf", [128, 2048], F32)
    angb = sb("angb", [128, 512], F32)
    esink = sb("esink", [128, 8], F32)
    hngt = sb("hngt", [128, 4], F32)
    lbt = sb("lbt", [128, 8], F32)
    lbv = sb("lbv", [128, 16], F32)
    xres = sb("xres", [128, 4, D], F32)
    rbuf = sb("rbuf", [128, D], F32)
    xb = sb("xb", [128, D], BF16)
    xT = sb("xT", [128, 8, 512], BF16)
    hT = sb("hT", [128, NJ, 512], BF16)
    Wbig = sb("Wbig", [128, NJ * 1024], BF16)
    wst = [sb(f"wst{i}", [128, 4096], BF16) for i in range(3)]
    Wout = sb("Wout", [128, 8 * 1024], BF16)
    sil = [sb(f"sil{i}", [128, 512], F32) for i in range(2)]
    hq_s = sb("hq_s", [128, 4, 512], BF16)
    G = sb("G", [128, 4, 512], BF16)
    KT = sb("KT", [128, 5, 128], BF16)
    Vaug = sb("Vaug", [128, 5, 2, 65], BF16)
    Ef = [sb(f"Ef{i}", [128, 512], F32) for i in range(2)]
    Pb = sb("Pb", [128, 4, 512], BF16)
    oa = sb("oa", [128, 512], F32)
    oan = sb("oan", [128, 512], BF16)
    mixT = sb("mixT", [128, 8, 512], BF16)
    Ktok4 = [sb(f"Ktok4_{i}", [128, 4, 128], BF16) for i in range(2)]
    Vtok4 = [sb(f"Vtok4_{i}", [128, 4, 128], BF16) for i in range(2)]
    Am4 = sb("Am4", [128, 4, 128], BF16)
    Ktok = [Ktok4[0][:, i, :] for i in range(2)]
    Vtok = [Vtok4[0][:, i, :] for i in range(2)]
    Am = [Am4[:, i, :] for i in range(2)]
    Sf = sb("Sf", [128, 4, 128], F32)
    Sb = sb("Sb", [128, 4, 128], BF16)
    EBL = sb("EBL", [128, 4, 16], F32)
    ohf = [sb(f"ohf{i}", [128, 128], F32) for i in range(2)]
    osq = [sb(f"osq{i}", [128, 128], F32) for i in range(2)]
    ors = [sb(f"ors{i}", [128, 128], F32) for i in range(2)]
    kvf = sb("kvf", [128, 2, 128], F32)
    kvt = sb("kvt", [128, 2, 128], F32)
    st6 = sb("st6", [128, 12], F32)
    mv = sb("mv", [128, 2], F32)
    sm = sb("sm", [128, 8], F32)
    den = sb("den", [128, 16], F32)
    xres_b = xres.bitcast(BF16)
    S0f = [xres[:, 1, i * 512:(i + 1) * 512].rearrange("p (h v) -> p h v", h=4) for i in range(2)]
    Snew = [xres[:, 2, i * 512:(i + 1) * 512].rearrange("p (h v) -> p h v", h=4) for i in range(2)]
    S0b = [xres_b[:, 3, i * 512:(i + 1) * 512].rearrange("p (h v) -> p h v", h=4) for i in range(2)]
    Vm = [xres_b[:, 3, 1024 + i * 128:1024 + (i + 1) * 128] for i in range(2)]
    KTs = [xres_b[:, 3, 1280 + i * 128:1280 + (i + 1) * 128] for i in range(2)]
    Pc = [xres_b[:, 3, 1536 + i * 64:1536 + (i + 1) * 64] for i in range(2)]
    Pn = [xres_b[:, 3, 1664 + i * 64:1664 + (i + 1) * 64] for i in range(2)]
    SAMPLE_KEYS = ([(n, i) for n in ("Vm", "KTs", "Pc", "Pn") for i in range(2)]
                   + [(n, r) for n in ("S0fb", "S0bb", "Snewb") for r in range(8)])

    ps = [st.enter_context(nc.psum_tensor(f"ps{i}", [128, 512], F32)) for i in range(8)]
    psTb = ps[6].bitcast(BF16)

    wstf = [w.bitcast(F32) for w in wst]

    tiles = [("p", t, 512 * t) for t in range(n_ptiles)] + ([("s", 0, 512 * n_ptiles)] if sample else [])
    tinfo = [dict(kind=k_, t=t_, row0=r_, NB=(4 if k_ == "p" else 1), loaded=set(), feat=set()) for (k_, t_, r_) in tiles]
    plan = []
    piece_pos = []
    for ti, tl in enumerate(tiles):
        for stg in "ABC":
            if stg not in stages:
                continue
            npieces = NPIECE_IN if stg == "B" else NPIECE_FF
            for p in range(npieces):
                piece_pos.append(len(plan))
                plan.append(("piece", stg, p, len(piece_pos) - 1, ti))
                if stg == "A" and p in (3, 5, 7, 9):
                    plan.append(("w2q", "a", (p - 3) // 2, ti))
                if stg == "B" and p == 1:
                    plan.append(("wout", ti))
                if stg == "B" and p == NPIECE_IN - 1 and "C" in stages:
                    for q_ in range(4):
                        plan.append(("w2q", "c", q_, ti))
                if stg == "C" and "B" not in stages and p in (3, 5, 7, 9):
                    plan.append(("w2q", "c", (p - 3) // 2, ti))
                if stg == stages[-1] and p == npieces - 1 and ti + 1 < len(tiles):
                    plan.append(("xcast", ti + 1))
    state = dict(cursor=0, npiece=0, limit=len(plan))
    xcast_done = set()

    def ensure_xcast(tl):
        if id(tl) not in xcast_done:
            xcast_done.add(id(tl))
            emit_xcast(tl)

    pending_wb = []

    def flush_wb(keep=0):
        while len(pending_wb) > keep:
            fn, rk, wk = pending_wb.pop(0)
            S.dma("pool", fn, "scrwb", reads=rk, writes=wk)

    NCONV = max(1, min(2, n_ptiles))

    def emit_plan_entry(ent):
        kind = ent[0]
        if kind == "piece":
            _, stg, p, n, ti_ = ent
            slot = n % 3
            ct = p % NCONV
            if ti_ <= ct:
                src = win[p] if stg == "B" else w13["a" if stg == "A" else "c"][p]
                S.dma("pool", lambda e, slot=slot, src=src: e.dma_start(out=wst[slot][:], in_=src),
                      ("wst", slot), writes=[("wst", slot)])
            else:
                src = scr[(stg,)][p]
                S.dma("pool", lambda e, slot=slot, src=src: e.dma_start(out=wst[slot][:], in_=src),
                      ("wst", slot), reads=[("scr", stg, p)], writes=[("wst", slot)])
            flush_wb(0)
            if ti_ == ct:
                dst = scr[(stg,)][p]
                pending_wb.append((lambda e, slot=slot, dst=dst: e.dma_start(out=dst, in_=wst[slot][:]),
                                   [("wst", slot)], [("scr", stg, p)]))
        elif kind == "w2q":
            _, x_, q, ti_ = ent
            a, b_ = q * 5632, (q + 1) * 5632
            ct = q % NCONV
            if ti_ <= ct:
                src = w2[x_]
                S.dma("pool", lambda e, a=a, b_=b_, src=src: e.dma_start(out=Wbig[:, a:b_], in_=src[:, a:b_]),
                      ("Wbig", q), writes=[("Wbig", q)])
                if ti_ == ct:
                    dst = scr[("w2", x_)]
                    pending_wb.append((lambda e, a=a, b_=b_, dst=dst: e.dma_start(out=dst[:, a:b_], in_=Wbig[:, a:b_]),
                                       [("Wbig", q)], [("scr", "w2", x_, q)]))
            else:
                src = scr[("w2", x_)]
                S.dma("pool", lambda e, a=a, b_=b_, src=src: e.dma_start(out=Wbig[:, a:b_], in_=src[:, a:b_]),
                      ("Wbig", q), reads=[("scr", "w2", x_, q)], writes=[("Wbig", q)])
        elif kind == "xcast":
            ensure_xcast(tinfo[ent[1]])
        elif kind == "wout":
            ti_ = ent[1]
            for q in range(2):
                a, b_ = q * 4096, (q + 1) * 4096
                ct = (2 + q) % NCONV
                if ti_ <= ct:
                    S.dma("pool", lambda e, a=a, b_=b_: e.dma_start(out=Wout[:, a:b_], in_=wout[:, a:b_]),
                          ("Wout", q), writes=[("Wout", q)])
                    if ti_ == ct:
                        dst = scr[("wout",)]
                        pending_wb.append((lambda e, a=a, b_=b_, dst=dst: e.dma_start(out=dst[:, a:b_], in_=Wout[:, a:b_]),
                                           [("Wout", q)], [("scr", "wout", q)]))
                else:
                    src = scr[("wout",)]
                    S.dma("pool", lambda e, a=a, b_=b_, src=src: e.dma_start(out=Wout[:, a:b_], in_=src[:, a:b_]),
                          ("Wout", q), reads=[("scr", "wout", q)], writes=[("Wout", q)])

    def ensure(k):
        k = min(k, state["limit"] - 1, len(plan) - 1)
        while state["cursor"] <= k:
            emit_plan_entry(plan[state["cursor"]])
            state["cursor"] += 1

    def next_piece():
        n = state["npiece"]
        state["npiece"] += 1
        ensure(piece_pos[min(n + 2, len(piece_pos) - 1)])
        assert state["cursor"] > piece_pos[n], "piece not emitted (limit too tight)"
        return n % 3

    def wbig_keys(j):
        return [("Wbig", (j * 1024) // 5632), ("Wbig", (j * 1024 + 1023) // 5632)]

    S.dma("sp", lambda e: e.dma_start(out=cst[:], in_=cst_d), "cst", writes=["cst"])
    S.dma("sp", lambda e: e.dma_start(out=angb[:], in_=ang[0:1, :].to_broadcast([128, 512])), "angb", writes=["angb"])
    S.dma("sp", lambda e: e.dma_start(out=esink[:], in_=sink[0:1, :].to_broadcast([128, 8])), "esink", writes=["esink"])
    S.dma("sp", lambda e: e.dma_start(out=hngt[:], in_=hng), "hngt", writes=["hngt"])
    S.dma("sp", lambda e: e.dma_start(out=lbt[:], in_=lbp), "lbt", writes=["lbt"])
    S.op("dve", lambda e: e.tensor_copy(out=identb[:], in_=cst[:, C_ID:C_ID + 128]), reads=["cst"], writes=["identb"])
    for r in range(4):
        S.op("dve", lambda e, r=r: e.tensor_copy(out=msk512[:, r * 128:(r + 1) * 128], in_=cst[:, C_SCAN:C_SCAN + 128]),
             reads=["cst"], writes=[("msk512", r)])
    S.op("act", lambda e: e.activation(out=esink[:], in_=esink[:], func=AF.Exp), reads=["esink"], writes=["esink"])
    S.op("dve", lambda e: e.tensor_tensor(out=lbv[:, 12:16], in0=lbt[:, 4:8], in1=lbt[:, 0:4], op=ALU.subtract),
         reads=["lbt"], writes=["lbv"])
    S.op("act", lambda e: e.activation(out=lbv[:, 12:16], in_=lbv[:, 12:16], func=AF.Exp), reads=["lbv"], writes=["lbv"])
    S.op("dve", lambda e: e.tensor_scalar(out=lbv[:, 12:16], in0=lbv[:, 12:16], scalar1=1.0, scalar2=None, op0=ALU.add),
         reads=["lbv"], writes=["lbv"])
    S.op("dve", lambda e: e.reciprocal(out=lbv[:, 0:4], in_=lbv[:, 12:16]), reads=["lbv"], writes=["lbv"])
    S.op("dve", lambda e: e.tensor_scalar(out=lbv[:, 4:8], in0=lbv[:, 0:4], scalar1=-1.0, scalar2=1.0, op0=ALU.mult, op1=ALU.add),
         reads=["lbv"], writes=["lbv"])
    S.op("dve", lambda e: e.tensor_scalar(out=lbv[:, 8:12], in0=lbv[:, 4:8], scalar1=-1.0, scalar2=None, op0=ALU.mult),
         reads=["lbv"], writes=["lbv"])
    S.op("dve", lambda e: e.memset(Vaug[:], 1.0), writes=["Vaug_all"])

    eps_ln = cst[:, C_EPS:C_EPS + 1]
    eps_rms = cst[:, C_EPS + 1:C_EPS + 2]
    idf = cst[:, C_ID:C_ID + 128]
    onesf = cst[:, C_ONES:C_ONES + 128]

    def to_featmajor(NB, m, src_key):
        S.op("dve", lambda e: e.tensor_copy(out=xb[:], in_=xres[:, m, :]), reads=[src_key], writes=["xb"])
        for kc in range(8):
            S.op("pe", lambda e, kc=kc: e.transpose(psTb[:, kc * 128:(kc + 1) * 128], xb[:, kc * 128:(kc + 1) * 128], identb[:]),
                 reads=["xb", "identb"], writes=[("ps", 6)])
        S.op("act", lambda e: e.copy(out=xT[:, :, m * 128:(m + 1) * 128],
                                     in_=psTb[:, :].rearrange("p (k t) -> p k t", k=8)),
             reads=[("ps", 6)], writes=[("xT", m)])

    def load_ln(idx):
        S.dma("sp", lambda e: e.dma_start(out=lnbuf[:, 0:1024], in_=lnp[2 * idx:2 * idx + 1, :].to_broadcast([128, 1024])),
              "lnbuf_g", writes=["lnbuf_g"])
        S.dma("sp", lambda e: e.dma_start(out=lnbuf[:, 1024:2048], in_=lnp[2 * idx + 1:2 * idx + 2, :].to_broadcast([128, 1024])),
              "lnbuf_b", writes=["lnbuf_b"])

    def ln_epilogue(m, psd, scale):
        xk = ("xres", m)
        for h in range(2):
            S.op("dve", lambda e, h=h: e.scalar_tensor_tensor(out=xres[:, m, h * 512:(h + 1) * 512], in0=ps[psd[h]][:], scalar=scale / ALPHA,
                                                          in1=xres[:, m, h * 512:(h + 1) * 512], op0=ALU.mult, op1=ALU.add),
                 reads=[("ps", psd[h]), xk], writes=[xk])
        S.op("dve", lambda e: e.bn_stats(out=st6[:, 0:6], in_=xres[:, m, 0:512]), reads=[xk], writes=["st6a"])
        S.op("dve", lambda e: e.bn_stats(out=st6[:, 6:12], in_=xres[:, m, 512:1024]), reads=[xk], writes=["st6b"])
        S.op("dve", lambda e: e.bn_aggr(out=mv[:], in_=st6[:]), reads=["st6a", "st6b"], writes=["mv"])
        S.op("act", lambda e: e.activation(out=sm[:, 0:1], in_=mv[:, 1:2], func=AF.Ln, bias=eps_ln), reads=["mv", "cst"], writes=["sm0"])
        S.op("act", lambda e: e.activation(out=sm[:, 1:2], in_=sm[:, 0:1], func=AF.Exp, scale=-0.5), reads=["sm0"], writes=["sm1"])
        S.op("dve", lambda e: e.tensor_scalar(out=xres[:, m, :], in0=xres[:, m, :], scalar1=mv[:, 0:1], scalar2=sm[:, 1:2],
                                              op0=ALU.subtract, op1=ALU.mult),
             reads=[xk, "mv", "sm1"], writes=[xk])
        S.op("dve", lambda e: e.tensor_tensor(out=xres[:, m, :], in0=xres[:, m, :], in1=lnbuf[:, 0:1024], op=ALU.mult),
             reads=[xk, "lnbuf_g"], writes=[xk])
        S.op("dve", lambda e: e.tensor_tensor(out=xres[:, m, :], in0=xres[:, m, :], in1=lnbuf[:, 1024:2048], op=ALU.add),
             reads=[xk, "lnbuf_b"], writes=[xk])

    xstage = mixT[:].rearrange("p (m a) t -> p m (a t)", m=4)
    ALLMIX = ([("mixT", "a", mm) for mm in range(4)] + [("mixT", "h", mm) for mm in range(4)]
              + [("mixT", "hh", hh_, mm) for hh_ in range(3) for mm in range(4)])

    def emit_xcast(tl):
        for m in range(tl["NB"]):
            r0 = tl["row0"] + m * 128
            S.dma("pool", lambda e, m=m, r0=r0: e.dma_start(out=xstage[:, m, :], in_=xin[r0:r0 + 128, :]), ("xstage", m),
                  writes=ALLMIX + [("xstage", m)])

    def emit_feat_stage(tl, m):
        if m in tl["feat"]:
            return
        tl["feat"].add(m)
        for kc in range(8):
            S.op("pe", lambda e, kc=kc: e.transpose(psTb[:, kc * 128:(kc + 1) * 128], xstage[:, m, kc * 128:(kc + 1) * 128], identb[:]),
                 reads=ALLMIX + [("xstage", m), "identb"], writes=[("ps", 6)])
        S.op("act", lambda e: e.copy(out=xT[:, :, m * 128:(m + 1) * 128], in_=psTb[:, :].rearrange("p (k t) -> p k t", k=8)),
             reads=[("ps", 6)], writes=[("xT", m)])

    def emit_xload(tl, m):
        if m in tl["loaded"]:
            return
        tl["loaded"].add(m)
        r0 = tl["row0"] + m * 128
        S.dma("sp", lambda e: e.dma_start(out=xres[:, m, :], in_=xin[r0:r0 + 128, :]), ("xload", m), writes=[("xres", m)])

    def emit_feat(tl, m):
        if m in tl["feat"]:
            return
        emit_xload(tl, m)
        tl["feat"].add(m)
        to_featmajor(tl["NB"], m, ("xres", m))

    def ffn_stage(which, NB, row0, final, nxt=None):
        N = NB * 128
        load_ln(0 if which == "a" else 2)
        xkeys = [("xT", m) for m in range(NB)]
        for p in range(NPIECE_FF):
            slot = next_piece()
            for c in range(2):
                j = 2 * p + c
                bg, bu = 2 * (j % 2), 2 * (j % 2) + 1
                for kc in range(8):
                    S.op("pe", lambda e, kc=kc, c=c, slot=slot, bg=bg: e.matmul(
                        ps[bg][:, 0:N], lhsT=wst[slot][:, kc * 512 + c * 128:kc * 512 + (c + 1) * 128],
                        rhs=xT[:, kc, 0:N], start=(kc == 0), stop=(kc == 7)),
                        reads=[("wst", slot)] + xkeys, writes=[("ps", bg)])
                for kc in range(8):
                    S.op("pe", lambda e, kc=kc, c=c, slot=slot, bu=bu: e.matmul(
                        ps[bu][:, 0:N], lhsT=wst[slot][:, kc * 512 + 256 + c * 128:kc * 512 + 256 + (c + 1) * 128],
                        rhs=xT[:, kc, 0:N], start=(kc == 0), stop=(kc == 7)),
                        reads=[("wst", slot)] + xkeys, writes=[("ps", bu)])
                si = j % 2
                S.op("act", lambda e, si=si, bg=bg: e.activation(out=sil[si][:, 0:N], in_=ps[bg][:, 0:N], func=AF.Silu),
                     reads=[("ps", bg)], writes=[("sil", si)])
                S.op("dve", lambda e, si=si, bu=bu, j=j: e.tensor_tensor(out=hT[:, j, 0:N], in0=sil[si][:, 0:N], in1=ps[bu][:, 0:N], op=ALU.mult),
                     reads=[("sil", si), ("ps", bu)], writes=[("hT", j)])
        pairs = ((4, 5), (0, 1), (2, 3))
        for m in range(NB):
            pr = pairs[m % 3]
            for h in range(2):
                for j in range(NJ):
                    S.op("pe", lambda e, j=j, h=h, m=m, pr=pr: e.matmul(
                        ps[pr[h]][:], lhsT=hT[:, j, m * 128:(m + 1) * 128],
                        rhs=Wbig[:, j * 1024 + h * 512:j * 1024 + (h + 1) * 512], start=(j == 0), stop=(j == NJ - 1)),
                        reads=[("hT", j)] + wbig_keys(j), writes=[("ps", pr[h])])
            if final:
                if nxt is not None and m < nxt["NB"]:
                    emit_feat_stage(nxt, m)
            elif m >= 2:
                to_featmajor(NB, m - 2, ("xres", m - 2))
            ln_epilogue(m, pr, 0.5)
            if final:
                S.dma("sp", lambda e, m=m: e.dma_start(out=y[row0 + m * 128:row0 + (m + 1) * 128, :], in_=xres[:, m, :]),
                      ("yout", m), reads=[("xres", m)])
                if nxt is not None and m < nxt["NB"]:
                    emit_xload(nxt, m)
        if final:
            if nxt is not None:
                for m in range(nxt["NB"]):
                    emit_feat_stage(nxt, m)
        else:
            for m in range(max(0, NB - 2), NB):
                to_featmajor(NB, m, ("xres", m))

    QT = lambda: hT[:, 0:4, :]

    def attn_finish(m, pso, split=False):
        for g in range(2):
            pv = ps[pso[g]][:, 0:260].rearrange("p (j d) -> p j d", j=4)
            S.op("dve", lambda e, g=g, pv=pv: e.tensor_tensor(out=den[:, 4 * g:4 * g + 4], in0=pv[:, :, 64], in1=esink[:, 4 * g:4 * g + 4], op=ALU.add),
                 reads=[("ps", pso[g]), "esink"], writes=[("den", g)])
        S.op("dve", lambda e: e.reciprocal(out=den[:, 8:16], in_=den[:, 0:8]), reads=[("den", 0), ("den", 1)], writes=["rden"])
        for g in range(2):
            for j in range(4):
                hh = 4 * g + j
                S.op("dve", lambda e, g=g, j=j, hh=hh: e.tensor_scalar(
                    out=oa[:, hh * 64:(hh + 1) * 64], in0=ps[pso[g]][:, j * 65:j * 65 + 64], scalar1=den[:, 8 + hh:9 + hh], scalar2=None, op0=ALU.mult),
                    reads=[("ps", pso[g]), "rden"], writes=[("oa", hh)])
        oak = [("oa", hh) for hh in range(8)]
        S.op("act", lambda e: e.activation(out=sil[0][:, 0:512], in_=oa[:], func=AF.Square, accum_out=sm[:, 2:3]),
             reads=oak, writes=[("sil", 0), "sm2"])
        S.op("act", lambda e: e.activation(out=sm[:, 3:4], in_=sm[:, 2:3], func=AF.Ln, bias=eps_rms, scale=1.0 / 512.0),
             reads=["sm2", "cst"], writes=["sm3"])
        S.op("act", lambda e: e.activation(out=sm[:, 4:5], in_=sm[:, 3:4], func=AF.Exp, scale=-0.5), reads=["sm3"], writes=["sm4"])
        S.op("dve", lambda e: e.scalar_tensor_tensor(out=oan[:], in0=oa[:], scalar=sm[:, 4:5], in1=angb[:], op0=ALU.mult, op1=ALU.mult),
             reads=oak + ["sm4", "angb"], writes=["oan"])
        if not split:
            attn_finish_c(m)

    def attn_finish_c(m):
        for c in range(4):
            S.op("pe", lambda e, c=c: e.transpose(psTb[:, c * 128:(c + 1) * 128], oan[:, c * 128:(c + 1) * 128], identb[:]),
                 reads=["oan", "identb"], writes=[("ps", 6)])
        S.op("act", lambda e: e.copy(out=mixT[:, 0:4, m * 128:(m + 1) * 128], in_=psTb[:, 0:512].rearrange("p (k t) -> p k t", k=4)),
             reads=[("ps", 6)], writes=[("mixT", "a", m)])

    def hgrn_out(h, m, hi_):
        i = hi_ % 2
        src = ps[7][:, 128 * (h % 4):128 * (h % 4) + 128] if False else None
        return i

    def mixer_stage(kind, t, NB, row0, final):
        N = NB * 128
        is_s = (kind == "s")
        pos = 0 if is_s else (t % 4)
        seq = 0 if is_s else (t // 4)
        load_ln(1)
        xkeys = [("xT", m) for m in range(NB)]
        n0 = state["npiece"]
        state["limit"] = piece_pos[n0 + NPIECE_IN - 1] + 1 if True else 0
        state["limit"] = (piece_pos[n0 + NPIECE_IN] if (is_s and n0 + NPIECE_IN < len(piece_pos)) else len(plan))
        T1 = wstf[(n0 + 3) % 3]
        T2 = wstf[(n0 + 4) % 3]
        T3 = wstf[(n0 + 5) % 3]
        k1, k2, k3 = ("wst", (n0 + 3) % 3), ("wst", (n0 + 4) % 3), ("wst", (n0 + 5) % 3)
        want_cache = (is_s or pos == 3) and "nokvf" not in SKIP
        hTf = hT.bitcast(F32)
        mixTf = mixT.bitcast(F32)
        T1h = [hTf[:, 17:19, :].rearrange("p a t -> p (a t)"), hTf[:, 19:21, :].rearrange("p a t -> p (a t)"),
               mixTf[:, 4:6, :].rearrange("p a t -> p (a t)"), mixTf[:, 6:8, :].rearrange("p a t -> p (a t)")]
        MIXH = [("mixT", "h", mm) for mm in range(4)] + [("mixT", "hh", hh_, mm) for hh_ in range(3) for mm in range(4)]
        T1K = [[("T1h", 0)], [("T1h", 1)], [("T1h", 2)] + MIXH, [("T1h", 3)] + MIXH]
        per = 8 if is_s else 64
        mask = cst[:, C_SCAN + 128:C_SCAN + 256] if is_s else msk512[:, 0:N]
        mkeys = ["cst"] if is_s else [("msk512", r) for r in range(4)]

        def b2_chain(h):
            t1 = T1h[h]
            if h % 2 == 0:
                t2, t3, k2_, k3_ = rbuf[:, 0:N], rbuf[:, 512:512 + N], ("rbuf", 0), ("rbuf", 1)
            else:
                t2, t3, k2_, k3_ = sil[1][:, 0:N], xb.bitcast(F32)[:, 0:N], ("sil", 1), "xb"
            S.op("act", lambda e: e.activation(out=t2, in_=t1[:, 0:N], func=AF.Ln, bias=lbv[:, h:h + 1], scale=lbv[:, 4 + h:5 + h]),
                 reads=T1K[h] + ["lbv"], writes=[k2_])
            S.op("dve", lambda e: e.tensor_scalar(out=t1[:, 0:N], in0=t1[:, 0:N], scalar1=lbv[:, 8 + h:9 + h], scalar2=lbv[:, 4 + h:5 + h],
                                                  op0=ALU.mult, op1=ALU.add),
                 reads=T1K[h] + ["lbv", k2_], writes=T1K[h])
            S.op("dve", lambda e: e.tensor_tensor_scan(out=t3, data0=mask, data1=t2, initial=0.0, op0=ALU.mult, op1=ALU.add),
                 reads=[k2_] + mkeys, writes=[k3_])
            S.op("act", lambda e: e.activation(out=t2, in_=t3, func=AF.Exp), reads=[k3_], writes=[k2_])
            S.op("act", lambda e: e.activation(out=t3, in_=t3, func=AF.Exp, scale=-1.0), reads=[k3_], writes=[k3_])
            S.op("dve", lambda e: e.tensor_tensor(out=hT[:, 8 + h, 0:N], in0=hq_s[:, h, 0:N], in1=t2, op=ALU.mult),
                 reads=[("hq_s", h), k2_], writes=[("hT", 8 + h)])
            S.op("dve", lambda e: e.tensor_tensor(out=hT[:, 12 + h, 0:N], in0=t1[:, 0:N], in1=t3, op=ALU.mult),
                 reads=T1K[h] + [k3_], writes=[("hT", 12 + h)])
            nl = N // per
            S.op("act", lambda e: e.copy(out=EBL[:, h, 0:nl], in_=t2.rearrange("p (c s) -> p c s", s=per)[:, :, per - 1]),
                 reads=[k2_], writes=[("EBL", h)])

        pbanks = (0, 1, 7)
        nchunk = 0
        pend_a2 = []
        for p in range(NPIECE_IN):
            slot = next_piece()
            nch = 2 if p == 1 else 4
            for c in range(nch):
                b = pbanks[nchunk % 3]
                nchunk += 1
                for kc in range(8):
                    S.op("pe", lambda e, kc=kc, c=c, slot=slot, b=b: e.matmul(
                        ps[b][:, 0:N], lhsT=wst[slot][:, kc * 512 + c * 128:kc * 512 + (c + 1) * 128],
                        rhs=xT[:, kc, 0:N], start=(kc == 0), stop=(kc == 7)),
                        reads=[("wst", slot)] + xkeys, writes=[("ps", b)])
                pk = [("ps", b)]
                if c == 1 and pend_a2:
                    prompt_attn_a(pend_a2.pop(), pos, seq, gsel=(1,))
                if p == 0:
                    S.op("act", lambda e, c=c, b=b: e.copy(out=hT[:, c, 0:N], in_=ps[b][:, 0:N]), reads=pk, writes=[("hT", c)])
                elif p == 1 and c == 0:
                    S.op("act", lambda e, b=b: e.copy(out=KT[:, 1:1 + NB, :], in_=ps[b][:, 0:N].rearrange("p (m t) -> p m t", m=NB)),
                         reads=pk, writes=[("KT", 1 + m) for m in range(NB)])
                    if want_cache:
                        S.op("act", lambda e, b=b: e.copy(out=kvf[:, 0, :], in_=ps[b][:, N - 128:N]), reads=pk, writes=[("kvf", 0)])
                elif p == 1:
                    S.op("act", lambda e, b=b: e.copy(out=hT[:, 16, 0:N], in_=ps[b][:, 0:N]), reads=pk, writes=[("hT", 16)])
                    if want_cache:
                        S.op("act", lambda e, b=b: e.copy(out=kvf[:, 1, :], in_=ps[b][:, N - 128:N]), reads=pk, writes=[("kvf", 1)])
                elif p == 2:
                    S.op("act", lambda e, c=c, b=b: e.activation(out=hq_s[:, c, 0:N], in_=ps[b][:, 0:N], func=AF.Silu), reads=pk, writes=[("hq_s", c)])
                elif p == 3:
                    S.op("act", lambda e, c=c, b=b: e.activation(out=T1h[c][:, 0:N], in_=ps[b][:, 0:N], func=AF.Sigmoid), reads=pk, writes=T1K[c])
                elif p == 4:
                    S.op("dve", lambda e, c=c, b=b: e.tensor_copy(out=hT[:, 4 + c, 0:N], in_=ps[b][:, 0:N]), reads=pk, writes=[("hT", 4 + c)])
                    b2_chain(c)
                else:
                    S.op("act", lambda e, c=c, b=b: e.activation(out=G[:, c, 0:N], in_=ps[b][:, 0:N], func=AF.Silu), reads=pk, writes=[("G", c)])
            if not is_s:
                if p >= 3:
                    prompt_attn_c(p - 3, pos, seq)
                if p >= 2:
                    prompt_attn_b(p - 2, pos, seq)
                if 1 <= p <= 4:
                    prompt_attn_a(p - 1, pos, seq, gsel=(0,))
                    pend_a2.append(p - 1)
        if not is_s:
            if pend_a2:
                prompt_attn_a(pend_a2.pop(), pos, seq, gsel=(1,))
            prompt_hgrn1(0, pos, seq)
            prompt_attn_c(3, pos, seq)
        if is_s:
            sample_core(k1, k2, k3, T1, T2, T3)
            S.op("dve", lambda e: e.memset(sm[:, 7:8], 0.0), writes=[("xres", 1), ("xres", 2), ("xres", 3)] + SAMPLE_KEYS
                 + [("Ef", 0), ("Ef", 1), ("Ktok", 0), ("Ktok", 1), ("Vtok", 0), ("Vtok", 1), ("Am", 0), ("Am", 1),
                    ("Ktok4", 0), ("Vtok4", 0), ("Am4", 0), ("Am4", 1)])
        state["limit"] = len(plan)
        WB = (0, 1, 7)

        def wout_mm(m):
            wb = (WB[(2 * m) % 3], WB[(2 * m + 1) % 3])
            for hf_ in range(2):
                for c in range(8):
                    S.op("pe", lambda e, c=c, hf_=hf_, m=m, wb=wb: e.matmul(
                        ps[wb[hf_]][:], lhsT=mixT[:, c, m * 128:(m + 1) * 128],
                        rhs=Wout[:, c * 1024 + hf_ * 512:c * 1024 + (hf_ + 1) * 512], start=(c == 0), stop=(c == 7)),
                        reads=[("mixT", "a", m), ("mixT", "h", m), ("Wout", c // 4)], writes=[("ps", wb[hf_])])

        def wout_ep(m):
            wb = (WB[(2 * m) % 3], WB[(2 * m + 1) % 3])
            if (not final) and m >= 2:
                to_featmajor(NB, m - 2, ("xres", m - 2))
            ln_epilogue(m, wb, 1.0)
            if final:
                S.dma("sp", lambda e, m=m: e.dma_start(out=y[row0 + m * 128:row0 + (m + 1) * 128, :], in_=xres[:, m, :]),
                      ("yout", m), reads=[("xres", m)])

        for m in range(NB):
            if m >= 1:
                wout_mm(m - 1)
            if not is_s:
                prompt_hgrn2(m, pos, seq)
                prompt_hgrn3(m, pos, seq)
                if m + 1 < NB:
                    prompt_hgrn1(m + 1, pos, seq)
            if m >= 1:
                wout_ep(m - 1)
        wout_mm(NB - 1)
        wout_ep(NB - 1)
        if not final:
            for m in range(max(0, NB - 2), NB):
                to_featmajor(NB, m, ("xres", m))
        for m in range(0):
            pass
        if (not is_s) and pos < 3:
            S.op("act", lambda e: e.copy(out=KT[:, 0, :], in_=KT[:, 4, :]), reads=[("KT", 4)], writes=[("KT", 0)])
            S.op("act", lambda e: e.copy(out=Vaug[:, 0, :, 0:64], in_=Vaug[:, 4, :, 0:64]), reads=[("Vaug", 4)], writes=[("Vaug", 0)])

    def sample_core(k1, k2, k3, T1, T2, T3):
        n0s = state["npiece"] - NPIECE_IN
        slot2 = (n0s + 4) % 3
        CVaug = wst[slot2][:, 0:2080].rearrange("p (b g d) -> p b g d", b=16, g=2)
        seqm = cst[:, C_SEQM:C_SEQM + 16]
        S.op("dve", lambda e: e.memset(sm[:, 7:8], 0.0), writes=[("xres", 1), ("xres", 2), ("xres", 3)] + SAMPLE_KEYS)
        if "cload" not in SKIP:
            S.dma("sp", lambda e: e.dma_start(out=T1[:, 0:2048].rearrange("p (b f) -> p b f", b=16), in_=ck_in.rearrange("b w f -> w b f")),
                  "ckload", writes=[k1])
            S.dma("sp", lambda e: e.dma_start(out=T3[:, 0:2048].rearrange("p (b f) -> p b f", b=16), in_=cv_in.rearrange("b w f -> w b f")),
                  "cvload", writes=[k3])
        if "dramcopy" not in SKIP:
            S.dma("sp", lambda e: e.dma_start(out=ck_s[:, 0:120, :], in_=ck_in[:, 8:128, :]), "ckcopy")
            S.dma("sp", lambda e: e.dma_start(out=cv_s[:, 0:120, :], in_=cv_in[:, 8:128, :]), "cvcopy")
        if "cvms" not in SKIP:
            S.op("dve", lambda e: e.memset(wst[slot2][:, 0:2080], 1.0), writes=[k2])
        for g in range(2):
            if "cvaug" in SKIP:
                continue
            S.op("dve", lambda e, g=g: e.tensor_copy(out=CVaug[:, :, g, 0:64],
                                                  in_=T3[:, 0:2048].rearrange("p (b f) -> p b f", b=16)[:, :, g * 64:(g + 1) * 64]),
                 reads=[k3], writes=[k2])
        if "vt" not in SKIP:
            v_to_tok(0, 1)
        for w_, dst in ((0, ck_s), (1, cv_s)):
            if "kvtr" in SKIP:
                continue
            S.op("pe", lambda e, w_=w_: e.transpose(ps[3][:, w_ * 128:(w_ + 1) * 128], kvf[:, w_, :], idf), reads=[("kvf", w_), "cst"], writes=[("ps", 3)])
            S.op("act", lambda e, w_=w_: e.copy(out=kvt[:, w_, :], in_=ps[3][:, w_ * 128:(w_ + 1) * 128]), reads=[("ps", 3)], writes=[("kvt", w_)])
            for b in range(16):
                if "kvrows" in SKIP:
                    continue
                S.dma("sp", lambda e, w_=w_, dst=dst, b=b: e.dma_start(out=dst[b, 120:128, :], in_=kvt[8 * b:8 * b + 8, w_, :]),
                      ("kvt", w_), reads=[("kvt", w_)])
        def attn_seq(b):
            i = b % 2
            if "s_tr" not in SKIP:
                S.op("pe", lambda e, b=b: e.transpose(ps[6][:, 0:128], T1[:, b * 128:(b + 1) * 128], idf), reads=[k1, "cst"], writes=[("ps", 6)])
                S.op("act", lambda e, i=i: e.copy(out=KTs[i], in_=ps[6][:, 0:128]), reads=[("ps", 6)], writes=[("KTs", i)])
            for g in range(2):
                bk = 2 if g == 0 else 0
                q_ = hT[64 * g:64 * g + 64, 0:4, b * 8:(b + 1) * 8]
                S.op("pe", lambda e, g=g, i=i, q_=q_, bk=bk: e.matmul(ps[bk][:, 0:32], lhsT=KTs[i][64 * g:64 * g + 64, :], rhs=q_,
                                                                  start=True, stop=True),
                     reads=[("KTs", i)] + [("hT", c) for c in range(4)], writes=[("ps", bk)])
                S.op("pe", lambda e, g=g, q_=q_, bk=bk: e.matmul(ps[bk][:, 32:64], lhsT=KT[64 * g:64 * g + 64, 1, :], rhs=q_,
                                                             start=True, stop=True),
                     reads=[("KT", 1)] + [("hT", c) for c in range(4)], writes=[("ps", bk)])
                S.op("act", lambda e, i=i, g=g, bk=bk: e.activation(out=Ef[i][:, g * 64:(g + 1) * 64], in_=ps[bk][:, 0:64], func=AF.Exp, scale=SCALE),
                     reads=[("ps", bk)], writes=[("Ef", i)])
            efk = [("Ef", i)]
            efv = Ef[i][:, 0:128].rearrange("p (g c) -> p g c", g=2)
            S.op("dve", lambda e, i=i, efv=efv: e.tensor_tensor(out=Pc[i].rearrange("p (g c) -> p g c", g=2), in0=efv[:, :, 0:32],
                                                           in1=cst[:, C_DSC:C_DSC + 64].rearrange("p (g c) -> p g c", g=2), op=ALU.mult),
                 reads=efk + ["cst"], writes=[("Pc", i)])
            S.op("dve", lambda e, i=i, b=b, efv=efv: e.scalar_tensor_tensor(out=Pn[i].rearrange("p (g c) -> p g c", g=2), in0=efv[:, :, 32:64],
                                                                       scalar=seqm[:, b:b + 1],
                                                                       in1=cst[:, C_DSN:C_DSN + 64].rearrange("p (g c) -> p g c", g=2),
                                                                       op0=ALU.mult, op1=ALU.mult),
                 reads=efk + ["cst"], writes=[("Pn", i)])
            for g in range(2):
                if "s_pv" in SKIP:
                    continue
                o_ = ps[4 + g][0:65, b * 32:(b + 1) * 32]
                S.op("pe", lambda e, g=g, i=i, b=b, o_=o_: e.matmul(o_, lhsT=CVaug[:, b, g, :], rhs=Pc[i][:, g * 32:(g + 1) * 32], start=True, stop=False),
                     reads=[k2, ("Pc", i)], writes=[("ps", 4 + g)])
                S.op("pe", lambda e, g=g, i=i, o_=o_: e.matmul(o_, lhsT=Vaug[:, 1, g, :], rhs=Pn[i][:, g * 32:(g + 1) * 32], start=False, stop=True),
                     reads=[("Vaug", 1), "Vaug_all", ("Pn", i)], writes=[("ps", 4 + g)])
        def s0_load(n):
            h_, b_ = divmod(n, 16)
            r = n % 8
            S.dma("sp", lambda e, h_=h_, b_=b_, r=r: e.dma_start(out=S0f[r // 4][:, r % 4, :], in_=st_in[b_, h_]), ("S0f", r), writes=[("S0fb", r)])
        for n in range(4):
            if "shgrn" not in SKIP:
                s0_load(n)
        for h in range(4):
            if "hg" in SKIP:
                continue
            i = hgrn_head_common(h, 0, C_AM + 128)
            for b in range(16):
                if b % 4 == 0:
                    attn_seq(4 * h + b // 4)
                n = h * 16 + b
                if n + 4 < 64:
                    s0_load(n + 4)
                r = n % 8
                s0f = S0f[r // 4][:, r % 4, :]
                s0b = S0b[r // 4][:, r % 4, :]
                snw = Snew[r // 4][:, r % 4, :]
                vm = Vm[n % 2]
                cs = slice(b * 8, (b + 1) * 8)
                ebl = EBL[:, h, b:b + 1]
                S.op("act", lambda e, s0f=s0f, s0b=s0b: e.copy(out=s0b, in_=s0f), reads=[("S0fb", r)], writes=[("S0bb", r)])
                S.op("pe", lambda e, i=i, cs=cs: e.matmul(ps[7][:, 128 + cs.start:128 + cs.stop], lhsT=Vtok[i], rhs=Am[i][:, cs], start=True, stop=False),
                     reads=[("Vtok", i), ("Am", i)], writes=[("ps", 7)])
                S.op("pe", lambda e, h=h, cs=cs, s0b=s0b: e.matmul(ps[7][:, 128 + cs.start:128 + cs.stop], lhsT=s0b, rhs=hT[:, 8 + h, cs], start=False, stop=True),
                     reads=[("S0bb", r), ("hT", 8 + h)], writes=[("ps", 7)])
                S.op("dve", lambda e, i=i, b=b, vm=vm: e.tensor_scalar(out=vm, in0=Vtok[i], scalar1=seqm[:, b:b + 1], scalar2=None, op0=ALU.mult),
                     reads=[("Vtok", i), "cst"], writes=[("Vm", n % 2)])
                bu = 3 if n % 2 == 0 else 1
                S.op("pe", lambda e, i=i, vm=vm, bu=bu: e.matmul(ps[bu][:, 0:128], lhsT=Ktok[i], rhs=vm, start=True, stop=True),
                     reads=[("Ktok", i), ("Vm", n % 2)], writes=[("ps", bu)])
                S.op("dve", lambda e, s0f=s0f, ebl=ebl: e.tensor_scalar(out=s0f, in0=s0f, scalar1=ebl, scalar2=None, op0=ALU.mult),
                     reads=[("S0fb", r), ("S0bb", r), ("EBL", h)], writes=[("S0fb", r)])
                S.op("dve", lambda e, s0f=s0f, snw=snw, ebl=ebl, bu=bu: e.scalar_tensor_tensor(out=snw, in0=ps[bu][:, 0:128], scalar=ebl, in1=s0f,
                                                                                           op0=ALU.mult, op1=ALU.add),
                     reads=[("ps", bu), ("S0fb", r), ("EBL", h)], writes=[("Snewb", r)])
                S.dma("sp", lambda e, snw=snw, b=b, h=h: e.dma_start(out=st_s[b, h], in_=snw), ("Snew", r), reads=[("Snewb", r)])
            hgrn_finish(h, 0, i)
        for hh in range(8):
            if "ostr" in SKIP:
                continue
            g, j = divmod(hh, 4)
            i = hh % 2
            S.op("act", lambda e, g=g, j=j, i=i: e.copy(out=Ef[i][0:65, 256:384].rearrange("p (b t) -> p b t", b=16), in_=ps[4 + g][0:65, :].rearrange("p (b j t) -> p b j t", b=16, j=4)[:, :, j, :]),
                 reads=[("ps", 4 + g)], writes=[("Ef", i)])
            S.op("pe", lambda e, g=g, j=j, i=i: e.transpose(ps[2 + g][:, j * 65:(j + 1) * 65], Ef[i][0:65, 256:384], cst[0:65, C_ID:C_ID + 65]),
                 reads=[("Ef", i), "cst"], writes=[("ps", 2 + g)])
        if "af" not in SKIP:
            attn_finish(0, (2, 3))

    def v_to_tok(m, slot):
        S.op("pe", lambda e: e.transpose(psTb[:, 512:640], hT[:, 16, m * 128:(m + 1) * 128], identb[:]),
             reads=[("hT", 16), "identb"], writes=[("ps", 6)])
        S.op("act", lambda e: e.copy(out=Vaug[:, slot, :, 0:64], in_=psTb[:, 512:640].rearrange("p (g d) -> p g d", g=2)),
             reads=[("ps", 6), "Vaug_all"], writes=[("Vaug", slot)])

    def hgrn_head_common(h, m, am_off):
        i = h % 2
        S.op("pe", lambda e: e.transpose(psTb[:, 640:768], hT[:, 12 + h, m * 128:(m + 1) * 128], identb[:]),
             reads=[("hT", 12 + h), "identb"], writes=[("ps", 6)])
        S.op("act", lambda e: e.copy(out=Ktok[i], in_=psTb[:, 640:768]), reads=[("ps", 6)], writes=[("Ktok", i)])
        S.op("pe", lambda e: e.transpose(psTb[:, 768:896], hT[:, 4 + h, m * 128:(m + 1) * 128], identb[:]),
             reads=[("hT", 4 + h), "identb"], writes=[("ps", 6)])
        S.op("act", lambda e: e.copy(out=Vtok[i], in_=psTb[:, 768:896]), reads=[("ps", 6)], writes=[("Vtok", i)])
        S.op("pe", lambda e: e.matmul(ps[7][:, 0:128], lhsT=hT[:, 12 + h, m * 128:(m + 1) * 128], rhs=hT[:, 8 + h, m * 128:(m + 1) * 128],
                                      start=True, stop=True),
             reads=[("hT", 12 + h), ("hT", 8 + h)], writes=[("ps", 7)])
        S.op("dve", lambda e: e.tensor_tensor(out=Am[i], in0=ps[7][:, 0:128], in1=cst[:, am_off:am_off + 128], op=ALU.mult),
             reads=[("ps", 7), "cst"], writes=[("Am", i)])
        return i

    def hgrn_finish(h, m, i):
        S.op("act", lambda e: e.copy(out=ohf[i][:], in_=ps[7][:, 128:256]), reads=[("ps", 7)], writes=[("ohf", i)])
        S.op("act", lambda e: e.activation(out=osq[i][:], in_=ps[7][:, 128:256], func=AF.Square), reads=[("ps", 7)], writes=[("osq", i)])
        S.op("pe", lambda e: e.matmul(ps[7][:, 384:512], lhsT=onesf, rhs=osq[i][:], start=True, stop=True),
             reads=["cst", ("osq", i)], writes=[("ps", 7)])
        S.op("act", lambda e: e.activation(out=ors[i][:], in_=ps[7][:, 384:512], func=AF.Ln, bias=eps_rms, scale=1.0 / 128.0),
             reads=[("ps", 7), "cst"], writes=[("ors", i)])
        S.op("act", lambda e: e.activation(out=ors[i][:], in_=ors[i][:], func=AF.Exp, scale=-0.5), reads=[("ors", i)], writes=[("ors", i)])
        S.op("dve", lambda e: e.scalar_tensor_tensor(out=ohf[i][:], in0=ohf[i][:], scalar=hngt[:, h:h + 1], in1=ors[i][:], op0=ALU.mult, op1=ALU.mult),
             reads=[("ohf", i), ("ors", i), "hngt"], writes=[("ohf", i)])
        S.op("dve", lambda e: e.tensor_tensor(out=mixT[:, 4 + h, m * 128:(m + 1) * 128], in0=ohf[i][:], in1=G[:, h, m * 128:(m + 1) * 128], op=ALU.mult),
             reads=[("ohf", i), ("G", h)], writes=[("mixT", "h", m)] if h == 3 else [("mixT", "hh", h, m)])

    def prompt_attn_a(m, pos, seq, gsel=(0, 1)):
        gb = pos * 4 + m
        if 0 in gsel:
            v_to_tok(m, 1 + m)
        for g in gsel:
            for kh in range(2):
                if kh == 0 and gb == 0:
                    continue
                bk = 2 + kh
                S.op("pe", lambda e, g=g, kh=kh, bk=bk: e.matmul(
                    ps[bk][:], lhsT=KT[64 * g:64 * g + 64, m + kh, :], rhs=hT[64 * g:64 * g + 64, 0:4, m * 128:(m + 1) * 128],
                    start=True, stop=True),
                    reads=[("KT", m + kh)] + [("hT", c) for c in range(4)], writes=[("ps", bk)])
                S.op("act", lambda e, kh=kh, bk=bk: e.activation(out=Ef[kh][:], in_=ps[bk][:], func=AF.Exp, scale=SCALE),
                     reads=[("ps", bk)], writes=[("Ef", kh)])
                o = C_DMP + (kh * 2 + g) * 512
                S.op("dve", lambda e, kh=kh, g=g, o=o: e.tensor_tensor(out=Pb[:, kh * 2 + g, :], in0=Ef[kh][:], in1=cst[:, o:o + 512], op=ALU.mult),
                     reads=[("Ef", kh), "cst"], writes=[("Pb", kh * 2 + g)])

    def prompt_attn_b(m, pos, seq):
        gb = pos * 4 + m
        for g in range(2):
            for j in range(4):
                khs = [1] if gb == 0 else [0, 1]
                for n_, kh in enumerate(khs):
                    S.op("pe", lambda e, g=g, j=j, kh=kh, n_=n_, khs=khs: e.matmul(
                        ps[4 + g][:, j * 65:(j + 1) * 65], lhsT=Pb[:, kh * 2 + g, j * 128:(j + 1) * 128], rhs=Vaug[:, m + kh, g, :],
                        start=(n_ == 0), stop=(n_ == len(khs) - 1)),
                        reads=[("Pb", kh * 2 + g), ("Vaug", m + kh), "Vaug_all"], writes=[("ps", 4 + g)])
        attn_finish(m, (4, 5), split=True)

    def prompt_attn_c(m, pos, seq):
        gb = pos * 4 + m
        attn_finish_c(m)
        if pos == 3 and m == 3:
            for w_, dst in ((0, ck_p), (1, cv_p)):
                S.op("pe", lambda e, w_=w_: e.transpose(ps[3][:, w_ * 128:(w_ + 1) * 128], kvf[:, w_, :], idf), reads=[("kvf", w_), "cst"], writes=[("ps", 3)])
                S.op("act", lambda e, w_=w_: e.copy(out=kvt[:, w_, :], in_=ps[3][:, w_ * 128:(w_ + 1) * 128]), reads=[("ps", 3)], writes=[("kvt", w_)])
                S.dma("sp", lambda e, w_=w_, dst=dst: e.dma_start(out=dst[seq], in_=kvt[:, w_, :]), ("kvt", w_), reads=[("kvt", w_)])
    def OB(h):
        return 2 + h // 2

    def oc(h):
        return (h % 2) * 128

    def ac(h):
        return 256 + (h % 2) * 128


    def prompt_hgrn1(m, pos, seq):
        gb = pos * 4 + m
        par = m % 2
        blk = slice(m * 128, (m + 1) * 128)
        for h in range(4):
            S.op("pe", lambda e, h=h: e.transpose(psTb[:, h * 128:(h + 1) * 128], hT[:, 12 + h, blk], identb[:]),
                 reads=[("hT", 12 + h), "identb"], writes=[("ps", 6)])
        for h in range(4):
            S.op("pe", lambda e, h=h: e.transpose(psTb[:, 512 + h * 128:512 + (h + 1) * 128], hT[:, 4 + h, blk], identb[:]),
                 reads=[("hT", 4 + h), "identb"], writes=[("ps", 6)])
        S.op("act", lambda e: e.copy(out=Ktok4[par][:].rearrange("p h t -> p (h t)"), in_=psTb[:, 0:512]), reads=[("ps", 6)], writes=[("Ktok4", par)])
        S.op("act", lambda e: e.copy(out=Vtok4[par][:].rearrange("p h t -> p (h t)"), in_=psTb[:, 512:1024]), reads=[("ps", 6)], writes=[("Vtok4", par)])
        for h in range(4):
            B = OB(h)
            S.op("pe", lambda e, h=h, B=B: e.matmul(ps[B][:, ac(h):ac(h) + 128], lhsT=hT[:, 12 + h, blk], rhs=hT[:, 8 + h, blk], start=True, stop=True),
                 reads=[("hT", 12 + h), ("hT", 8 + h)], writes=[("ps", B)])
            S.op("dve", lambda e, h=h, B=B: e.tensor_tensor(out=Am4[:, h, :], in0=ps[B][:, ac(h):ac(h) + 128], in1=cst[:, C_AM:C_AM + 128], op=ALU.mult),
                 reads=[("ps", B), "cst"], writes=[("Am4", h)])

    def prompt_hgrn2(m, pos, seq):
        gb = pos * 4 + m
        par = m % 2
        blk = slice(m * 128, (m + 1) * 128)
        for c in range(2):
            first = (gb == 0 and c == 0)
            cs = slice(m * 128 + c * 64, m * 128 + (c + 1) * 64)
            if not first:
                for h in range(4):
                    ebl0 = EBL[:, h, 2 * m + c:2 * m + c + 1]
                    S.op("dve", lambda e, h=h, ebl0=ebl0: e.tensor_scalar(out=Sf[:, h, :], in0=Sf[:, h, :], scalar1=ebl0, scalar2=None, op0=ALU.mult),
                         reads=[("Sf", h), ("EBL", h)], writes=[("Sf", h)])
            for h in range(4):
                B = OB(h)
                S.op("pe", lambda e, h=h, B=B, c=c, first=first: e.matmul(ps[B][:, oc(h) + c * 64:oc(h) + (c + 1) * 64], lhsT=Vtok4[par][:, h, :],
                                                    rhs=Am4[:, h, c * 64:(c + 1) * 64], start=True, stop=first),
                     reads=[("Vtok4", par), ("Am4", h)], writes=[("ps", B)])
                if not first:
                    S.op("pe", lambda e, h=h, B=B, c=c, cs=cs: e.matmul(ps[B][:, oc(h) + c * 64:oc(h) + (c + 1) * 64], lhsT=Sb[:, h, :], rhs=hT[:, 8 + h, cs],
                                                        start=False, stop=True),
                         reads=[("Sb", h), ("hT", 8 + h)], writes=[("ps", B)])
                S.op("pe", lambda e, h=h, c=c: e.matmul(ps[4][:, h * 128:(h + 1) * 128], lhsT=Ktok4[par][64 * c:64 * c + 64, h, :],
                                                    rhs=Vtok4[par][64 * c:64 * c + 64, h, :], start=True, stop=True),
                     reads=[("Ktok4", par), ("Vtok4", par)], writes=[("ps", 4)])
            for h in range(4):
                B = 4
                ebl = EBL[:, h, 2 * m + c:2 * m + c + 1]
                if first:
                    S.op("dve", lambda e, h=h, B=B, ebl=ebl: e.tensor_scalar(out=Sb[:, h, :], in0=ps[4][:, h * 128:(h + 1) * 128], scalar1=ebl, scalar2=None, op0=ALU.mult),
                         reads=[("ps", B), ("EBL", h)], writes=[("Sb", h)])
                    S.op("dve", lambda e, h=h, B=B, ebl=ebl: e.tensor_scalar(out=Sf[:, h, :], in0=ps[4][:, h * 128:(h + 1) * 128], scalar1=ebl, scalar2=None, op0=ALU.mult),
                         reads=[("ps", B), ("EBL", h)], writes=[("Sf", h)])
                else:
                    S.op("dve", lambda e, h=h, B=B, ebl=ebl: e.scalar_tensor_tensor(out=Sb[:, h, :], in0=ps[4][:, h * 128:(h + 1) * 128], scalar=ebl, in1=Sf[:, h, :],
                                                                               op0=ALU.mult, op1=ALU.add),
                         reads=[("ps", B), ("EBL", h), ("Sf", h)], writes=[("Sb", h)])
                    S.op("dve", lambda e, h=h, B=B, ebl=ebl: e.scalar_tensor_tensor(out=Sf[:, h, :], in0=ps[4][:, h * 128:(h + 1) * 128], scalar=ebl, in1=Sf[:, h, :],
                                                                               op0=ALU.mult, op1=ALU.add),
                         reads=[("ps", B), ("EBL", h), ("Sf", h)], writes=[("Sf", h)])

    def prompt_hgrn3(m, pos, seq):
        gb = pos * 4 + m
        par = m % 2
        blk = slice(m * 128, (m + 1) * 128)
        for h in range(4):
            B = OB(h)
            i = h % 2
            S.op("act", lambda e, B=B, i=i, h=h: e.copy(out=ohf[i][:], in_=ps[B][:, oc(h):oc(h) + 128]), reads=[("ps", B)], writes=[("ohf", i)])
            S.op("act", lambda e, B=B, i=i, h=h: e.activation(out=osq[i][:], in_=ps[B][:, oc(h):oc(h) + 128], func=AF.Square), reads=[("ps", B)], writes=[("osq", i)])
            S.op("pe", lambda e, i=i, h=h: e.matmul(ps[5][:, h * 128:(h + 1) * 128], lhsT=onesf, rhs=osq[i][:], start=True, stop=True),
                 reads=["cst", ("osq", i)], writes=[("ps", 5)])
            S.op("act", lambda e, i=i, h=h: e.activation(out=ors[i][:], in_=ps[5][:, h * 128:(h + 1) * 128], func=AF.Ln, bias=eps_rms, scale=1.0 / 128.0),
                 reads=[("ps", 5), "cst"], writes=[("ors", i)])
            S.op("act", lambda e, i=i: e.activation(out=ors[i][:], in_=ors[i][:], func=AF.Exp, scale=-0.5), reads=[("ors", i)], writes=[("ors", i)])
            S.op("dve", lambda e, h=h, i=i: e.scalar_tensor_tensor(out=ohf[i][:], in0=ohf[i][:], scalar=hngt[:, h:h + 1], in1=ors[i][:], op0=ALU.mult, op1=ALU.mult),
                 reads=[("ohf", i), ("ors", i), "hngt"], writes=[("ohf", i)])
            S.op("dve", lambda e, h=h, i=i: e.tensor_tensor(out=mixT[:, 4 + h, blk], in0=ohf[i][:], in1=G[:, h, blk], op=ALU.mult),
                 reads=[("ohf", i), ("G", h)], writes=[("mixT", "h", m)] if h == 3 else [("mixT", "hh", h, m)])
        if pos == 3 and m == 3:
            S.dma("sp", lambda e: e.dma_start(out=st_p[seq].rearrange("h k v -> k h v"), in_=Sf[:]), "stp",
                  reads=[("Sf", h) for h in range(4)])

    for ti, tl in enumerate(tinfo):
        kind, t, row0, NB = tl["kind"], tl["t"], tl["row0"], tl["NB"]
        N = NB * 128
        nxt = tinfo[ti + 1] if ti + 1 < len(tinfo) else None
        if ti == 0:
            ensure_xcast(tl)
        for m in range(NB):
            emit_xload(tl, m)
        for m in range(NB):
            emit_feat_stage(tl, m)
        if "A" in stages:
            ffn_stage("a", NB, row0, final=(stages == "A"), nxt=nxt)
        if "B" in stages:
            mixer_stage(kind, t, NB, row0, final=(stages[-1] == "B"))
        if "C" in stages:
            ffn_stage("c", NB, row0, final=True, nxt=nxt)

    flush_wb(0)
    S.emit(st)
    st.close()
    return nc


def _w13_layout(w13):
    g = w13[:, :DFF].reshape(8, 128, NPIECE_FF, 256)
    u = w13[:, DFF:].reshape(8, 128, NPIECE_FF, 256)
    gu = np.concatenate([g, u], axis=3)
    return np.ascontiguousarray(gu.transpose(2, 1, 0, 3)).reshape(NPIECE_FF, 128, 4096)


def _w2_layout(w2):
    return np.ascontiguousarray(w2.reshape(NJ, 128, D).transpose(1, 0, 2)).reshape(128, NJ * D)


def _win_layout(w_in):
    qa = w_in[:, 0:512].reshape(D, 8, 64)
    qperm = np.concatenate([np.concatenate([qa[:, j], qa[:, j + 4]], axis=1) for j in range(4)], axis=1)
    ka = w_in[:, 512:640]
    va = w_in[:, 640:768]
    hq = w_in[:, 768:1280]
    hf = w_in[:, 1280:1792]
    hi = w_in[:, 1792:2304]
    hg = w_in[:, 2304:2816]
    pad = np.zeros((D, 256), np.float32)
    pieces = [qperm, np.concatenate([ka, va, pad], axis=1), hq, hf, hi, hg]
    out = np.stack([p.reshape(8, 128, 512).transpose(1, 0, 2).reshape(128, 4096) for p in pieces])
    return np.ascontiguousarray(out)


def prep_shared(inp):
    sh = {}
    sh["w13a"] = _w13_layout(inp["ffn1_w13"][0])
    sh["w13b"] = _w13_layout(inp["ffn2_w13"][0])
    sh["w2a"] = _w2_layout(inp["ffn1_w2"][0])
    sh["w2b"] = _w2_layout(inp["ffn2_w2"][0])
    sh["win"] = _win_layout(inp["w_in"][0])
    sh["wout"] = np.ascontiguousarray(inp["w_out"][0].reshape(8, 128, D).transpose(1, 0, 2)).reshape(128, 8 * D)
    sh["lnp"] = np.ascontiguousarray(np.stack([inp["ln1_g"][0], inp["ln1_b"][0], inp["ln2_g"][0], inp["ln2_b"][0],
                                               inp["ln3_g"][0], inp["ln3_b"][0]]))
    sh["ang"] = np.ascontiguousarray(inp["attn_norm_g"][0].reshape(1, 512))
    sh["sink"] = np.ascontiguousarray(inp["attn_sink"][0].reshape(1, 8))
    sh["hng"] = np.ascontiguousarray(inp["hgrn_norm_g"][0].reshape(4, 128).T)
    sh["lbp"] = np.ascontiguousarray(inp["lb_param"].reshape(2, 4, 128).transpose(2, 0, 1).reshape(128, 8))
    sh["cst"] = make_consts()
    return sh


def kernel(**inputs):
    inp = {k: np.asarray(v) for k, v in inputs.items()}
    n = 8
    sh = prep_shared(inp)
    xp = inp["x_prompt"]
    xs = inp["x_sample"]
    in_maps = []
    for c in range(n):
        m = dict(sh)
        m["xin"] = np.ascontiguousarray(np.concatenate(
            [xp[2 * c:2 * c + 2].reshape(4096, D), xs[16 * c:16 * c + 16].reshape(128, D)], axis=0))
        m["st_in"] = np.ascontiguousarray(inp["state_hgrn"][0, 16 * c:16 * c + 16])
        m["ck_in"] = np.ascontiguousarray(inp["cache_win_k"][0, 16 * c:16 * c + 16].reshape(16, 128, 128))
        m["cv_in"] = np.ascontiguousarray(inp["cache_win_v"][0, 16 * c:16 * c + 16].reshape(16, 128, 128))
        in_maps.append(m)
    nc = build(n_ptiles=8, sample=True, stages="ABC")
    res = run_bass_kernel_spmd(nc, in_maps, core_ids=list(range(n)))
    R = res.results
    y_p = np.concatenate([r["y"][:4096].reshape(2, 2048, D) for r in R], axis=0)
    y_s = np.concatenate([r["y"][4096:].reshape(16, 8, D) for r in R], axis=0)
    st_p = np.concatenate([r["st_p"] for r in R], axis=0)[None]
    ck_p = np.concatenate([r["ck_p"].reshape(2, 128, 2, 64) for r in R], axis=0)[None]
    cv_p = np.concatenate([r["cv_p"].reshape(2, 128, 2, 64) for r in R], axis=0)[None]
    st_s = np.concatenate([r["st_s"] for r in R], axis=0)[None]
    ck_s = np.concatenate([r["ck_s"].reshape(16, 128, 2, 64) for r in R], axis=0)[None]
    cv_s = np.concatenate([r["cv_s"].reshape(16, 128, 2, 64) for r in R], axis=0)[None]
    f = lambda a: np.ascontiguousarray(a, dtype=np.float32)
    return (f(y_p), f(y_s), f(st_p), f(ck_p), f(cv_p), f(st_s), f(ck_s), f(cv_s))
```

```python
import numpy as np
import concourse.bass as bass
import concourse.mybir as mybir
from concourse.bass_utils import run_bass_kernel_spmd

F32 = mybir.dt.float32
BF16 = mybir.dt.bfloat16
AF = mybir.ActivationFunctionType
ALU = mybir.AluOpType
AX = mybir.AxisListType

ENGS = ("pe", "act", "dve", "pool", "sp")
EPOCH = 30000


class Sched:
    def __init__(self, nc):
        self.nc = nc
        self.prog = {e: [] for e in ENGS}
        self.known = {e: {} for e in ENGS}
        self.rw = {}
        self.rr = {}
        self.dcount = {}
        self.needed = {e: set() for e in ENGS if e != "sp"}

    def _need(self, eng, tok, waits):
        kind, k, v = tok
        if kind == "E" and k == eng and eng == "pe":
            return
        if kind == "D":
            v = max(v, self.dcount.get(k, 0))
            tok = (kind, k, v)
        kk = (kind, k)
        if self.known[eng].get(kk, 0) >= v:
            return
        self.known[eng][kk] = v
        waits.append(tok)
        if kind == "E":
            self.needed[k].add(v)

    def _deps(self, eng, reads, writes):
        waits = []
        for r in reads:
            for tok in self.rw.get(r, ()):
                self._need(eng, tok, waits)
            if isinstance(r, tuple) and r[0] == "ps":
                for tok in self.rr.get(r, ()):
                    if not (tok[0] == "E" and tok[1] == eng):
                        self._need(eng, tok, waits)
        for w in writes:
            for tok in self.rw.get(w, ()):
                self._need(eng, tok, waits)
            for tok in self.rr.get(w, ()):
                self._need(eng, tok, waits)
        return waits

    def _commit(self, tok, reads, writes):
        for r in reads:
            if r in writes:
                continue
            self.rr.setdefault(r, []).append(tok)
            if len(self.rr[r]) > 24:
                self.rr[r] = self._compact(self.rr[r])
        for w in writes:
            self.rw[w] = [tok]
            self.rr[w] = []

    @staticmethod
    def _compact(toks):
        best = {}
        for t in toks:
            kk = (t[0], t[1])
            if kk not in best or best[kk][2] < t[2]:
                best[kk] = t
        return list(best.values())

    def op(self, eng, fn, reads=(), writes=()):
        reads = tuple(reads)
        writes = tuple(writes)
        waits = self._deps(eng, reads, writes)
        idx = len(self.prog[eng]) + 1
        self.prog[eng].append(dict(waits=waits, fn=fn, dma=None))
        tok = ("E", eng, idx)
        self._commit(tok, reads, writes)
        return tok

    def dma(self, queue, fn, key, reads=(), writes=()):
        reads = tuple(reads)
        writes = tuple(writes)
        waits = self._deps(queue, reads, writes)
        n = self.dcount.get(key, 0) + 1
        self.dcount[key] = n
        tok = ("D", key, n)
        self.prog[queue].append(dict(waits=waits, fn=fn, dma=tok))
        self._commit(tok, reads, writes)
        return tok

    def emit(self, stack):
        nc = self.nc
        esem = {}
        nidx = {}
        for e in self.needed:
            need = sorted(self.needed[e])
            nidx[e] = {v: i + 1 for i, v in enumerate(need)}
            nep = (len(need) + EPOCH - 1) // EPOCH + 1
            esem[e] = [stack.enter_context(nc.semaphore(f"s_{e}_{k}")) for k in range(nep)]
        dsem = {k: stack.enter_context(nc.semaphore(f"d_{i}")) for i, k in enumerate(self.dcount)}

        def semval(tok):
            kind, k, v = tok
            if kind == "D":
                return dsem[k], 16 * v
            i = nidx[k][v]
            return esem[k][(i - 1) // EPOCH], (i - 1) % EPOCH + 1

        block = stack.enter_context(nc.Block())
        final = [("D", k, n) for k, n in self.dcount.items()]

        def run(eng_name):
            def body(eng):
                for i, rec in enumerate(self.prog[eng_name]):
                    for tok in rec["waits"]:
                        s, v = semval(tok)
                        eng.wait_ge(s, v)
                    ins = rec["fn"](eng)
                    if rec["dma"] is not None:
                        s, _ = semval(rec["dma"])
                        ins.then_inc(s, 16)
                    elif eng_name != "sp" and (i + 1) in nidx[eng_name]:
                        s, _ = semval(("E", eng_name, i + 1))
                        ins.then_inc(s, 1)
                if eng_name == "sp":
                    for tok in final:
                        s, v = semval(tok)
                        eng.wait_ge(s, v)
            return body

        block.tensor(run("pe"))
        block.scalar(run("act"))
        block.vector(run("dve"))
        block.gpsimd(run("pool"))
        block.sync(run("sp"))


D = 1024
DFF = 2816
NJ = 22
ALPHA = 2.0 ** 0.25
LN_EPS = 1e-5
RMS_EPS = 1e-6
SCALE = 0.125
NPIECE_FF = 11
NPIECE_IN = 6

C_ID = 0
C_DMP = 128
C_SCAN = C_DMP + 2048
C_AM = C_SCAN + 256
C_DSC = C_AM + 256
C_DSN = C_DSC + 64
C_ONES = C_DSN + 64
C_SEQM = C_ONES + 128
C_EPS = C_SEQM + 16
NCST = C_EPS + 2


def make_consts():
    c = np.zeros((128, NCST), np.float32)
    c[:, C_ID:C_ID + 128] = np.eye(128, dtype=np.float32)
    slopes = (2.0 ** -(np.arange(1, 9, dtype=np.float64))).reshape(2, 4)
    k = np.arange(128)[:, None]
    q = np.arange(128)[None, :]
    for kh in range(2):
        for g in range(2):
            dist = (q - k) + (128 if kh == 0 else 0)
            valid = (dist >= 0) & (dist < 128)
            blk = np.zeros((128, 4, 128), np.float64)
            for j in range(4):
                blk[:, j, :] = np.where(valid, np.exp(-slopes[g, j] * dist), 0.0)
            o = C_DMP + (kh * 2 + g) * 512
            c[:, o:o + 512] = blk.reshape(128, 512)
    m64 = np.ones(128, np.float32); m64[::64] = 0
    m8 = np.ones(128, np.float32); m8[::8] = 0
    c[:, C_SCAN:C_SCAN + 128] = m64[None, :]
    c[:, C_SCAN + 128:C_SCAN + 256] = m8[None, :]
    s = np.arange(128)[:, None]
    t = np.arange(128)[None, :]
    c[:, C_AM:C_AM + 128] = ((s // 64 == t // 64) & (s <= t)).astype(np.float32)
    c[:, C_AM + 128:C_AM + 256] = ((s // 8 == t // 8) & (s <= t)).astype(np.float32)
    for g in range(2):
        blk = np.zeros((128, 4, 8), np.float64)
        blkn = np.zeros((128, 4, 8), np.float64)
        for j in range(4):
            for tt in range(8):
                dist = 128 + tt - np.arange(128)
                blk[:, j, tt] = np.where((dist >= 0) & (dist < 128), np.exp(-slopes[g, j] * dist), 0.0)
                distn = tt - (np.arange(128) % 8)
                blkn[:, j, tt] = np.where(distn >= 0, np.exp(-slopes[g, j] * np.maximum(distn, 0)), 0.0)
        c[:, C_DSC + g * 32:C_DSC + (g + 1) * 32] = blk.reshape(128, 32)
        c[:, C_DSN + g * 32:C_DSN + (g + 1) * 32] = blkn.reshape(128, 32)
    c[:, C_ONES:C_ONES + 128] = 1.0
    c[:, C_SEQM:C_SEQM + 16] = (np.arange(128)[:, None] // 8 == np.arange(16)[None, :]).astype(np.float32)
    c[:, C_EPS] = LN_EPS / (ALPHA * ALPHA)
    c[:, C_EPS + 1] = RMS_EPS
    return c


def build(n_ptiles=8, sample=True, stages="ABC"):
    from contextlib import ExitStack
    nc = bass.Bass("TRN2", target_bir_lowering=False)
    NT = 512 * n_ptiles + (128 if sample else 0)
    NSEQ = max(1, (n_ptiles + 3) // 4)

    def din(name, shape):
        return nc.dram_tensor(name, shape, F32, kind="ExternalInput").ap()

    def dout(name, shape):
        return nc.dram_tensor(name, shape, F32, kind="ExternalOutput").ap()

    xin = din("xin", [NT, D])
    w13 = {"a": din("w13a", [NPIECE_FF, 128, 4096]), "c": din("w13b", [NPIECE_FF, 128, 4096])}
    w2 = {"a": din("w2a", [128, NJ * 1024]), "c": din("w2b", [128, NJ * 1024])}
    win = din("win", [NPIECE_IN, 128, 4096])
    wout = din("wout", [128, 8 * 1024])
    lnp = din("lnp", [6, D])
    ang = din("ang", [1, 512])
    sink = din("sink", [1, 8])
    hng = din("hng", [128, 4])
    lbp = din("lbp", [128, 8])
    cst_d = din("cst", [128, NCST])
    st_in = din("st_in", [16, 4, 128, 128])
    ck_in = din("ck_in", [16, 128, 128])
    cv_in = din("cv_in", [16, 128, 128])
    y = dout("y", [NT, D])
    st_p = dout("st_p", [2, 4, 128, 128])
    ck_p = dout("ck_p", [2, 128, 128])
    cv_p = dout("cv_p", [2, 128, 128])
    st_s = dout("st_s", [16, 4, 128, 128])
    ck_s = dout("ck_s", [16, 128, 128])
    cv_s = dout("cv_s", [16, 128, 128])

    scr = {
        ("A",): nc.dram_tensor("scr_w13a", [NPIECE_FF, 128, 4096], BF16).ap(),
        ("C",): nc.dram_tensor("scr_w13b", [NPIECE_FF, 128, 4096], BF16).ap(),
        ("B",): nc.dram_tensor("scr_win", [NPIECE_IN, 128, 4096], BF16).ap(),
        ("w2", "a"): nc.dram_tensor("scr_w2a", [128, NJ * 1024], BF16).ap(),
        ("w2", "c"): nc.dram_tensor("scr_w2b", [128, NJ * 1024], BF16).ap(),
        ("wout",): nc.dram_tensor("scr_wout", [128, 8 * 1024], BF16).ap(),
    }
    st = ExitStack()
    S = Sched(nc)
    SKIP = set()

    def sb(name, shape, dt):
        return st.enter_context(nc.sbuf_tensor(name, shape, dt))

    cst = sb("cst_sb", [128, NCST], F32)
    identb = sb("identb", [128, 128], BF16)
    msk512 = sb("msk512", [128, 512], F32)
    lnbuf = sb("lnbuf", [128, 2048], F32)
    angb = sb("angb", [128, 512], F32)
    esink = sb("esink", [128, 8], F32)
    hngt = sb("hngt", [128, 4], F32)
    lbt = sb("lbt", [128, 8], F32)
    lbv = sb("lbv", [128, 16], F32)
    xres = sb("xres", [128, 4, D], F32)
    rbuf = sb("rbuf", [128, D], F32)
    xb = sb("xb", [128, D], BF16)
    xT = sb("xT", [128, 8, 512], BF16)
    hT = sb("hT", [128, NJ, 512], BF16)
    Wbig = sb("Wbig", [128, NJ * 1024], BF16)
    wst = [sb(f"wst{i}", [128, 4096], BF16) for i in range(3)]
    Wout = sb("Wout", [128, 8 * 1024], BF16)
    sil = [sb(f"sil{i}", [128, 512], F32) for i in range(2)]
    hq_s = sb("hq_s", [128, 4, 512], BF16)
    G = sb("G", [128, 4, 512], BF16)
    KT = sb("KT", [128, 5, 128], BF16)
    Vaug = sb("Vaug", [128, 5, 2, 65], BF16)
    Ef = [sb(f"Ef{i}", [128, 512], F32) for i in range(2)]
    Pb = sb("Pb", [128, 4, 512], BF16)
    oa = sb("oa", [128, 512], F32)
    oan = sb("oan", [128, 512], BF16)
    mixT = sb("mixT", [128, 8, 512], BF16)
    Ktok4 = [sb(f"Ktok4_{i}", [128, 4, 128], BF16) for i in range(2)]
    Vtok4 = [sb(f"Vtok4_{i}", [128, 4, 128], BF16) for i in range(2)]
    Am4 = sb("Am4", [128, 4, 128], BF16)
    Ktok = [Ktok4[0][:, i, :] for i in range(2)]
    Vtok = [Vtok4[0][:, i, :] for i in range(2)]
    Am = [Am4[:, i, :] for i in range(2)]
    Sf = sb("Sf", [128, 4, 128], F32)
    Sb = sb("Sb", [128, 4, 128], BF16)
    EBL = sb("EBL", [128, 4, 16], F32)
    ohf = [sb(f"ohf{i}", [128, 128], F32) for i in range(2)]
    osq = [sb(f"osq{i}", [128, 128], F32) for i in range(2)]
    ors = [sb(f"ors{i}", [128, 128], F32) for i in range(2)]
    kvf = sb("kvf", [128, 2, 128], F32)
    kvt = sb("kvt", [128, 2, 128], F32)
    st6 = sb("st6", [128, 12], F32)
    mv = sb("mv", [128, 2], F32)
    sm = sb("sm", [128, 8], F32)
    den = sb("den", [128, 16], F32)
    xres_b = xres.bitcast(BF16)
    S0f = [xres[:, 1, i * 512:(i + 1) * 512].rearrange("p (h v) -> p h v", h=4) for i in range(2)]
    Snew = [xres[:, 2, i * 512:(i + 1) * 512].rearrange("p (h v) -> p h v", h=4) for i in range(2)]
    S0b = [xres_b[:, 3, i * 512:(i + 1) * 512].rearrange("p (h v) -> p h v", h=4) for i in range(2)]
    Vm = [xres_b[:, 3, 1024 + i * 128:1024 + (i + 1) * 128] for i in range(2)]
    KTs = [xres_b[:, 3, 1280 + i * 128:1280 + (i + 1) * 128] for i in range(2)]
    Pc = [xres_b[:, 3, 1536 + i * 64:1536 + (i + 1) * 64] for i in range(2)]
    Pn = [xres_b[:, 3, 1664 + i * 64:1664 + (i + 1) * 64] for i in range(2)]
    SAMPLE_KEYS = ([(n, i) for n in ("Vm", "KTs", "Pc", "Pn") for i in range(2)]
                   + [(n, r) for n in ("S0fb", "S0bb", "Snewb") for r in range(8)])

    ps = [st.enter_context(nc.psum_tensor(f"ps{i}", [128, 512], F32)) for i in range(8)]
    psTb = ps[6].bitcast(BF16)

    wstf = [w.bitcast(F32) for w in wst]

    tiles = [("p", t, 512 * t) for t in range(n_ptiles)] + ([("s", 0, 512 * n_ptiles)] if sample else [])
    tinfo = [dict(kind=k_, t=t_, row0=r_, NB=(4 if k_ == "p" else 1), loaded=set(), feat=set()) for (k_, t_, r_) in tiles]
    plan = []
    piece_pos = []
    for ti, tl in enumerate(tiles):
        for stg in "ABC":
            if stg not in stages:
                continue
            npieces = NPIECE_IN if stg == "B" else NPIECE_FF
            for p in range(npieces):
                piece_pos.append(len(plan))
                plan.append(("piece", stg, p, len(piece_pos) - 1, ti))
                if stg == "A" and p in (3, 5, 7, 9):
                    plan.append(("w2q", "a", (p - 3) // 2, ti))
                if stg == "B" and p == 1:
                    plan.append(("wout", ti))
                if stg == "B" and p == NPIECE_IN - 1 and "C" in stages:
                    for q_ in range(4):
                        plan.append(("w2q", "c", q_, ti))
                if stg == "C" and "B" not in stages and p in (3, 5, 7, 9):
                    plan.append(("w2q", "c", (p - 3) // 2, ti))
                if stg == stages[-1] and p == npieces - 1 and ti + 1 < len(tiles):
                    plan.append(("xcast", ti + 1))
    state = dict(cursor=0, npiece=0, limit=len(plan))
    xcast_done = set()

    def ensure_xcast(tl):
        if id(tl) not in xcast_done:
            xcast_done.add(id(tl))
            emit_xcast(tl)

    pending_wb = []

    def flush_wb(keep=0):
        while len(pending_wb) > keep:
            fn, rk, wk = pending_wb.pop(0)
            S.dma("pool", fn, "scrwb", reads=rk, writes=wk)

    NCONV = max(1, min(4, n_ptiles))

    def emit_plan_entry(ent):
        kind = ent[0]
        if kind == "piece":
            _, stg, p, n, ti_ = ent
            slot = n % 3
            ct = p % NCONV
            if ti_ <= ct:
                src = win[p] if stg == "B" else w13["a" if stg == "A" else "c"][p]
                S.dma("pool", lambda e, slot=slot, src=src: e.dma_start(out=wst[slot][:], in_=src),
                      ("wst", slot), writes=[("wst", slot)])
            else:
                src = scr[(stg,)][p]
                S.dma("pool", lambda e, slot=slot, src=src: e.dma_start(out=wst[slot][:], in_=src),
                      ("wst", slot), reads=[("scr", stg, p)], writes=[("wst", slot)])
            flush_wb(0)
            if ti_ == ct:
                dst = scr[(stg,)][p]
                pending_wb.append((lambda e, slot=slot, dst=dst: e.dma_start(out=dst, in_=wst[slot][:]),
                                   [("wst", slot)], [("scr", stg, p)]))
        elif kind == "w2q":
            _, x_, q, ti_ = ent
            a, b_ = q * 5632, (q + 1) * 5632
            ct = q % NCONV
            if ti_ <= ct:
                src = w2[x_]
                S.dma("pool", lambda e, a=a, b_=b_, src=src: e.dma_start(out=Wbig[:, a:b_], in_=src[:, a:b_]),
                      ("Wbig", q), writes=[("Wbig", q)])
                if ti_ == ct:
                    dst = scr[("w2", x_)]
                    pending_wb.append((lambda e, a=a, b_=b_, dst=dst: e.dma_start(out=dst[:, a:b_], in_=Wbig[:, a:b_]),
                                       [("Wbig", q)], [("scr", "w2", x_, q)]))
            else:
                src = scr[("w2", x_)]
                S.dma("pool", lambda e, a=a, b_=b_, src=src: e.dma_start(out=Wbig[:, a:b_], in_=src[:, a:b_]),
                      ("Wbig", q), reads=[("scr", "w2", x_, q)], writes=[("Wbig", q)])
        elif kind == "xcast":
            ensure_xcast(tinfo[ent[1]])
        elif kind == "wout":
            ti_ = ent[1]
            for q in range(2):
                a, b_ = q * 4096, (q + 1) * 4096
                ct = (2 + q) % NCONV
                if ti_ <= ct:
                    S.dma("pool", lambda e, a=a, b_=b_: e.dma_start(out=Wout[:, a:b_], in_=wout[:, a:b_]),
                          ("Wout", q), writes=[("Wout", q)])
                    if ti_ == ct:
                        dst = scr[("wout",)]
                        pending_wb.append((lambda e, a=a, b_=b_, dst=dst: e.dma_start(out=dst[:, a:b_], in_=Wout[:, a:b_]),
                                           [("Wout", q)], [("scr", "wout", q)]))
                else:
                    src = scr[("wout",)]
                    S.dma("pool", lambda e, a=a, b_=b_, src=src: e.dma_start(out=Wout[:, a:b_], in_=src[:, a:b_]),
                          ("Wout", q), reads=[("scr", "wout", q)], writes=[("Wout", q)])

    def ensure(k):
        k = min(k, state["limit"] - 1, len(plan) - 1)
        while state["cursor"] <= k:
            emit_plan_entry(plan[state["cursor"]])
            state["cursor"] += 1

    def next_piece():
        n = state["npiece"]
        state["npiece"] += 1
        ensure(piece_pos[min(n + 2, len(piece_pos) - 1)])
        assert state["cursor"] > piece_pos[n], "piece not emitted (limit too tight)"
        return n % 3

    def wbig_keys(j):
        return [("Wbig", (j * 1024) // 5632), ("Wbig", (j * 1024 + 1023) // 5632)]

    S.dma("sp", lambda e: e.dma_start(out=cst[:], in_=cst_d), "cst", writes=["cst"])
    S.dma("sp", lambda e: e.dma_start(out=angb[:], in_=ang[0:1, :].to_broadcast([128, 512])), "angb", writes=["angb"])
    S.dma("sp", lambda e: e.dma_start(out=esink[:], in_=sink[0:1, :].to_broadcast([128, 8])), "esink", writes=["esink"])
    S.dma("sp", lambda e: e.dma_start(out=hngt[:], in_=hng), "hngt", writes=["hngt"])
    S.dma("sp", lambda e: e.dma_start(out=lbt[:], in_=lbp), "lbt", writes=["lbt"])
    S.op("dve", lambda e: e.tensor_copy(out=identb[:], in_=cst[:, C_ID:C_ID + 128]), reads=["cst"], writes=["identb"])
    for r in range(4):
        S.op("dve", lambda e, r=r: e.tensor_copy(out=msk512[:, r * 128:(r + 1) * 128], in_=cst[:, C_SCAN:C_SCAN + 128]),
             reads=["cst"], writes=[("msk512", r)])
    S.op("act", lambda e: e.activation(out=esink[:], in_=esink[:], func=AF.Exp), reads=["esink"], writes=["esink"])
    S.op("dve", lambda e: e.tensor_tensor(out=lbv[:, 12:16], in0=lbt[:, 4:8], in1=lbt[:, 0:4], op=ALU.subtract),
         reads=["lbt"], writes=["lbv"])
    S.op("act", lambda e: e.activation(out=lbv[:, 12:16], in_=lbv[:, 12:16], func=AF.Exp), reads=["lbv"], writes=["lbv"])
    S.op("dve", lambda e: e.tensor_scalar(out=lbv[:, 12:16], in0=lbv[:, 12:16], scalar1=1.0, scalar2=None, op0=ALU.add),
         reads=["lbv"], writes=["lbv"])
    S.op("dve", lambda e: e.reciprocal(out=lbv[:, 0:4], in_=lbv[:, 12:16]), reads=["lbv"], writes=["lbv"])
    S.op("dve", lambda e: e.tensor_scalar(out=lbv[:, 4:8], in0=lbv[:, 0:4], scalar1=-1.0, scalar2=1.0, op0=ALU.mult, op1=ALU.add),
         reads=["lbv"], writes=["lbv"])
    S.op("dve", lambda e: e.tensor_scalar(out=lbv[:, 8:12], in0=lbv[:, 4:8], scalar1=-1.0, scalar2=None, op0=ALU.mult),
         reads=["lbv"], writes=["lbv"])
    S.op("dve", lambda e: e.memset(Vaug[:], 1.0), writes=["Vaug_all"])

    eps_ln = cst[:, C_EPS:C_EPS + 1]
    eps_rms = cst[:, C_EPS + 1:C_EPS + 2]
    idf = cst[:, C_ID:C_ID + 128]
    onesf = cst[:, C_ONES:C_ONES + 128]

    def to_featmajor(NB, m, src_key):
        S.op("dve", lambda e: e.tensor_copy(out=xb[:], in_=xres[:, m, :]), reads=[src_key], writes=["xb"])
        for kc in range(8):
            S.op("pe", lambda e, kc=kc: e.transpose(psTb[:, kc * 128:(kc + 1) * 128], xb[:, kc * 128:(kc + 1) * 128], identb[:]),
                 reads=["xb", "identb"], writes=[("ps", 6)])
        S.op("act", lambda e: e.copy(out=xT[:, :, m * 128:(m + 1) * 128],
                                     in_=psTb[:, :].rearrange("p (k t) -> p k t", k=8)),
             reads=[("ps", 6)], writes=[("xT", m)])

    def load_ln(idx):
        S.dma("sp", lambda e: e.dma_start(out=lnbuf[:, 0:1024], in_=lnp[2 * idx:2 * idx + 1, :].to_broadcast([128, 1024])),
              "lnbuf_g", writes=["lnbuf_g"])
        S.dma("sp", lambda e: e.dma_start(out=lnbuf[:, 1024:2048], in_=lnp[2 * idx + 1:2 * idx + 2, :].to_broadcast([128, 1024])),
              "lnbuf_b", writes=["lnbuf_b"])

    def ln_epilogue(m, psd, scale):
        xk = ("xres", m)
        for h in range(2):
            S.op("dve", lambda e, h=h: e.scalar_tensor_tensor(out=xres[:, m, h * 512:(h + 1) * 512], in0=ps[psd[h]][:], scalar=scale / ALPHA,
                                                          in1=xres[:, m, h * 512:(h + 1) * 512], op0=ALU.mult, op1=ALU.add),
                 reads=[("ps", psd[h]), xk], writes=[xk])
        S.op("dve", lambda e: e.bn_stats(out=st6[:, 0:6], in_=xres[:, m, 0:512]), reads=[xk], writes=["st6a"])
        S.op("dve", lambda e: e.bn_stats(out=st6[:, 6:12], in_=xres[:, m, 512:1024]), reads=[xk], writes=["st6b"])
        S.op("dve", lambda e: e.bn_aggr(out=mv[:], in_=st6[:]), reads=["st6a", "st6b"], writes=["mv"])
        S.op("act", lambda e: e.activation(out=sm[:, 0:1], in_=mv[:, 1:2], func=AF.Ln, bias=eps_ln), reads=["mv", "cst"], writes=["sm0"])
        S.op("act", lambda e: e.activation(out=sm[:, 1:2], in_=sm[:, 0:1], func=AF.Exp, scale=-0.5), reads=["sm0"], writes=["sm1"])
        S.op("dve", lambda e: e.tensor_scalar(out=xres[:, m, :], in0=xres[:, m, :], scalar1=mv[:, 0:1], scalar2=sm[:, 1:2],
                                              op0=ALU.subtract, op1=ALU.mult),
             reads=[xk, "mv", "sm1"], writes=[xk])
        S.op("dve", lambda e: e.tensor_tensor(out=xres[:, m, :], in0=xres[:, m, :], in1=lnbuf[:, 0:1024], op=ALU.mult),
             reads=[xk, "lnbuf_g"], writes=[xk])
        S.op("dve", lambda e: e.tensor_tensor(out=xres[:, m, :], in0=xres[:, m, :], in1=lnbuf[:, 1024:2048], op=ALU.add),
             reads=[xk, "lnbuf_b"], writes=[xk])

    xstage = mixT[:].rearrange("p (m a) t -> p m (a t)", m=4)
    ALLMIX = ([("mixT", "a", mm) for mm in range(4)] + [("mixT", "h", mm) for mm in range(4)]
              + [("mixT", "hh", hh_, mm) for hh_ in range(3) for mm in range(4)])

    def emit_xcast(tl):
        nb_, r0 = tl["NB"], tl["row0"]
        S.dma("pool", lambda e: e.dma_start(out=xstage[:, 0:nb_, :], in_=xin[r0:r0 + nb_ * 128, :].rearrange("(m p) d -> p m d", p=128)),
              ("xstage", 0), writes=ALLMIX + [("xstage", m) for m in range(nb_)])

    def emit_feat_stage(tl, m):
        if m in tl["feat"]:
            return
        tl["feat"].add(m)
        for kc in range(8):
            S.op("pe", lambda e, kc=kc: e.transpose(psTb[:, kc * 128:(kc + 1) * 128], xstage[:, m, kc * 128:(kc + 1) * 128], identb[:]),
                 reads=ALLMIX + [("xstage", m), "identb"], writes=[("ps", 6)])
        S.op("act", lambda e: e.copy(out=xT[:, :, m * 128:(m + 1) * 128], in_=psTb[:, :].rearrange("p (k t) -> p k t", k=8)),
             reads=[("ps", 6)], writes=[("xT", m)])

    def emit_xload(tl, m):
        if m in tl["loaded"]:
            return
        tl["loaded"].add(m)
        r0 = tl["row0"] + m * 128
        S.dma("sp", lambda e: e.dma_start(out=xres[:, m, :], in_=xin[r0:r0 + 128, :]), ("xload", m), writes=[("xres", m)])

    def emit_feat(tl, m):
        if m in tl["feat"]:
            return
        emit_xload(tl, m)
        tl["feat"].add(m)
        to_featmajor(tl["NB"], m, ("xres", m))

    def ffn_stage(which, NB, row0, final, nxt=None):
        N = NB * 128
        load_ln(0 if which == "a" else 2)
        xkeys = [("xT", m) for m in range(NB)]
        for p in range(NPIECE_FF):
            slot = next_piece()
            for c in range(2):
                j = 2 * p + c
                bg, bu = 2 * (j % 2), 2 * (j % 2) + 1
                for kc in range(8):
                    S.op("pe", lambda e, kc=kc, c=c, slot=slot, bg=bg: e.matmul(
                        ps[bg][:, 0:N], lhsT=wst[slot][:, kc * 512 + c * 128:kc * 512 + (c + 1) * 128],
                        rhs=xT[:, kc, 0:N], start=(kc == 0), stop=(kc == 7)),
                        reads=[("wst", slot)] + xkeys, writes=[("ps", bg)])
                for kc in range(8):
                    S.op("pe", lambda e, kc=kc, c=c, slot=slot, bu=bu: e.matmul(
                        ps[bu][:, 0:N], lhsT=wst[slot][:, kc * 512 + 256 + c * 128:kc * 512 + 256 + (c + 1) * 128],
                        rhs=xT[:, kc, 0:N], start=(kc == 0), stop=(kc == 7)),
                        reads=[("wst", slot)] + xkeys, writes=[("ps", bu)])
                si = j % 2
                S.op("act", lambda e, si=si, bg=bg: e.activation(out=sil[si][:, 0:N], in_=ps[bg][:, 0:N], func=AF.Silu),
                     reads=[("ps", bg)], writes=[("sil", si)])
                S.op("dve", lambda e, si=si, bu=bu, j=j: e.tensor_tensor(out=hT[:, j, 0:N], in0=sil[si][:, 0:N], in1=ps[bu][:, 0:N], op=ALU.mult),
                     reads=[("sil", si), ("ps", bu)], writes=[("hT", j)])
        pairs = ((4, 5), (0, 1), (2, 3))
        for m in range(NB):
            pr = pairs[m % 3]
            for h in range(2):
                for j in range(NJ):
                    S.op("pe", lambda e, j=j, h=h, m=m, pr=pr: e.matmul(
                        ps[pr[h]][:], lhsT=hT[:, j, m * 128:(m + 1) * 128],
                        rhs=Wbig[:, j * 1024 + h * 512:j * 1024 + (h + 1) * 512], start=(j == 0), stop=(j == NJ - 1)),
                        reads=[("hT", j)] + wbig_keys(j), writes=[("ps", pr[h])])
            if final:
                if nxt is not None and m < nxt["NB"]:
                    emit_feat_stage(nxt, m)
            elif m >= 2:
                to_featmajor(NB, m - 2, ("xres", m - 2))
            ln_epilogue(m, pr, 0.5)
            if final:
                S.dma("sp", lambda e, m=m: e.dma_start(out=y[row0 + m * 128:row0 + (m + 1) * 128, :], in_=xres[:, m, :]),
                      ("yout", m), reads=[("xres", m)])
                if nxt is not None and m < nxt["NB"]:
                    emit_xload(nxt, m)
        if final:
            if nxt is not None:
                for m in range(nxt["NB"]):
                    emit_feat_stage(nxt, m)
        else:
            for m in range(max(0, NB - 2), NB):
                to_featmajor(NB, m, ("xres", m))

    QT = lambda: hT[:, 0:4, :]

    def attn_finish(m, pso, split=False):
        for g in range(2):
            pv = ps[pso[g]][:, 0:260].rearrange("p (j d) -> p j d", j=4)
            S.op("dve", lambda e, g=g, pv=pv: e.tensor_tensor(out=den[:, 4 * g:4 * g + 4], in0=pv[:, :, 64], in1=esink[:, 4 * g:4 * g + 4], op=ALU.add),
                 reads=[("ps", pso[g]), "esink"], writes=[("den", g)])
        S.op("dve", lambda e: e.reciprocal(out=den[:, 8:16], in_=den[:, 0:8]), reads=[("den", 0), ("den", 1)], writes=["rden"])
        for g in range(2):
            for j in range(4):
                hh = 4 * g + j
                S.op("dve", lambda e, g=g, j=j, hh=hh: e.tensor_scalar(
                    out=oa[:, hh * 64:(hh + 1) * 64], in0=ps[pso[g]][:, j * 65:j * 65 + 64], scalar1=den[:, 8 + hh:9 + hh], scalar2=None, op0=ALU.mult),
                    reads=[("ps", pso[g]), "rden"], writes=[("oa", hh)])
        oak = [("oa", hh) for hh in range(8)]
        S.op("act", lambda e: e.activation(out=sil[0][:, 0:512], in_=oa[:], func=AF.Square, accum_out=sm[:, 2:3]),
             reads=oak, writes=[("sil", 0), "sm2"])
        S.op("act", lambda e: e.activation(out=sm[:, 3:4], in_=sm[:, 2:3], func=AF.Ln, bias=eps_rms, scale=1.0 / 512.0),
             reads=["sm2", "cst"], writes=["sm3"])
        S.op("act", lambda e: e.activation(out=sm[:, 4:5], in_=sm[:, 3:4], func=AF.Exp, scale=-0.5), reads=["sm3"], writes=["sm4"])
        S.op("dve", lambda e: e.scalar_tensor_tensor(out=oan[:], in0=oa[:], scalar=sm[:, 4:5], in1=angb[:], op0=ALU.mult, op1=ALU.mult),
             reads=oak + ["sm4", "angb"], writes=["oan"])
        if not split:
            attn_finish_c(m)

    def attn_finish_c(m):
        for c in range(4):
            S.op("pe", lambda e, c=c: e.transpose(psTb[:, c * 128:(c + 1) * 128], oan[:, c * 128:(c + 1) * 128], identb[:]),
                 reads=["oan", "identb"], writes=[("ps", 6)])
        S.op("act", lambda e: e.copy(out=mixT[:, 0:4, m * 128:(m + 1) * 128], in_=psTb[:, 0:512].rearrange("p (k t) -> p k t", k=4)),
             reads=[("ps", 6)], writes=[("mixT", "a", m)])

    def hgrn_out(h, m, hi_):
        i = hi_ % 2
        src = ps[7][:, 128 * (h % 4):128 * (h % 4) + 128] if False else None
        return i

    def mixer_stage(kind, t, NB, row0, final):
        N = NB * 128
        is_s = (kind == "s")
        pos = 0 if is_s else (t % 4)
        seq = 0 if is_s else (t // 4)
        load_ln(1)
        xkeys = [("xT", m) for m in range(NB)]
        n0 = state["npiece"]
        state["limit"] = piece_pos[n0 + NPIECE_IN - 1] + 1 if True else 0
        state["limit"] = (piece_pos[n0 + NPIECE_IN] if (is_s and n0 + NPIECE_IN < len(piece_pos)) else len(plan))
        T1 = wstf[(n0 + 3) % 3]
        T2 = wstf[(n0 + 4) % 3]
        T3 = wstf[(n0 + 5) % 3]
        k1, k2, k3 = ("wst", (n0 + 3) % 3), ("wst", (n0 + 4) % 3), ("wst", (n0 + 5) % 3)
        want_cache = (is_s or pos == 3) and "nokvf" not in SKIP
        hTf = hT.bitcast(F32)
        mixTf = mixT.bitcast(F32)
        T1h = [hTf[:, 17:19, :].rearrange("p a t -> p (a t)"), hTf[:, 19:21, :].rearrange("p a t -> p (a t)"),
               mixTf[:, 4:6, :].rearrange("p a t -> p (a t)"), mixTf[:, 6:8, :].rearrange("p a t -> p (a t)")]
        MIXH = [("mixT", "h", mm) for mm in range(4)] + [("mixT", "hh", hh_, mm) for hh_ in range(3) for mm in range(4)]
        T1K = [[("T1h", 0)], [("T1h", 1)], [("T1h", 2)] + MIXH, [("T1h", 3)] + MIXH]
        per = 8 if is_s else 64
        mask = cst[:, C_SCAN + 128:C_SCAN + 256] if is_s else msk512[:, 0:N]
        mkeys = ["cst"] if is_s else [("msk512", r) for r in range(4)]

        def b2_chain(h):
            t1 = T1h[h]
            if h % 2 == 0:
                t2, t3, k2_, k3_ = rbuf[:, 0:N], rbuf[:, 512:512 + N], ("rbuf", 0), ("rbuf", 1)
            else:
                t2, t3, k2_, k3_ = sil[1][:, 0:N], xb.bitcast(F32)[:, 0:N], ("sil", 1), "xb"
            S.op("act", lambda e: e.activation(out=t2, in_=t1[:, 0:N], func=AF.Ln, bias=lbv[:, h:h + 1], scale=lbv[:, 4 + h:5 + h]),
                 reads=T1K[h] + ["lbv"], writes=[k2_])
            S.op("dve", lambda e: e.tensor_scalar(out=t1[:, 0:N], in0=t1[:, 0:N], scalar1=lbv[:, 8 + h:9 + h], scalar2=lbv[:, 4 + h:5 + h],
                                                  op0=ALU.mult, op1=ALU.add),
                 reads=T1K[h] + ["lbv", k2_], writes=T1K[h])
            S.op("dve", lambda e: e.tensor_tensor_scan(out=t3, data0=mask, data1=t2, initial=0.0, op0=ALU.mult, op1=ALU.add),
                 reads=[k2_] + mkeys, writes=[k3_])
            S.op("act", lambda e: e.activation(out=t2, in_=t3, func=AF.Exp), reads=[k3_], writes=[k2_])
            S.op("act", lambda e: e.activation(out=t3, in_=t3, func=AF.Exp, scale=-1.0), reads=[k3_], writes=[k3_])
            S.op("dve", lambda e: e.tensor_tensor(out=hT[:, 8 + h, 0:N], in0=hq_s[:, h, 0:N], in1=t2, op=ALU.mult),
                 reads=[("hq_s", h), k2_], writes=[("hT", 8 + h)])
            S.op("dve", lambda e: e.tensor_tensor(out=hT[:, 12 + h, 0:N], in0=t1[:, 0:N], in1=t3, op=ALU.mult),
                 reads=T1K[h] + [k3_], writes=[("hT", 12 + h)])
            nl = N // per
            S.op("act", lambda e: e.copy(out=EBL[:, h, 0:nl], in_=t2.rearrange("p (c s) -> p c s", s=per)[:, :, per - 1]),
                 reads=[k2_], writes=[("EBL", h)])

        pbanks = (0, 1, 7)
        nchunk = 0
        pend_a2 = []
        for p in range(NPIECE_IN):
            slot = next_piece()
            nch = 2 if p == 1 else 4
            for c in range(nch):
                b = pbanks[nchunk % 3]
                nchunk += 1
                for kc in range(8):
                    S.op("pe", lambda e, kc=kc, c=c, slot=slot, b=b: e.matmul(
                        ps[b][:, 0:N], lhsT=wst[slot][:, kc * 512 + c * 128:kc * 512 + (c + 1) * 128],
                        rhs=xT[:, kc, 0:N], start=(kc == 0), stop=(kc == 7)),
                        reads=[("wst", slot)] + xkeys, writes=[("ps", b)])
                pk = [("ps", b)]
                if c == 1 and pend_a2:
                    prompt_attn_a(pend_a2.pop(), pos, seq, gsel=(1,))
                if p == 0:
                    S.op("act", lambda e, c=c, b=b: e.copy(out=hT[:, c, 0:N], in_=ps[b][:, 0:N]), reads=pk, writes=[("hT", c)])
                elif p == 1 and c == 0:
                    S.op("act", lambda e, b=b: e.copy(out=KT[:, 1:1 + NB, :], in_=ps[b][:, 0:N].rearrange("p (m t) -> p m t", m=NB)),
                         reads=pk, writes=[("KT", 1 + m) for m in range(NB)])
                    if want_cache:
                        S.op("act", lambda e, b=b: e.copy(out=kvf[:, 0, :], in_=ps[b][:, N - 128:N]), reads=pk, writes=[("kvf", 0)])
                elif p == 1:
                    S.op("act", lambda e, b=b: e.copy(out=hT[:, 16, 0:N], in_=ps[b][:, 0:N]), reads=pk, writes=[("hT", 16)])
                    if want_cache:
                        S.op("act", lambda e, b=b: e.copy(out=kvf[:, 1, :], in_=ps[b][:, N - 128:N]), reads=pk, writes=[("kvf", 1)])
                elif p == 2:
                    S.op("act", lambda e, c=c, b=b: e.activation(out=hq_s[:, c, 0:N], in_=ps[b][:, 0:N], func=AF.Silu), reads=pk, writes=[("hq_s", c)])
                elif p == 3:
                    S.op("act", lambda e, c=c, b=b: e.activation(out=T1h[c][:, 0:N], in_=ps[b][:, 0:N], func=AF.Sigmoid), reads=pk, writes=T1K[c])
                elif p == 4:
                    S.op("dve", lambda e, c=c, b=b: e.tensor_copy(out=hT[:, 4 + c, 0:N], in_=ps[b][:, 0:N]), reads=pk, writes=[("hT", 4 + c)])
                    b2_chain(c)
                else:
                    S.op("act", lambda e, c=c, b=b: e.activation(out=G[:, c, 0:N], in_=ps[b][:, 0:N], func=AF.Silu), reads=pk, writes=[("G", c)])
            if not is_s:
                if p >= 3:
                    prompt_attn_c(p - 3, pos, seq)
                if p >= 2:
                    prompt_attn_b(p - 2, pos, seq)
                if 1 <= p <= 4:
                    prompt_attn_a(p - 1, pos, seq, gsel=(0,))
                    pend_a2.append(p - 1)
        if not is_s:
            if pend_a2:
                prompt_attn_a(pend_a2.pop(), pos, seq, gsel=(1,))
            prompt_hgrn1(0, pos, seq)
            prompt_attn_c(3, pos, seq)
        if is_s:
            sample_core(k1, k2, k3, T1, T2, T3)
            S.op("dve", lambda e: e.memset(sm[:, 7:8], 0.0), writes=[("xres", 1), ("xres", 2), ("xres", 3)] + SAMPLE_KEYS
                 + [("Ef", 0), ("Ef", 1), ("Ktok", 0), ("Ktok", 1), ("Vtok", 0), ("Vtok", 1), ("Am", 0), ("Am", 1),
                    ("Ktok4", 0), ("Vtok4", 0), ("Am4", 0), ("Am4", 1)])
        state["limit"] = len(plan)
        WB = (0, 1, 7)

        def wout_mm(m):
            wb = (WB[(2 * m) % 3], WB[(2 * m + 1) % 3])
            for hf_ in range(2):
                for c in range(8):
                    S.op("pe", lambda e, c=c, hf_=hf_, m=m, wb=wb: e.matmul(
                        ps[wb[hf_]][:], lhsT=mixT[:, c, m * 128:(m + 1) * 128],
                        rhs=Wout[:, c * 1024 + hf_ * 512:c * 1024 + (hf_ + 1) * 512], start=(c == 0), stop=(c == 7)),
                        reads=[("mixT", "a", m), ("mixT", "h", m), ("Wout", c // 4)], writes=[("ps", wb[hf_])])

        def wout_ep(m):
            wb = (WB[(2 * m) % 3], WB[(2 * m + 1) % 3])
            if (not final) and m >= 2:
                to_featmajor(NB, m - 2, ("xres", m - 2))
            ln_epilogue(m, wb, 1.0)
            if final:
                S.dma("sp", lambda e, m=m: e.dma_start(out=y[row0 + m * 128:row0 + (m + 1) * 128, :], in_=xres[:, m, :]),
                      ("yout", m), reads=[("xres", m)])

        for m in range(NB):
            if m >= 1:
                wout_mm(m - 1)
            if not is_s:
                prompt_hgrn2(m, pos, seq)
                prompt_hgrn3(m, pos, seq)
                if m + 1 < NB:
                    prompt_hgrn1(m + 1, pos, seq)
            if m >= 1:
                wout_ep(m - 1)
        wout_mm(NB - 1)
        wout_ep(NB - 1)
        if not final:
            for m in range(max(0, NB - 2), NB):
                to_featmajor(NB, m, ("xres", m))
        for m in range(0):
            pass
        if (not is_s) and pos < 3:
            S.op("act", lambda e: e.copy(out=KT[:, 0, :], in_=KT[:, 4, :]), reads=[("KT", 4)], writes=[("KT", 0)])
            S.op("act", lambda e: e.copy(out=Vaug[:, 0, :, 0:64], in_=Vaug[:, 4, :, 0:64]), reads=[("Vaug", 4)], writes=[("Vaug", 0)])

    def sample_core(k1, k2, k3, T1, T2, T3):
        n0s = state["npiece"] - NPIECE_IN
        slot2 = (n0s + 4) % 3
        CVaug = wst[slot2][:, 0:2080].rearrange("p (b g d) -> p b g d", b=16, g=2)
        seqm = cst[:, C_SEQM:C_SEQM + 16]
        S.op("dve", lambda e: e.memset(sm[:, 7:8], 0.0), writes=[("xres", 1), ("xres", 2), ("xres", 3)] + SAMPLE_KEYS)
        if "cload" not in SKIP:
            S.dma("sp", lambda e: e.dma_start(out=T1[:, 0:2048].rearrange("p (b f) -> p b f", b=16), in_=ck_in.rearrange("b w f -> w b f")),
                  "ckload", writes=[k1])
            S.dma("sp", lambda e: e.dma_start(out=T3[:, 0:2048].rearrange("p (b f) -> p b f", b=16), in_=cv_in.rearrange("b w f -> w b f")),
                  "cvload", writes=[k3])
        if "dramcopy" not in SKIP:
            S.dma("sp", lambda e: e.dma_start(out=ck_s[:, 0:120, :], in_=ck_in[:, 8:128, :]), "ckcopy")
            S.dma("sp", lambda e: e.dma_start(out=cv_s[:, 0:120, :], in_=cv_in[:, 8:128, :]), "cvcopy")
        if "cvms" not in SKIP:
            S.op("dve", lambda e: e.memset(wst[slot2][:, 0:2080], 1.0), writes=[k2])
        for g in range(2):
            if "cvaug" in SKIP:
                continue
            S.op("dve", lambda e, g=g: e.tensor_copy(out=CVaug[:, :, g, 0:64],
                                                  in_=T3[:, 0:2048].rearrange("p (b f) -> p b f", b=16)[:, :, g * 64:(g + 1) * 64]),
                 reads=[k3], writes=[k2])
        if "vt" not in SKIP:
            v_to_tok(0, 1)
        for w_, dst in ((0, ck_s), (1, cv_s)):
            if "kvtr" in SKIP:
                continue
            S.op("pe", lambda e, w_=w_: e.transpose(ps[3][:, w_ * 128:(w_ + 1) * 128], kvf[:, w_, :], idf), reads=[("kvf", w_), "cst"], writes=[("ps", 3)])
            S.op("act", lambda e, w_=w_: e.copy(out=kvt[:, w_, :], in_=ps[3][:, w_ * 128:(w_ + 1) * 128]), reads=[("ps", 3)], writes=[("kvt", w_)])
            for b in range(16):
                if "kvrows" in SKIP:
                    continue
                S.dma("sp", lambda e, w_=w_, dst=dst, b=b: e.dma_start(out=dst[b, 120:128, :], in_=kvt[8 * b:8 * b + 8, w_, :]),
                      ("kvt", w_), reads=[("kvt", w_)])
        def attn_seq(b):
            i = b % 2
            if "s_tr" not in SKIP:
                S.op("pe", lambda e, b=b: e.transpose(ps[6][:, 0:128], T1[:, b * 128:(b + 1) * 128], idf), reads=[k1, "cst"], writes=[("ps", 6)])
                S.op("act", lambda e, i=i: e.copy(out=KTs[i], in_=ps[6][:, 0:128]), reads=[("ps", 6)], writes=[("KTs", i)])
            for g in range(2):
                bk = 2 if g == 0 else 0
                q_ = hT[64 * g:64 * g + 64, 0:4, b * 8:(b + 1) * 8]
                S.op("pe", lambda e, g=g, i=i, q_=q_, bk=bk: e.matmul(ps[bk][:, 0:32], lhsT=KTs[i][64 * g:64 * g + 64, :], rhs=q_,
                                                                  start=True, stop=True),
                     reads=[("KTs", i)] + [("hT", c) for c in range(4)], writes=[("ps", bk)])
                S.op("pe", lambda e, g=g, q_=q_, bk=bk: e.matmul(ps[bk][:, 32:64], lhsT=KT[64 * g:64 * g + 64, 1, :], rhs=q_,
                                                             start=True, stop=True),
                     reads=[("KT", 1)] + [("hT", c) for c in range(4)], writes=[("ps", bk)])
                S.op("act", lambda e, i=i, g=g, bk=bk: e.activation(out=Ef[i][:, g * 64:(g + 1) * 64], in_=ps[bk][:, 0:64], func=AF.Exp, scale=SCALE),
                     reads=[("ps", bk)], writes=[("Ef", i)])
            efk = [("Ef", i)]
            efv = Ef[i][:, 0:128].rearrange("p (g c) -> p g c", g=2)
            S.op("dve", lambda e, i=i, efv=efv: e.tensor_tensor(out=Pc[i].rearrange("p (g c) -> p g c", g=2), in0=efv[:, :, 0:32],
                                                           in1=cst[:, C_DSC:C_DSC + 64].rearrange("p (g c) -> p g c", g=2), op=ALU.mult),
                 reads=efk + ["cst"], writes=[("Pc", i)])
            S.op("dve", lambda e, i=i, b=b, efv=efv: e.scalar_tensor_tensor(out=Pn[i].rearrange("p (g c) -> p g c", g=2), in0=efv[:, :, 32:64],
                                                                       scalar=seqm[:, b:b + 1],
                                                                       in1=cst[:, C_DSN:C_DSN + 64].rearrange("p (g c) -> p g c", g=2),
                                                                       op0=ALU.mult, op1=ALU.mult),
                 reads=efk + ["cst"], writes=[("Pn", i)])
            for g in range(2):
                if "s_pv" in SKIP:
                    continue
                o_ = ps[4 + g][0:65, b * 32:(b + 1) * 32]
                S.op("pe", lambda e, g=g, i=i, b=b, o_=o_: e.matmul(o_, lhsT=CVaug[:, b, g, :], rhs=Pc[i][:, g * 32:(g + 1) * 32], start=True, stop=False),
                     reads=[k2, ("Pc", i)], writes=[("ps", 4 + g)])
                S.op("pe", lambda e, g=g, i=i, o_=o_: e.matmul(o_, lhsT=Vaug[:, 1, g, :], rhs=Pn[i][:, g * 32:(g + 1) * 32], start=False, stop=True),
                     reads=[("Vaug", 1), "Vaug_all", ("Pn", i)], writes=[("ps", 4 + g)])
        def s0_load(n):
            h_, b_ = divmod(n, 16)
            r = n % 8
            S.dma("sp", lambda e, h_=h_, b_=b_, r=r: e.dma_start(out=S0f[r // 4][:, r % 4, :], in_=st_in[b_, h_]), ("S0f", r), writes=[("S0fb", r)])
        for n in range(4):
            if "shgrn" not in SKIP:
                s0_load(n)
        for h in range(4):
            if "hg" in SKIP:
                continue
            i = hgrn_head_common(h, 0, C_AM + 128)
            for b in range(16):
                if b % 4 == 0:
                    attn_seq(4 * h + b // 4)
                n = h * 16 + b
                if n + 4 < 64:
                    s0_load(n + 4)
                r = n % 8
                s0f = S0f[r // 4][:, r % 4, :]
                s0b = S0b[r // 4][:, r % 4, :]
                snw = Snew[r // 4][:, r % 4, :]
                vm = Vm[n % 2]
                cs = slice(b * 8, (b + 1) * 8)
                ebl = EBL[:, h, b:b + 1]
                S.op("act", lambda e, s0f=s0f, s0b=s0b: e.copy(out=s0b, in_=s0f), reads=[("S0fb", r)], writes=[("S0bb", r)])
                S.op("pe", lambda e, i=i, cs=cs: e.matmul(ps[7][:, 128 + cs.start:128 + cs.stop], lhsT=Vtok[i], rhs=Am[i][:, cs], start=True, stop=False),
                     reads=[("Vtok", i), ("Am", i)], writes=[("ps", 7)])
                S.op("pe", lambda e, h=h, cs=cs, s0b=s0b: e.matmul(ps[7][:, 128 + cs.start:128 + cs.stop], lhsT=s0b, rhs=hT[:, 8 + h, cs], start=False, stop=True),
                     reads=[("S0bb", r), ("hT", 8 + h)], writes=[("ps", 7)])
                S.op("dve", lambda e, i=i, b=b, vm=vm: e.tensor_scalar(out=vm, in0=Vtok[i], scalar1=seqm[:, b:b + 1], scalar2=None, op0=ALU.mult),
                     reads=[("Vtok", i), "cst"], writes=[("Vm", n % 2)])
                bu = 3 if n % 2 == 0 else 1
                S.op("pe", lambda e, i=i, vm=vm, bu=bu: e.matmul(ps[bu][:, 0:128], lhsT=Ktok[i], rhs=vm, start=True, stop=True),
                     reads=[("Ktok", i), ("Vm", n % 2)], writes=[("ps", bu)])
                S.op("dve", lambda e, s0f=s0f, ebl=ebl: e.tensor_scalar(out=s0f, in0=s0f, scalar1=ebl, scalar2=None, op0=ALU.mult),
                     reads=[("S0fb", r), ("S0bb", r), ("EBL", h)], writes=[("S0fb", r)])
                S.op("dve", lambda e, s0f=s0f, snw=snw, ebl=ebl, bu=bu: e.scalar_tensor_tensor(out=snw, in0=ps[bu][:, 0:128], scalar=ebl, in1=s0f,
                                                                                           op0=ALU.mult, op1=ALU.add),
                     reads=[("ps", bu), ("S0fb", r), ("EBL", h)], writes=[("Snewb", r)])
                S.dma("sp", lambda e, snw=snw, b=b, h=h: e.dma_start(out=st_s[b, h], in_=snw), ("Snew", r), reads=[("Snewb", r)])
            hgrn_finish(h, 0, i)
        for hh in range(8):
            if "ostr" in SKIP:
                continue
            g, j = divmod(hh, 4)
            i = hh % 2
            S.op("act", lambda e, g=g, j=j, i=i: e.copy(out=Ef[i][0:65, 256:384].rearrange("p (b t) -> p b t", b=16), in_=ps[4 + g][0:65, :].rearrange("p (b j t) -> p b j t", b=16, j=4)[:, :, j, :]),
                 reads=[("ps", 4 + g)], writes=[("Ef", i)])
            S.op("pe", lambda e, g=g, j=j, i=i: e.transpose(ps[2 + g][:, j * 65:(j + 1) * 65], Ef[i][0:65, 256:384], cst[0:65, C_ID:C_ID + 65]),
                 reads=[("Ef", i), "cst"], writes=[("ps", 2 + g)])
        if "af" not in SKIP:
            attn_finish(0, (2, 3))

    def v_to_tok(m, slot):
        S.op("pe", lambda e: e.transpose(psTb[:, 512:640], hT[:, 16, m * 128:(m + 1) * 128], identb[:]),
             reads=[("hT", 16), "identb"], writes=[("ps", 6)])
        S.op("act", lambda e: e.copy(out=Vaug[:, slot, :, 0:64], in_=psTb[:, 512:640].rearrange("p (g d) -> p g d", g=2)),
             reads=[("ps", 6), "Vaug_all"], writes=[("Vaug", slot)])

    def hgrn_head_common(h, m, am_off):
        i = h % 2
        S.op("pe", lambda e: e.transpose(psTb[:, 640:768], hT[:, 12 + h, m * 128:(m + 1) * 128], identb[:]),
             reads=[("hT", 12 + h), "identb"], writes=[("ps", 6)])
        S.op("act", lambda e: e.copy(out=Ktok[i], in_=psTb[:, 640:768]), reads=[("ps", 6)], writes=[("Ktok", i)])
        S.op("pe", lambda e: e.transpose(psTb[:, 768:896], hT[:, 4 + h, m * 128:(m + 1) * 128], identb[:]),
             reads=[("hT", 4 + h), "identb"], writes=[("ps", 6)])
        S.op("act", lambda e: e.copy(out=Vtok[i], in_=psTb[:, 768:896]), reads=[("ps", 6)], writes=[("Vtok", i)])
        S.op("pe", lambda e: e.matmul(ps[7][:, 0:128], lhsT=hT[:, 12 + h, m * 128:(m + 1) * 128], rhs=hT[:, 8 + h, m * 128:(m + 1) * 128],
                                      start=True, stop=True),
             reads=[("hT", 12 + h), ("hT", 8 + h)], writes=[("ps", 7)])
        S.op("dve", lambda e: e.tensor_tensor(out=Am[i], in0=ps[7][:, 0:128], in1=cst[:, am_off:am_off + 128], op=ALU.mult),
             reads=[("ps", 7), "cst"], writes=[("Am", i)])
        return i

    def hgrn_finish(h, m, i):
        S.op("act", lambda e: e.copy(out=ohf[i][:], in_=ps[7][:, 128:256]), reads=[("ps", 7)], writes=[("ohf", i)])
        S.op("act", lambda e: e.activation(out=osq[i][:], in_=ps[7][:, 128:256], func=AF.Square), reads=[("ps", 7)], writes=[("osq", i)])
        S.op("pe", lambda e: e.matmul(ps[7][:, 384:512], lhsT=onesf, rhs=osq[i][:], start=True, stop=True),
             reads=["cst", ("osq", i)], writes=[("ps", 7)])
        S.op("act", lambda e: e.activation(out=ors[i][:], in_=ps[7][:, 384:512], func=AF.Ln, bias=eps_rms, scale=1.0 / 128.0),
             reads=[("ps", 7), "cst"], writes=[("ors", i)])
        S.op("act", lambda e: e.activation(out=ors[i][:], in_=ors[i][:], func=AF.Exp, scale=-0.5), reads=[("ors", i)], writes=[("ors", i)])
        S.op("dve", lambda e: e.scalar_tensor_tensor(out=ohf[i][:], in0=ohf[i][:], scalar=hngt[:, h:h + 1], in1=ors[i][:], op0=ALU.mult, op1=ALU.mult),
             reads=[("ohf", i), ("ors", i), "hngt"], writes=[("ohf", i)])
        S.op("dve", lambda e: e.tensor_tensor(out=mixT[:, 4 + h, m * 128:(m + 1) * 128], in0=ohf[i][:], in1=G[:, h, m * 128:(m + 1) * 128], op=ALU.mult),
             reads=[("ohf", i), ("G", h)], writes=[("mixT", "h", m)] if h == 3 else [("mixT", "hh", h, m)])

    def prompt_attn_a(m, pos, seq, gsel=(0, 1)):
        gb = pos * 4 + m
        if 0 in gsel:
            v_to_tok(m, 1 + m)
        for g in gsel:
            for kh in range(2):
                if kh == 0 and gb == 0:
                    continue
                bk = 2 + kh
                S.op("pe", lambda e, g=g, kh=kh, bk=bk: e.matmul(
                    ps[bk][:], lhsT=KT[64 * g:64 * g + 64, m + kh, :], rhs=hT[64 * g:64 * g + 64, 0:4, m * 128:(m + 1) * 128],
                    start=True, stop=True),
                    reads=[("KT", m + kh)] + [("hT", c) for c in range(4)], writes=[("ps", bk)])
                S.op("act", lambda e, kh=kh, bk=bk: e.activation(out=Ef[kh][:], in_=ps[bk][:], func=AF.Exp, scale=SCALE),
                     reads=[("ps", bk)], writes=[("Ef", kh)])
                o = C_DMP + (kh * 2 + g) * 512
                S.op("dve", lambda e, kh=kh, g=g, o=o: e.tensor_tensor(out=Pb[:, kh * 2 + g, :], in0=Ef[kh][:], in1=cst[:, o:o + 512], op=ALU.mult),
                     reads=[("Ef", kh), "cst"], writes=[("Pb", kh * 2 + g)])

    def prompt_attn_b(m, pos, seq):
        gb = pos * 4 + m
        for g in range(2):
            for j in range(4):
                khs = [1] if gb == 0 else [0, 1]
                for n_, kh in enumerate(khs):
                    S.op("pe", lambda e, g=g, j=j, kh=kh, n_=n_, khs=khs: e.matmul(
                        ps[4 + g][:, j * 65:(j + 1) * 65], lhsT=Pb[:, kh * 2 + g, j * 128:(j + 1) * 128], rhs=Vaug[:, m + kh, g, :],
                        start=(n_ == 0), stop=(n_ == len(khs) - 1)),
                        reads=[("Pb", kh * 2 + g), ("Vaug", m + kh), "Vaug_all"], writes=[("ps", 4 + g)])
        attn_finish(m, (4, 5), split=True)

    def prompt_attn_c(m, pos, seq):
        gb = pos * 4 + m
        attn_finish_c(m)
        if pos == 3 and m == 3:
            for w_, dst in ((0, ck_p), (1, cv_p)):
                S.op("pe", lambda e, w_=w_: e.transpose(ps[3][:, w_ * 128:(w_ + 1) * 128], kvf[:, w_, :], idf), reads=[("kvf", w_), "cst"], writes=[("ps", 3)])
                S.op("act", lambda e, w_=w_: e.copy(out=kvt[:, w_, :], in_=ps[3][:, w_ * 128:(w_ + 1) * 128]), reads=[("ps", 3)], writes=[("kvt", w_)])
                S.dma("sp", lambda e, w_=w_, dst=dst: e.dma_start(out=dst[seq], in_=kvt[:, w_, :]), ("kvt", w_), reads=[("kvt", w_)])
    def OB(h):
        return 2 + h // 2

    def oc(h):
        return (h % 2) * 128

    def ac(h):
        return 256 + (h % 2) * 128


    def prompt_hgrn1(m, pos, seq):
        gb = pos * 4 + m
        par = m % 2
        blk = slice(m * 128, (m + 1) * 128)
        for h in range(4):
            S.op("pe", lambda e, h=h: e.transpose(psTb[:, h * 128:(h + 1) * 128], hT[:, 12 + h, blk], identb[:]),
                 reads=[("hT", 12 + h), "identb"], writes=[("ps", 6)])
        for h in range(4):
            S.op("pe", lambda e, h=h: e.transpose(psTb[:, 512 + h * 128:512 + (h + 1) * 128], hT[:, 4 + h, blk], identb[:]),
                 reads=[("hT", 4 + h), "identb"], writes=[("ps", 6)])
        S.op("act", lambda e: e.copy(out=Ktok4[par][:].rearrange("p h t -> p (h t)"), in_=psTb[:, 0:512]), reads=[("ps", 6)], writes=[("Ktok4", par)])
        S.op("act", lambda e: e.copy(out=Vtok4[par][:].rearrange("p h t -> p (h t)"), in_=psTb[:, 512:1024]), reads=[("ps", 6)], writes=[("Vtok4", par)])
        for h in range(4):
            B = OB(h)
            S.op("pe", lambda e, h=h, B=B: e.matmul(ps[B][:, ac(h):ac(h) + 128], lhsT=hT[:, 12 + h, blk], rhs=hT[:, 8 + h, blk], start=True, stop=True),
                 reads=[("hT", 12 + h), ("hT", 8 + h)], writes=[("ps", B)])
            S.op("dve", lambda e, h=h, B=B: e.tensor_tensor(out=Am4[:, h, :], in0=ps[B][:, ac(h):ac(h) + 128], in1=cst[:, C_AM:C_AM + 128], op=ALU.mult),
                 reads=[("ps", B), "cst"], writes=[("Am4", h)])

    def prompt_hgrn2(m, pos, seq):
        gb = pos * 4 + m
        par = m % 2
        blk = slice(m * 128, (m + 1) * 128)
        for c in range(2):
            first = (gb == 0 and c == 0)
            cs = slice(m * 128 + c * 64, m * 128 + (c + 1) * 64)
            if not first:
                for h in range(4):
                    ebl0 = EBL[:, h, 2 * m + c:2 * m + c + 1]
                    S.op("dve", lambda e, h=h, ebl0=ebl0: e.tensor_scalar(out=Sf[:, h, :], in0=Sf[:, h, :], scalar1=ebl0, scalar2=None, op0=ALU.mult),
                         reads=[("Sf", h), ("EBL", h)], writes=[("Sf", h)])
            for h in range(4):
                B = OB(h)
                S.op("pe", lambda e, h=h, B=B, c=c, first=first: e.matmul(ps[B][:, oc(h) + c * 64:oc(h) + (c + 1) * 64], lhsT=Vtok4[par][:, h, :],
                                                    rhs=Am4[:, h, c * 64:(c + 1) * 64], start=True, stop=first),
                     reads=[("Vtok4", par), ("Am4", h)], writes=[("ps", B)])
                if not first:
                    S.op("pe", lambda e, h=h, B=B, c=c, cs=cs: e.matmul(ps[B][:, oc(h) + c * 64:oc(h) + (c + 1) * 64], lhsT=Sb[:, h, :], rhs=hT[:, 8 + h, cs],
                                                        start=False, stop=True),
                         reads=[("Sb", h), ("hT", 8 + h)], writes=[("ps", B)])
                S.op("pe", lambda e, h=h, c=c: e.matmul(ps[4][:, h * 128:(h + 1) * 128], lhsT=Ktok4[par][64 * c:64 * c + 64, h, :],
                                                    rhs=Vtok4[par][64 * c:64 * c + 64, h, :], start=True, stop=True),
                     reads=[("Ktok4", par), ("Vtok4", par)], writes=[("ps", 4)])
            for h in range(4):
                B = 4
                ebl = EBL[:, h, 2 * m + c:2 * m + c + 1]
                if first:
                    S.op("dve", lambda e, h=h, B=B, ebl=ebl: e.tensor_scalar(out=Sb[:, h, :], in0=ps[4][:, h * 128:(h + 1) * 128], scalar1=ebl, scalar2=None, op0=ALU.mult),
                         reads=[("ps", B), ("EBL", h)], writes=[("Sb", h)])
                    S.op("dve", lambda e, h=h, B=B, ebl=ebl: e.tensor_scalar(out=Sf[:, h, :], in0=ps[4][:, h * 128:(h + 1) * 128], scalar1=ebl, scalar2=None, op0=ALU.mult),
                         reads=[("ps", B), ("EBL", h)], writes=[("Sf", h)])
                else:
                    S.op("dve", lambda e, h=h, B=B, ebl=ebl: e.scalar_tensor_tensor(out=Sb[:, h, :], in0=ps[4][:, h * 128:(h + 1) * 128], scalar=ebl, in1=Sf[:, h, :],
                                                                               op0=ALU.mult, op1=ALU.add),
                         reads=[("ps", B), ("EBL", h), ("Sf", h)], writes=[("Sb", h)])
                    S.op("dve", lambda e, h=h, B=B, ebl=ebl: e.scalar_tensor_tensor(out=Sf[:, h, :], in0=ps[4][:, h * 128:(h + 1) * 128], scalar=ebl, in1=Sf[:, h, :],
                                                                               op0=ALU.mult, op1=ALU.add),
                         reads=[("ps", B), ("EBL", h), ("Sf", h)], writes=[("Sf", h)])

    def prompt_hgrn3(m, pos, seq):
        gb = pos * 4 + m
        par = m % 2
        blk = slice(m * 128, (m + 1) * 128)
        for h in range(4):
            B = OB(h)
            i = h % 2
            S.op("act", lambda e, B=B, i=i, h=h: e.copy(out=ohf[i][:], in_=ps[B][:, oc(h):oc(h) + 128]), reads=[("ps", B)], writes=[("ohf", i)])
            S.op("act", lambda e, B=B, i=i, h=h: e.activation(out=osq[i][:], in_=ps[B][:, oc(h):oc(h) + 128], func=AF.Square), reads=[("ps", B)], writes=[("osq", i)])
            S.op("pe", lambda e, i=i, h=h: e.matmul(ps[5][:, h * 128:(h + 1) * 128], lhsT=onesf, rhs=osq[i][:], start=True, stop=True),
                 reads=["cst", ("osq", i)], writes=[("ps", 5)])
            S.op("act", lambda e, i=i, h=h: e.activation(out=ors[i][:], in_=ps[5][:, h * 128:(h + 1) * 128], func=AF.Ln, bias=eps_rms, scale=1.0 / 128.0),
                 reads=[("ps", 5), "cst"], writes=[("ors", i)])
            S.op("act", lambda e, i=i: e.activation(out=ors[i][:], in_=ors[i][:], func=AF.Exp, scale=-0.5), reads=[("ors", i)], writes=[("ors", i)])
            S.op("dve", lambda e, h=h, i=i: e.scalar_tensor_tensor(out=ohf[i][:], in0=ohf[i][:], scalar=hngt[:, h:h + 1], in1=ors[i][:], op0=ALU.mult, op1=ALU.mult),
                 reads=[("ohf", i), ("ors", i), "hngt"], writes=[("ohf", i)])
            S.op("dve", lambda e, h=h, i=i: e.tensor_tensor(out=mixT[:, 4 + h, blk], in0=ohf[i][:], in1=G[:, h, blk], op=ALU.mult),
                 reads=[("ohf", i), ("G", h)], writes=[("mixT", "h", m)] if h == 3 else [("mixT", "hh", h, m)])
        if pos == 3 and m == 3:
            S.dma("sp", lambda e: e.dma_start(out=st_p[seq].rearrange("h k v -> k h v"), in_=Sf[:]), "stp",
                  reads=[("Sf", h) for h in range(4)])

    for ti, tl in enumerate(tinfo):
        kind, t, row0, NB = tl["kind"], tl["t"], tl["row0"], tl["NB"]
        N = NB * 128
        nxt = tinfo[ti + 1] if ti + 1 < len(tinfo) else None
        if ti == 0:
            ensure_xcast(tl)
        for m in range(NB):
            emit_xload(tl, m)
        for m in range(NB):
            emit_feat_stage(tl, m)
        if "A" in stages:
            ffn_stage("a", NB, row0, final=(stages == "A"), nxt=nxt)
        if "B" in stages:
            mixer_stage(kind, t, NB, row0, final=(stages[-1] == "B"))
        if "C" in stages:
            ffn_stage("c", NB, row0, final=True, nxt=nxt)

    flush_wb(0)
    S.emit(st)
    st.close()
    return nc


def _w13_layout(w13):
    g = w13[:, :DFF].reshape(8, 128, NPIECE_FF, 256)
    u = w13[:, DFF:].reshape(8, 128, NPIECE_FF, 256)
    gu = np.concatenate([g, u], axis=3)
    return np.ascontiguousarray(gu.transpose(2, 1, 0, 3)).reshape(NPIECE_FF, 128, 4096)


def _w2_layout(w2):
    return np.ascontiguousarray(w2.reshape(NJ, 128, D).transpose(1, 0, 2)).reshape(128, NJ * D)


def _win_layout(w_in):
    qa = w_in[:, 0:512].reshape(D, 8, 64)
    qperm = np.concatenate([np.concatenate([qa[:, j], qa[:, j + 4]], axis=1) for j in range(4)], axis=1)
    ka = w_in[:, 512:640]
    va = w_in[:, 640:768]
    hq = w_in[:, 768:1280]
    hf = w_in[:, 1280:1792]
    hi = w_in[:, 1792:2304]
    hg = w_in[:, 2304:2816]
    pad = np.zeros((D, 256), np.float32)
    pieces = [qperm, np.concatenate([ka, va, pad], axis=1), hq, hf, hi, hg]
    out = np.stack([p.reshape(8, 128, 512).transpose(1, 0, 2).reshape(128, 4096) for p in pieces])
    return np.ascontiguousarray(out)


def prep_shared(inp):
    sh = {}
    sh["w13a"] = _w13_layout(inp["ffn1_w13"][0])
    sh["w13b"] = _w13_layout(inp["ffn2_w13"][0])
    sh["w2a"] = _w2_layout(inp["ffn1_w2"][0])
    sh["w2b"] = _w2_layout(inp["ffn2_w2"][0])
    sh["win"] = _win_layout(inp["w_in"][0])
    sh["wout"] = np.ascontiguousarray(inp["w_out"][0].reshape(8, 128, D).transpose(1, 0, 2)).reshape(128, 8 * D)
    sh["lnp"] = np.ascontiguousarray(np.stack([inp["ln1_g"][0], inp["ln1_b"][0], inp["ln2_g"][0], inp["ln2_b"][0],
                                               inp["ln3_g"][0], inp["ln3_b"][0]]))
    sh["ang"] = np.ascontiguousarray(inp["attn_norm_g"][0].reshape(1, 512))
    sh["sink"] = np.ascontiguousarray(inp["attn_sink"][0].reshape(1, 8))
    sh["hng"] = np.ascontiguousarray(inp["hgrn_norm_g"][0].reshape(4, 128).T)
    sh["lbp"] = np.ascontiguousarray(inp["lb_param"].reshape(2, 4, 128).transpose(2, 0, 1).reshape(128, 8))
    sh["cst"] = make_consts()
    return sh


def kernel(**inputs):
    inp = {k: np.asarray(v) for k, v in inputs.items()}
    n = 8
    sh = prep_shared(inp)
    xp = inp["x_prompt"]
    xs = inp["x_sample"]
    in_maps = []
    for c in range(n):
        m = dict(sh)
        m["xin"] = np.ascontiguousarray(np.concatenate(
            [xp[2 * c:2 * c + 2].reshape(4096, D), xs[16 * c:16 * c + 16].reshape(128, D)], axis=0))
        m["st_in"] = np.ascontiguousarray(inp["state_hgrn"][0, 16 * c:16 * c + 16])
        m["ck_in"] = np.ascontiguousarray(inp["cache_win_k"][0, 16 * c:16 * c + 16].reshape(16, 128, 128))
        m["cv_in"] = np.ascontiguousarray(inp["cache_win_v"][0, 16 * c:16 * c + 16].reshape(16, 128, 128))
        in_maps.append(m)
    nc = build(n_ptiles=8, sample=True, stages="ABC")
    res = run_bass_kernel_spmd(nc, in_maps, core_ids=list(range(n)))
    R = res.results
    y_p = np.concatenate([r["y"][:4096].reshape(2, 2048, D) for r in R], axis=0)
    y_s = np.concatenate([r["y"][4096:].reshape(16, 8, D) for r in R], axis=0)
    st_p = np.concatenate([r["st_p"] for r in R], axis=0)[None]
    ck_p = np.concatenate([r["ck_p"].reshape(2, 128, 2, 64) for r in R], axis=0)[None]
    cv_p = np.concatenate([r["cv_p"].reshape(2, 128, 2, 64) for r in R], axis=0)[None]
    st_s = np.concatenate([r["st_s"] for r in R], axis=0)[None]
    ck_s = np.concatenate([r["ck_s"].reshape(16, 128, 2, 64) for r in R], axis=0)[None]
    cv_s = np.concatenate([r["cv_s"].reshape(16, 128, 2, 64) for r in R], axis=0)[None]
    f = lambda a: np.ascontiguousarray(a, dtype=np.float32)
    return (f(y_p), f(y_s), f(st_p), f(ck_p), f(cv_p), f(st_s), f(ck_s), f(cv_s))
```
